# Optimizing a Trainium2 kernel written in Bass

```python
import math
import jax
import jax.numpy as jnp
from jax import lax
import numpy as np

D_MODEL = 1024
BATCH = 2
SEQ = 16384
DEPTH = 4

N_MEM = 256
BRANCH_W = D_MODEL // 2
N_BRANCH = 3
MIX_W = N_BRANCH * BRANCH_W

FOX_HD = 64
FOX_HEADS = BRANCH_W // FOX_HD
FOX_W = FOX_HEADS * FOX_HD
Q_BLOCK = 128
FOX_FBIAS_OFFSET = 3.0

S5_W = BRANCH_W
S5_GROUP = 16
S5_GROUPS = S5_W // S5_GROUP
S5_STATE = 64

HG_DK = 128
HG_DV = 128
HG_HEADS = BRANCH_W // HG_DV
HG_W = HG_HEADS * HG_DV
HG_CHUNK = 64

X_HEADS = 4
X_HD = D_MODEL // X_HEADS

D_FF = ((8 * D_MODEL + 3 * 256 - 1) // (3 * 256)) * 256

IN_SIZES = (FOX_W, FOX_W, FOX_W, FOX_HEADS, S5_W, HG_HEADS * HG_DK, HG_HEADS * HG_DK, HG_W, HG_W, N_BRANCH * D_MODEL)
IN_COLS = sum(IN_SIZES)
EPS = 1e-6

kernel_name = 'hybrid_fox_s5_hgrn2_gated_trunk'


def rmsnorm(x, g):
    xf = x.astype(jnp.float32)
    y = xf * lax.rsqrt(jnp.mean(xf * xf, axis=-1, keepdims=True) + EPS)
    return (y * g.astype(jnp.float32)).astype(x.dtype)


def _split_points(sizes):
    pts, acc = [], 0
    for s in sizes[:-1]:
        acc += s
        pts.append(acc)
    return pts


def fox_attention(q, k, v, log_f):
    bsz, seq, n_h, hd = q.shape
    n_blk = seq // Q_BLOCK
    c = jnp.cumsum(log_f, axis=1)
    c_keys = c.transpose(0, 2, 1)[:, :, None, :]
    q_blocks = q.reshape(bsz, n_blk, Q_BLOCK, n_h, hd).transpose(1, 0, 2, 3, 4)
    c_blocks = c.reshape(bsz, n_blk, Q_BLOCK, n_h).transpose(1, 0, 3, 2)
    k_pos = jnp.arange(seq)
    scale = hd ** -0.5

    def block(args):
        qi, ci, bi = args
        s = jnp.einsum('bqhd,bkhd->bhqk', qi, k, preferred_element_type=jnp.float32) * scale
        s = s + ci[..., None] - c_keys
        q_pos = bi * Q_BLOCK + jnp.arange(Q_BLOCK)
        s = jnp.where(k_pos[None, :] <= q_pos[:, None], s, -jnp.inf)
        p = jax.nn.softmax(s, axis=-1).astype(v.dtype)
        return jnp.einsum('bhqk,bkhd->bqhd', p, v)

    out = lax.map(block, (q_blocks, c_blocks, jnp.arange(n_blk)))
    return out.transpose(1, 0, 2, 3, 4).reshape(bsz, seq, n_h, hd)


def _complex_linear_combine(e1, e2):
    a1r, a1i, b1r, b1i = e1
    a2r, a2i, b2r, b2i = e2
    return (a1r * a2r - a1i * a2i,
            a1r * a2i + a1i * a2r,
            a2r * b1r - a2i * b1i + b2r,
            a2r * b1i + a2i * b1r + b2i)


def s5_mixer(u, a_re, a_im, b_re, b_im, c_re, c_im, d_skip, log_dt, w_glu, b_glu):
    bsz, seq, _ = u.shape
    f32 = jnp.float32
    uf = u.astype(f32)
    ug = uf.reshape(bsz, seq, S5_GROUPS, S5_GROUP)
    dt = jnp.exp(log_dt.astype(f32))[:, None]
    ar, ai = a_re.astype(f32), a_im.astype(f32)
    mag = jnp.exp(dt * ar)
    abar_r, abar_i = mag * jnp.cos(dt * ai), mag * jnp.sin(dt * ai)
    inv_den = 1.0 / (ar * ar + ai * ai)
    nr, ni = abar_r - 1.0, abar_i
    coef_r = (nr * ar + ni * ai) * inv_den
    coef_i = (ni * ar - nr * ai) * inv_den
    br, bi = b_re.astype(f32), b_im.astype(f32)
    bbar_r = coef_r[..., None] * br - coef_i[..., None] * bi
    bbar_i = coef_r[..., None] * bi + coef_i[..., None] * br
    drive_r = jnp.einsum('bsgh,gph->bsgp', ug, bbar_r)
    drive_i = jnp.einsum('bsgh,gph->bsgp', ug, bbar_i)
    a_shape = (1, seq, S5_GROUPS, S5_STATE)
    elems = (jnp.broadcast_to(abar_r, a_shape), jnp.broadcast_to(abar_i, a_shape), drive_r, drive_i)
    _, _, h_r, h_i = lax.associative_scan(_complex_linear_combine, elems, axis=1)
    y = (jnp.einsum('bsgp,ghp->bsgh', h_r, c_re.astype(f32))
         - jnp.einsum('bsgp,ghp->bsgh', h_i, c_im.astype(f32)))
    y = y.reshape(bsz, seq, S5_W) + d_skip.astype(f32) * uf
    z = jax.nn.gelu(y)
    out = z * jax.nn.sigmoid(z @ w_glu.astype(f32) + b_glu.astype(f32))
    return out.astype(u.dtype)


def hgrn2_mixer(q, f_logit, i_in, g_out, lb, o_gain):
    bsz, seq = q.shape[0], q.shape[1]
    f32 = jnp.float32
    n_chunks = seq // HG_CHUNK
    z = f_logit.astype(f32)
    log_f = jnp.logaddexp(jnp.log(lb), jnp.log1p(-lb) + jax.nn.log_sigmoid(z))
    key = (1.0 - lb) * jax.nn.sigmoid(-z)
    qf = jax.nn.silu(q.astype(f32))
    vf = i_in.astype(f32)

    def chunks(t):
        return t.reshape(bsz, n_chunks, HG_CHUNK, HG_HEADS, t.shape[-1]).transpose(1, 0, 3, 2, 4)

    causal = jnp.tril(jnp.ones((HG_CHUNK, HG_CHUNK), dtype=bool))[None, None, :, :, None]

    def step(state, xs):
        qc, kc, vc, lfc = xs
        b = jnp.cumsum(lfc, axis=2)
        o_inter = jnp.einsum('bhtk,bhkv->bhtv', qc * jnp.exp(b), state)
        rel = jnp.where(causal, b[:, :, :, None, :] - b[:, :, None, :, :], -jnp.inf)
        scores = jnp.einsum('bhtsk,bhsk->bhts', qc[:, :, :, None, :] * jnp.exp(rel), kc)
        o_intra = jnp.einsum('bhts,bhsv->bhtv', scores, vc)
        b_last = b[:, :, -1:, :]
        new_state = (jnp.exp(b_last[:, :, 0, :])[..., None] * state
                     + jnp.einsum('bhsk,bhsv->bhkv', kc * jnp.exp(b_last - b), vc))
        return new_state, o_inter + o_intra

    init = jnp.zeros((bsz, HG_HEADS, HG_DK, HG_DV), f32)
    _, o = lax.scan(step, init, (chunks(qf), chunks(key), chunks(vf), chunks(log_f)))
    o = o.transpose(1, 0, 3, 2, 4).reshape(bsz, seq, HG_HEADS, HG_DV)
    o = rmsnorm(o, o_gain) * jax.nn.silu(g_out.astype(f32))
    return o.reshape(bsz, seq, HG_W).astype(q.dtype)


def cross_attention(h, m, wq, wk, wv, wo, gq, gk):
    bsz, seq, _ = h.shape
    n_mem = m.shape[1]
    q = rmsnorm((h @ wq).reshape(bsz, seq, X_HEADS, X_HD), gq)
    k = rmsnorm((m @ wk).reshape(bsz, n_mem, X_HEADS, X_HD), gk)
    v = (m @ wv).reshape(bsz, n_mem, X_HEADS, X_HD)
    s = jnp.einsum('bshd,bmhd->bhsm', q, k, preferred_element_type=jnp.float32) * (X_HD ** -0.5)
    p = jax.nn.softmax(s, axis=-1).astype(v.dtype)
    o = jnp.einsum('bhsm,bmhd->bshd', p, v).reshape(bsz, seq, X_HEADS * X_HD)
    return o @ wo


def swiglu(h, w_gu, w_d):
    a, b = jnp.split(h @ w_gu, 2, axis=-1)
    return (jax.nn.silu(a) * b) @ w_d


def setup_inputs(seed: int = 0) -> dict:
    key = jax.random.key(seed)
    keys = jax.random.split(key, 40)
    counter = [0]
    f32 = jnp.float32

    def nxt():
        k = keys[counter[0]]
        counter[0] += 1
        return k

    def nrm(shape, scale):
        return jax.random.normal(nxt(), shape, f32) * scale

    def gain(shape):
        return 1.0 + nrm(shape, 0.02)

    L, D = DEPTH, D_MODEL
    G, P, GS = S5_GROUPS, S5_STATE, S5_GROUP
    return {
        'x': nrm((BATCH, SEQ, D), 1.0),
        'mem': nrm((BATCH, N_MEM, D), 1.0),
        'norm_mix': gain((L, D)),
        'w_in': nrm((L, D, IN_COLS), D ** -0.5),
        'fox_fbias': FOX_FBIAS_OFFSET + nrm((L, FOX_HEADS), 0.5),
        'fox_qnorm': gain((L, FOX_HD)),
        'fox_knorm': gain((L, FOX_HD)),
        's5_a_re': -0.5 + nrm((L, G, P), 0.01),
        's5_a_im': math.pi * jnp.arange(P, dtype=f32)[None, None, :] + nrm((L, G, P), 0.01),
        's5_b_re': nrm((L, G, P, GS), (2 * GS) ** -0.5),
        's5_b_im': nrm((L, G, P, GS), (2 * GS) ** -0.5),
        's5_c_re': nrm((L, G, GS, P), (2 * P) ** -0.5),
        's5_c_im': nrm((L, G, GS, P), (2 * P) ** -0.5),
        's5_d': nrm((L, S5_W), 1.0),
        's5_log_dt': jax.random.uniform(nxt(), (L, G), f32, math.log(1e-3), math.log(1e-1)),
        's5_w_glu': nrm((L, S5_W, S5_W), S5_W ** -0.5),
        's5_b_glu': nrm((L, S5_W), 0.01),
        'hg_lb': nrm((L, HG_HEADS * HG_DK), 0.1),
        'hg_onorm': gain((L, HG_DV)),
        'w_branch': nrm((L, MIX_W, D), BRANCH_W ** -0.5),
        'w_out': nrm((L, D, D), D ** -0.5),
        'norm_x': gain((L, D)),
        'norm_mem': gain((L, D)),
        'xq': nrm((L, D, D), D ** -0.5),
        'xk': nrm((L, D, D), D ** -0.5),
        'xv': nrm((L, D, D), D ** -0.5),
        'xo': nrm((L, D, D), D ** -0.5),
        'x_qnorm': gain((L, X_HD)),
        'x_knorm': gain((L, X_HD)),
        'norm_ffn': gain((L, D)),
        'w_gate_up': nrm((L, D, 2 * D_FF), D ** -0.5),
        'w_down': nrm((L, D_FF, D), D_FF ** -0.5),
    }


def reference(x, mem, norm_mix, w_in, fox_fbias, fox_qnorm, fox_knorm, s5_a_re, s5_a_im,
              s5_b_re, s5_b_im, s5_c_re, s5_c_im, s5_d, s5_log_dt, s5_w_glu, s5_b_glu,
              hg_lb, hg_onorm, w_branch, w_out, norm_x, norm_mem, xq, xk, xv, xo,
              x_qnorm, x_knorm, norm_ffn, w_gate_up, w_down):
    bsz, seq, _ = x.shape
    split_pts = _split_points(IN_SIZES)
    lb_all = jnp.cumsum(jax.nn.softmax(hg_lb.astype(jnp.float32), axis=0), axis=0)
    lb_all = lb_all - lb_all[0:1]
    for l in range(DEPTH):
        h = rmsnorm(x, norm_mix[l])
        proj = h @ w_in[l]
        fq, fk, fv, ff, su, hq, hf, hi, hg, gl = jnp.split(proj, split_pts, axis=-1)
        q = rmsnorm(fq.reshape(bsz, seq, FOX_HEADS, FOX_HD), fox_qnorm[l])
        k = rmsnorm(fk.reshape(bsz, seq, FOX_HEADS, FOX_HD), fox_knorm[l])
        v = fv.reshape(bsz, seq, FOX_HEADS, FOX_HD)
        log_f = jax.nn.log_sigmoid((ff + fox_fbias[l]).astype(jnp.float32))
        y_fox = fox_attention(q, k, v, log_f).reshape(bsz, seq, FOX_W)
        y_s5 = s5_mixer(su, s5_a_re[l], s5_a_im[l], s5_b_re[l], s5_b_im[l], s5_c_re[l], s5_c_im[l],
                        s5_d[l], s5_log_dt[l], s5_w_glu[l], s5_b_glu[l])
        y_hg = hgrn2_mixer(hq.reshape(bsz, seq, HG_HEADS, HG_DK), hf.reshape(bsz, seq, HG_HEADS, HG_DK),
                           hi.reshape(bsz, seq, HG_HEADS, HG_DV), hg.reshape(bsz, seq, HG_HEADS, HG_DV),
                           lb_all[l].reshape(HG_HEADS, HG_DK), hg_onorm[l])
        gates = jax.nn.sigmoid(gl).reshape(bsz, seq, N_BRANCH, D_MODEL)
        wb = w_branch[l]
        merged = (gates[:, :, 0] * (y_fox @ wb[:BRANCH_W])
                  + gates[:, :, 1] * (y_s5 @ wb[BRANCH_W:2 * BRANCH_W])
                  + gates[:, :, 2] * (y_hg @ wb[2 * BRANCH_W:]))
        x = x + merged @ w_out[l]
        x = x + cross_attention(rmsnorm(x, norm_x[l]), rmsnorm(mem, norm_mem[l]),
                                xq[l], xk[l], xv[l], xo[l], x_qnorm[l], x_knorm[l])
        x = x + swiglu(rmsnorm(x, norm_ffn[l]), w_gate_up[l], w_down[l])
    return x
```

```python
import contextlib
import numpy as np
import ml_dtypes
import concourse.bass as bass
import concourse.mybir as mybir
from concourse.bass_utils import run_bass_kernel_spmd

F32 = mybir.dt.float32
BF16 = mybir.dt.bfloat16
AF = mybir.ActivationFunctionType
ALU = mybir.AluOpType
AX = mybir.AxisListType

D = 1024
DEPTH = 4
NMEM = 256
DFF = 2816
INC = 7176
EPS = 1e-6
TT = 512
SL = 64
HC = 64


class Sem:
    def __init__(self, h):
        self.h = h
        self.count = 0


class Tk:
    __slots__ = ("w", "r", "dsem")

    def __init__(self):
        self.w = {}
        self.r = {}
        self.dsem = None


class Eng:
    def __init__(self, name, e, sem, is_pe=False):
        self.name, self.e, self.sem, self.is_pe = name, e, sem, is_pe
        self.waited = {}


class Ctx:
    def __init__(self, nc):
        self.nc = nc
        self.st = contextlib.ExitStack()
        self.sem_free = []
        self.nsem = 0
        self.dma_sems = []
        self.pe = Eng("pe", nc.tensor, self._newsem("pe"), True)
        self.act = Eng("act", nc.scalar, self._newsem("act"))
        self.dve = Eng("dve", nc.vector, self._newsem("dve"))
        self.pool = Eng("pool", nc.gpsimd, self._newsem("pool"))
        self.sp = Eng("sp", nc.sync, self._newsem("sp"))
        self.engs = [self.pe, self.act, self.dve, self.pool, self.sp]
        self.ninstr = 0

    def _newsem(self, name):
        self.nsem += 1
        return Sem(self.st.enter_context(self.nc.semaphore(f"s{self.nsem}_{name}")))

    def get_dsem(self):
        if self.sem_free:
            return self.sem_free.pop()
        s = self._newsem("d")
        self.dma_sems.append(s)
        return s

    def _need(self, E, tokens):
        for S, v in tokens.items():
            if v <= 0 or (E.is_pe and S is E.sem) or E.waited.get(S, 0) >= v:
                continue
            E.e.wait_ge(S.h, v)
            E.waited[S] = v
            self.ninstr += 1

    def _deps(self, reads, writes):
        tok = {}
        for t in reads:
            for S, v in t.w.items():
                if tok.get(S, 0) < v:
                    tok[S] = v
        for t in writes:
            for S, v in t.w.items():
                if tok.get(S, 0) < v:
                    tok[S] = v
            for S, v in t.r.items():
                if tok.get(S, 0) < v:
                    tok[S] = v
        return tok

    def _commit(self, S, v, reads, writes):
        for t in writes:
            t.w = {S: v}
            t.r = {}
        for t in reads:
            if t.r.get(S, 0) < v:
                t.r[S] = v

    def op(self, E, fn, reads=(), writes=(), inc=True):
        self._need(E, self._deps(reads, writes))
        ins = fn(E.e)
        self.ninstr += 1
        if inc:
            E.sem.count += 1
            ins.then_inc(E.sem.h, 1)
            self._commit(E.sem, E.sem.count, reads, writes)
        else:
            self._commit(E.sem, E.sem.count + 1, reads, writes)
        return ins

    def dma(self, Q, out, in_, sb_tk, reads=(), writes=(), **kw):
        self._need(Q, self._deps(reads, writes))
        if sb_tk.dsem is None:
            sb_tk.dsem = self.get_dsem()
        S = sb_tk.dsem
        S.count += 16
        Q.e.dma_start(out=out, in_=in_, **kw).then_inc(S.h, 16)
        self.ninstr += 1
        for t in writes:
            if t is sb_tk:
                t.w = {S: S.count}
                t.r = {}
            else:
                t.w[S] = S.count
        for t in reads:
            if t.r.get(S, 0) < S.count:
                t.r[S] = S.count

    def release(self, tks):
        for t in tks:
            if t.dsem is not None:
                self.sem_free.append(t.dsem)
                t.dsem = None

    def barrier(self):
        tok = {E.sem: E.sem.count for E in self.engs}
        for S in self.dma_sems:
            tok[S] = S.count
        for E in self.engs:
            self._need(E, tok)


class Pool_:
    def __init__(self, tiles):
        self.tiles = [(t, Tk()) for t in tiles]
        self.i = 0

    def next(self):
        t = self.tiles[self.i % len(self.tiles)]
        self.i += 1
        return t

    def tks(self):
        return [t[1] for t in self.tiles]


class Phase:
    def __init__(self, cx, name):
        self.cx, self.name = cx, name
        self.st = contextlib.ExitStack()
        self.tks = []
        self.n = 0

    def sb(self, shape, dt):
        self.n += 1
        return self.st.enter_context(self.cx.nc.sbuf_tensor(f"{self.name}_{self.n}", list(shape), dt))

    def sbt(self, shape, dt):
        tk = Tk()
        self.tks.append(tk)
        return self.sb(shape, dt), tk

    def pool(self, n, shape, dt):
        p = Pool_([self.sb(shape, dt) for _ in range(n)])
        self.tks += p.tks()
        return p

    def close(self):
        self.cx.barrier()
        self.cx.release(self.tks)
        self.st.close()


def bc(ap, shape, axis):
    return ap.unsqueeze(axis).to_broadcast(list(shape))


def build(N, depth, debug=False, stages=("p1", "fox")):
    nc = bass.Bass("TRN2", target_bir_lowering=False)
    cx = Ctx(nc)
    NT = N // TT
    NK = N // 128
    NCH = N // HC

    def din(name, shape, dt=F32):
        return nc.dram_tensor(name, list(shape), dt, kind="ExternalInput").ap()

    dbg_out = {}

    def dscr(name, shape, dt):
        if debug:
            t = nc.dram_tensor(name, list(shape), dt, kind="ExternalOutput").ap()
            dbg_out[name] = t
            return t
        return nc.dram_tensor(name, list(shape), dt).ap()

    xT_in = din("xT", [D, N])
    memT_in = din("memT", [D, NMEM])
    consts = din("consts", [128, 6, 128])
    nvec_in = din("nvec", [128, 80])
    jv_in = din("jv", [128, TT])
    m16_in = din("m16", [128, 8])
    P = {}
    pshapes = dict(
        norm_mix=[DEPTH, D], w_in=[DEPTH, D, 4098], fox_fbias=[DEPTH, 2], fox_qnorm=[DEPTH, 64],
        fox_knorm=[DEPTH, 64], s5_a_re=[DEPTH, 8, 64], s5_a_im=[DEPTH, 8, 64],
        s5_b_re=[DEPTH, 8, 64, 16], s5_b_im=[DEPTH, 8, 64, 16], s5_c_re=[DEPTH, 8, 16, 64],
        s5_c_im=[DEPTH, 8, 16, 64], s5_d=[DEPTH, 128], s5_log_dt=[DEPTH, 8],
        s5_w_glu=[DEPTH, 512, 512], s5_b_glu=[DEPTH, 512], hg_lb=[DEPTH, 128], hg_onorm=[DEPTH, 128],
        w_branch=[DEPTH, 1536, D], w_out=[DEPTH, D, D], norm_x=[DEPTH, D], norm_mem=[DEPTH, D],
        xq=[DEPTH, D, D], xk=[DEPTH, D, D], xv=[DEPTH, D, D], xo=[DEPTH, D, D],
        x_qnorm=[DEPTH, 256], x_knorm=[DEPTH, 256], norm_ffn=[DEPTH, D],
        w_gate_up=[DEPTH, D, 2 * DFF], w_down=[DEPTH, DFF, D])
    for k, s in pshapes.items():
        P[k] = din(k, s)
    out_T = nc.dram_tensor("outT", [D, N], F32, kind="ExternalOutput").ap()

    SP, PL, PE, ACT, DVE = cx.sp, cx.pool, cx.pe, cx.act, cx.dve
    NOTK = Tk()

    G = Phase(cx, "g")
    cst_f = G.sb([128, 6, 128], F32)
    cst_b = G.sb([128, 6, 128], BF16)
    nvec = G.sb([128, 80], F32)
    lb_all = G.sb([128, 1, DEPTH], F32)
    c_tk = Tk()
    cx.dma(SP, cst_f[:], consts, c_tk, writes=[c_tk])
    cx.dma(SP, nvec[:], nvec_in, c_tk, writes=[c_tk])
    cx.op(DVE, lambda e: e.tensor_copy(out=cst_b[:], in_=cst_f[:]), reads=[c_tk], writes=[c_tk])
    ident_b, tri_b, blk64_b, ones_b = cst_b[:, 0, :], cst_b[:, 1, :], cst_b[:, 2, :], cst_b[:, 4, :]
    ident_f, swap_f, ones_f = cst_f[:, 0, :], cst_f[:, 3, :], cst_f[:, 4, :]
    psb = [cx.st.enter_context(nc.psum_tensor(f"ps{i}", [128, 512], F32)) for i in range(8)]
    PS = Pool_(psb[2:8])
    PSA = Pool_(psb[0:2])

    with contextlib.nullcontext():
        lbt = G.sb([128, 1, DEPTH], F32)
        lbs = G.sb([128, 1], F32)
        for l_ in range(DEPTH):
            cx.dma(SP, lbt[:, :, l_], P["hg_lb"][l_].rearrange("(h p) -> p h", p=128), c_tk, writes=[c_tk],
                   allow_slow_non_contiguous=True)
        cx.op(ACT, lambda e: e.activation(out=lbt[:], in_=lbt[:], func=AF.Exp), reads=[c_tk], writes=[c_tk])
        cx.op(DVE, lambda e: e.tensor_reduce(out=lbs[:], in_=lbt[:], axis=AX.X, op=ALU.add), reads=[c_tk], writes=[c_tk])
        cx.op(DVE, lambda e: e.reciprocal(out=lbs[:], in_=lbs[:]), reads=[c_tk], writes=[c_tk])
        cx.op(DVE, lambda e: e.tensor_tensor(out=lbt[:], in0=lbt[:], in1=bc(lbs[:], [128, 1, DEPTH], 2), op=ALU.mult),
              reads=[c_tk], writes=[c_tk])
        cx.op(DVE, lambda e: e.memset(lb_all[:, :, 0:1], 0.0), writes=[c_tk])
        for l in range(1, DEPTH):
            cx.op(DVE, lambda e, l=l: e.tensor_tensor(out=lb_all[:, :, l:l + 1], in0=lb_all[:, :, l - 1:l],
                                                     in1=lbt[:, :, l:l + 1], op=ALU.add), reads=[c_tk], writes=[c_tk])

    def mm_group(out_ap, pairs, reads, out_tk):
        n = len(pairs)
        for i, (lt, rh) in enumerate(pairs):
            cx.op(PE, lambda e, lt=lt, rh=rh, i=i: e.matmul(out_ap, lhsT=lt, rhs=rh, start=(i == 0), stop=(i == n - 1)),
                  reads=reads, writes=[out_tk], inc=(i == n - 1))

    def load_w(ph, src_rows_ap, nk, c0, c1, gain=None, stg_pool=None, w_out=None, w_tk=None, wc0=0):
        CH = 512
        for cc in range(c0, c1, CH):
            ce = min(c1, cc + CH)
            for k0 in range(0, nk, 8):
                k1 = min(nk, k0 + 8)
                stg, stk = stg_pool.next()
                cx.dma(SP, stg[:, 0:k1 - k0, 0:ce - cc],
                       src_rows_ap.rearrange("(kt p) c -> p kt c", p=128)[:, k0:k1, cc:ce], stk, writes=[stk])
                for kt in range(k0, k1):
                    dst = w_out[:, kt, wc0 + cc - c0: wc0 + ce - c0]
                    if gain is not None:
                        cx.op(ACT, lambda e, dst=dst, kt=kt, stg=stg, k0=k0, ce=ce, cc=cc: e.activation(
                            out=dst, in_=stg[:, kt - k0, 0:ce - cc], func=AF.Copy, scale=gain[0][:, kt:kt + 1]),
                            reads=[stk, gain[1]], writes=[w_tk])
                    else:
                        cx.op(PL, lambda e, dst=dst, kt=kt, stg=stg, k0=k0, ce=ce, cc=cc: e.tensor_copy(
                            out=dst, in_=stg[:, kt - k0, 0:ce - cc]), reads=[stk], writes=[w_tk])

    def load_vec(ph, src1d, nk, tk):
        t = ph.sb([128, nk], F32)
        cx.dma(SP, t[:], src1d.rearrange("(kt p) -> p kt", p=128), tk, writes=[tk], allow_slow_non_contiguous=True)
        return t

    def rms_rstd(ph, xt, x_tk, nkt, width, sq_pool, rs_pool):
        sq, sqk = sq_pool.next()
        cx.op(ACT, lambda e: e.activation(out=sq[:, 0:nkt, 0:width], in_=xt, func=AF.Square), reads=[x_tk], writes=[sqk])
        pb, pk = PS.next()
        mm_group(pb[:, 0:width], [(ones_b, sq[:, k, 0:width]) for k in range(nkt)], [sqk, c_tk], pk)
        rs, rk = rs_pool.next()
        cx.op(ACT, lambda e: e.activation(out=rs[:, 0:width], in_=pb[:, 0:width], func=AF.Sqrt, scale=1.0 / (nkt * 128), bias=eps_t[:, 0:1]),
              reads=[pk, c_tk], writes=[rk])
        cx.op(DVE, lambda e: e.reciprocal(out=rs[:, 0:width], in_=rs[:, 0:width]), reads=[rk], writes=[rk])
        return rs, rk

    eps_t = G.sb([128, 1], F32)
    cx.op(DVE, lambda e: e.memset(eps_t[:], EPS), writes=[c_tk])

    def mk_scratch(l):
        S = {}
        S["qT"] = dscr(f"qT{l}", [128, N], BF16)
        S["kT"] = dscr(f"kT{l}", [128, N], BF16)
        S["vP"] = dscr(f"vP{l}", [2, 128, NK, 64], BF16)
        S["lfT"] = dscr(f"lfT{l}", [2, N], F32)
        S["aug"] = dscr(f"aug{l}", [2, 6, 2, N], BF16)
        S["uT"] = dscr(f"uT{l}", [128, N], BF16)
        S["hqT"] = dscr(f"hqT{l}", [128, N], BF16)
        S["lfhT"] = dscr(f"lfhT{l}", [128, N], F32)
        S["khT"] = dscr(f"khT{l}", [128, N], BF16)
        S["hvP"] = dscr(f"hvP{l}", [1, 64, NCH, 128], BF16)
        S["hgT"] = dscr(f"hgT{l}", [128, N], BF16)
        S["ysrc"] = nc.dram_tensor(f"ysrc{l}", [N // 256, 384, 256], BF16).ap()
        S["ydst"] = nc.dram_tensor(f"ydst{l}", [N // 256, 1536, 256], BF16).ap()
        S["x1"] = dscr(f"x1_{l}", [D, N], F32)
        S["x2"] = dscr(f"x2_{l}", [D, N], F32)
        S["x3"] = out_T if l == depth - 1 else dscr(f"x3_{l}", [D, N], F32)
        S["tk"] = {k: Tk() for k in list(S.keys())}
        return S

    def phase1(l, xT, x_tk, S):
        ph = Phase(cx, f"p1_{l}")
        tk = S["tk"]
        W, w_tk = ph.sbt([128, 8, 1026], BF16)
        stg_pool = ph.pool(2, [128, 8, 512], F32)
        g_tk = Tk()
        ph.tks.append(g_tk)
        g1 = load_vec(ph, P["norm_mix"][l], 8, g_tk)
        load_w(ph, P["w_in"][l], 8, 0, 1026, gain=(g1, g_tk), stg_pool=stg_pool, w_out=W, w_tk=w_tk)
        gqk = ph.sb([128, 2], F32)
        for hf in range(2):
            cx.dma(SP, gqk[hf * 64:(hf + 1) * 64, 0:1], P["fox_qnorm"][l].rearrange("(p o) -> p o", o=1), g_tk, writes=[g_tk])
            cx.dma(SP, gqk[hf * 64:(hf + 1) * 64, 1:2], P["fox_knorm"][l].rearrange("(p o) -> p o", o=1), g_tk, writes=[g_tk])
        cx.op(DVE, lambda e: e.tensor_scalar(out=gqk[:, 0:1], in0=gqk[:, 0:1], scalar1=0.125, scalar2=None, op0=ALU.mult),
              reads=[g_tk], writes=[g_tk])
        nfb = ph.sb([2, 1], F32)
        cx.dma(SP, nfb[:], P["fox_fbias"][l].rearrange("(p o) -> p o", o=1), g_tk, writes=[g_tk])
        cx.op(DVE, lambda e: e.tensor_scalar(out=nfb[:], in0=nfb[:], scalar1=-1.0, scalar2=None, op0=ALU.mult), reads=[g_tk], writes=[g_tk])
        oml = ph.sb([128, 1], F32)
        noml = ph.sb([128, 1], F32)
        cx.op(DVE, lambda e: e.tensor_scalar(out=oml[:], in0=lb_all[:, :, l], scalar1=-1.0, scalar2=1.0, op0=ALU.mult, op1=ALU.add),
              reads=[c_tk], writes=[g_tk])
        cx.op(DVE, lambda e: e.tensor_scalar(out=noml[:], in0=oml[:], scalar1=-1.0, scalar2=None, op0=ALU.mult), reads=[g_tk], writes=[g_tk])

        x_pool = ph.pool(2, [128, 8, TT], F32)
        sq_pool = ph.pool(1, [128, 8, TT], BF16)
        h_pool = ph.pool(2, [128, 8, TT], BF16)
        rs_pool = ph.pool(2, [128, TT], F32)
        evb = ph.pool(4, [128, TT], BF16)
        evf = ph.pool(3, [128, TT], F32)
        xr = xT.rearrange("(kt p) n -> p kt n", p=128)

        def ld(i):
            xt, xk = x_pool.next()
            cx.dma(SP, xt[:], xr[:, :, i * TT:(i + 1) * TT], xk, reads=[x_tk], writes=[xk])
            return xt, xk

        nxt = ld(0)
        for i in range(NT):
            xt, xk = nxt
            if i + 1 < NT:
                nxt = ld(i + 1)
            sl = slice(i * TT, (i + 1) * TT)
            rs, rk = rms_rstd(ph, xt[:], xk, 8, TT, sq_pool, rs_pool)
            h, hk = h_pool.next()
            cx.op(DVE, lambda e: e.tensor_tensor(out=h[:], in0=xt[:], in1=bc(rs[:], [128, 8, TT], 1), op=ALU.mult),
                  reads=[xk, rk], writes=[hk])

            def proj(c0, m):
                pb, pk = PS.next()
                mm_group(pb[0:m, :], [(W[:, k, c0:c0 + m], h[:, k, :]) for k in range(8)], [w_tk, hk], pk)
                return pb, pk

            for ct in range(2):
                pb, pk = proj(ct * 128, 128)
                sqh, sk = evb.next()
                cx.op(ACT, lambda e: e.activation(out=sqh[:], in_=pb[:], func=AF.Square), reads=[pk], writes=[sk])
                pb2, pk2 = PS.next()
                mm_group(pb2[:], [(blk64_b, sqh[:])], [sk, c_tk], pk2)
                rh, rhk = evf.next()
                cx.op(ACT, lambda e: e.activation(out=rh[:], in_=pb2[:], func=AF.Sqrt, scale=1.0 / 64, bias=eps_t[:, 0:1]),
                      reads=[pk2, c_tk], writes=[rhk])
                cx.op(DVE, lambda e: e.reciprocal(out=rh[:], in_=rh[:]), reads=[rhk], writes=[rhk])
                qn, qk_ = evb.next()
                gcol = gqk[:, 0:1] if ct < 1 else gqk[:, 1:2]
                cx.op(DVE, lambda e: e.scalar_tensor_tensor(out=qn[:], in0=pb[:], scalar=gcol, in1=rh[:], op0=ALU.mult, op1=ALU.mult),
                      reads=[pk, rhk, g_tk], writes=[qk_])
                dst, dk = (S["qT"], tk["qT"]) if ct < 1 else (S["kT"], tk["kT"])
                r0 = 0
                cx.dma(PL, dst[r0:r0 + 128, sl], qn[:], qk_, reads=[qk_], writes=[dk])
            pb, pk = proj(384, 2)
            ef, ek = evf.next()
            cx.op(ACT, lambda e: e.activation(out=ef[0:2, :], in_=pb[0:2, :], func=AF.Exp, scale=-1.0, bias=nfb[:, 0:1]),
                  reads=[pk, g_tk], writes=[ek])
            cx.op(ACT, lambda e: e.activation(out=ef[0:2, :], in_=ef[0:2, :], func=AF.Ln, bias=1.0), reads=[ek], writes=[ek])
            cx.op(DVE, lambda e: e.tensor_scalar(out=ef[0:2, :], in0=ef[0:2, :], scalar1=-1.0, scalar2=None, op0=ALU.mult),
                  reads=[ek], writes=[ek])
            cx.dma(PL, S["lfT"][:, sl], ef[0:2, :], ek, reads=[ek], writes=[tk["lfT"]])
            for (c0, name, fn) in ((386, "uT", AF.Copy), (514, "hqT", AF.Silu), (898, "hgT", AF.Silu)):
                for ct in range(1):
                    pb, pk = proj(c0 + ct * 128, 128)
                    ob, ok = evb.next()
                    cx.op(ACT, lambda e, fn=fn: e.activation(out=ob[:], in_=pb[:], func=fn), reads=[pk], writes=[ok])
                    cx.dma(PL, S[name][ct * 128:(ct + 1) * 128, sl], ob[:], ok, reads=[ok], writes=[tk[name]])
            for ct in range(1):
                pb, pk = proj(642 + ct * 128, 128)
                sg, sgk = evf.next()
                cx.op(ACT, lambda e: e.activation(out=sg[:], in_=pb[:], func=AF.Sigmoid), reads=[pk], writes=[sgk])
                lf, lfk = evf.next()
                cx.op(ACT, lambda e, ct=ct: e.activation(out=lf[:], in_=sg[:], func=AF.Ln, scale=oml[:, ct:ct + 1], bias=lb_all[:, ct, l:l + 1]),
                      reads=[sgk, g_tk, c_tk], writes=[lfk])
                cx.dma(PL, S["lfhT"][ct * 128:(ct + 1) * 128, sl], lf[:], lfk, reads=[lfk], writes=[tk["lfhT"]])
                kh, khk = evb.next()
                cx.op(DVE, lambda e, ct=ct: e.tensor_scalar(out=kh[:], in0=sg[:], scalar1=noml[:, ct:ct + 1], scalar2=oml[:, ct:ct + 1],
                                                           op0=ALU.mult, op1=ALU.add), reads=[sgk, g_tk], writes=[khk])
                cx.dma(PL, S["khT"][ct * 128:(ct + 1) * 128, sl], kh[:], khk, reads=[khk], writes=[tk["khT"]])
            for ts in range(4):
                ktg = i * 4 + ts
                for which in range(2):
                    c0 = 256 if which == 0 else 770
                    pb, pk = PS.next()
                    mm_group(pb[:, 0:128], [(h[:, k, ts * 128:(ts + 1) * 128], W[:, k, c0:c0 + 128]) for k in range(8)], [w_tk, hk], pk)
                    ob, ok = evb.next()
                    cx.op(ACT, lambda e: e.activation(out=ob[:, 0:128], in_=pb[:, 0:128], func=AF.Copy), reads=[pk], writes=[ok])
                    if which == 0:
                        cx.dma(PL, S["vP"][:, :, ktg, :].rearrange("h p d -> p h d"), ob[:, 0:128].rearrange("p (h d) -> p h d", h=2),
                               ok, reads=[ok], writes=[tk["vP"]])
                    else:
                        for hf in range(2):
                            cx.dma(PL, S["hvP"][:, :, ktg * 2 + hf, :].rearrange("h p d -> p h d"),
                                   ob[hf * 64:(hf + 1) * 64, 0:128].rearrange("p (h d) -> p h d", h=1), ok, reads=[ok], writes=[tk["hvP"]])
        ph.close()

    def phase_fox(l, S):
        ph = Phase(cx, f"fx_{l}")
        tk = S["tk"]
        SEG = N // 64
        lf, lk = ph.sbt([128, SEG], F32)
        cc, ck = ph.sbt([128, SEG], F32)
        r1, r1k = ph.sbt([128, SEG], F32)
        spl = ph.sb([128, 2, 3, SEG], BF16)
        onb = ph.sb([128, SEG], BF16)
        sk_ = Tk(); ph.tks.append(sk_)
        cx.dma(SP, lf[:], S["lfT"].rearrange("h (s j) -> (h s) j", s=64), lk, reads=[tk["lfT"]], writes=[lk])
        cx.op(DVE, lambda e: e.memset(r1[:], 1.0), writes=[r1k])
        cx.op(DVE, lambda e: e.memset(onb[:], 1.0), writes=[sk_])
        cx.op(DVE, lambda e: e.tensor_tensor_scan(out=cc[:], data0=r1[:], data1=lf[:], initial=0.0, op0=ALU.mult, op1=ALU.add),
              reads=[lk, r1k], writes=[ck])
        pb, pk = PS.next()
        mm_group(pb[:, 0:2], [(cst_f[:, 5, :], cc[:, SEG - 2:SEG])], [ck, c_tk], pk)
        off = ph.sb([128, 1], F32)
        cx.op(DVE, lambda e: e.tensor_copy(out=off[:], in_=pb[:, 1:2]), reads=[pk], writes=[lk])
        cx.op(DVE, lambda e: e.tensor_scalar(out=cc[:], in0=cc[:], scalar1=off[:, 0:1], scalar2=None, op0=ALU.add),
              reads=[ck, lk], writes=[ck])
        cx.op(DVE, lambda e: e.tensor_copy(out=spl[:, 0, 0, :], in_=cc[:]), reads=[ck], writes=[sk_])
        cx.op(DVE, lambda e: e.tensor_tensor(out=r1[:], in0=cc[:], in1=spl[:, 0, 0, :], op=ALU.subtract), reads=[ck, sk_], writes=[r1k])
        cx.op(DVE, lambda e: e.tensor_copy(out=spl[:, 0, 1, :], in_=r1[:]), reads=[r1k], writes=[sk_])
        cx.op(DVE, lambda e: e.tensor_tensor(out=r1[:], in0=r1[:], in1=spl[:, 0, 1, :], op=ALU.subtract), reads=[r1k, sk_], writes=[r1k])
        cx.op(DVE, lambda e: e.tensor_copy(out=spl[:, 0, 2, :], in_=r1[:]), reads=[r1k], writes=[sk_])
        cx.op(DVE, lambda e: e.tensor_scalar(out=spl[:, 1, :, :], in0=spl[:, 0, :, :], scalar1=-1.0, scalar2=None, op0=ALU.mult),
              reads=[sk_], writes=[sk_])
        for j in range(3):
            for (side, row, src) in ((0, j, onb[:]), (0, 3 + j, spl[:, 1, j, :]), (1, j, spl[:, 0, j, :]), (1, 3 + j, onb[:])):
                cx.dma(PL, S["aug"][side, row].rearrange("h (s j) -> (h s) j", s=64), src, sk_, reads=[sk_], writes=[tk["aug"]])

        QT = ph.pool(1, [70, N], BF16)
        KT = ph.pool(1, [70, N], BF16)
        VP = ph.pool(1, [128, NK, 65], BF16)
        for (vt, vk) in VP.tiles:
            cx.op(PL, lambda e, vt=vt: e.memset(vt[:, :, 64:65], 1.0), writes=[vk])
        pt_pool = ph.pool(3, [128, TT], BF16)
        rr_pool = ph.pool(2, [65, TT], F32)
        bc_pool = ph.pool(2, [64, TT], F32)
        y_pool = ph.pool(2, [64, TT], BF16)

        def ldh(hd):
            q, qk = QT.next(); k, kk = KT.next(); v, vk = VP.next()
            cx.dma(SP, q[0:64, :], S["qT"][hd * 64:(hd + 1) * 64, :], qk, reads=[tk["qT"]], writes=[qk])
            cx.dma(SP, q[64:70, :], S["aug"][1, :, hd, :], qk, reads=[tk["aug"]], writes=[qk])
            cx.dma(SP, k[0:64, :], S["kT"][hd * 64:(hd + 1) * 64, :], kk, reads=[tk["kT"]], writes=[kk])
            cx.dma(SP, k[64:70, :], S["aug"][0, :, hd, :], kk, reads=[tk["aug"]], writes=[kk])
            cx.dma(SP, v[:, :, 0:64], S["vP"][hd], vk, reads=[tk["vP"]], writes=[vk])
            return (q, qk, k, kk, v, vk)

        for hd in range(2):
            q, qk, k, kk, v, vk = ldh(hd)
            for qt in range(NT):
                ob, ok = PSA.next()
                nkb = 4 * qt + 4
                for kb in range(nkb):
                    r = kb - 4 * qt
                    f0 = 128 * r if r > 0 else 0
                    w = TT - f0
                    sb_, sk2 = PS.next()
                    mm_group(sb_[:, 0:w], [(k[0:70, kb * 128:(kb + 1) * 128], q[0:70, qt * TT + f0:(qt + 1) * TT])], [qk, kk], sk2)
                    pt, ptk = pt_pool.next()
                    cx.op(ACT, lambda e, pt=pt, sb_=sb_, w=w: e.activation(out=pt[:, 0:w], in_=sb_[:, 0:w], func=AF.Exp), reads=[sk2], writes=[ptk])
                    if r >= 0:
                        cx.op(DVE, lambda e, pt=pt: e.tensor_tensor(out=pt[:, 0:128], in0=pt[:, 0:128], in1=tri_b, op=ALU.mult),
                              reads=[ptk, c_tk], writes=[ptk])
                    cx.op(PE, lambda e, pt=pt, w=w, f0=f0, kb=kb, ob=ob, v=v, nkb=nkb: e.matmul(
                        ob[0:65, f0:TT], lhsT=v[:, kb, 0:65], rhs=pt[:, 0:w], start=(kb == 0), stop=(kb == nkb - 1)),
                        reads=[ptk, vk], writes=[ok], inc=True)
                rr, rrk = rr_pool.next()
                cx.op(DVE, lambda e: e.reciprocal(out=rr[64:65, :], in_=ob[64:65, :]), reads=[ok], writes=[rrk])
                bb, bk = PS.next()
                mm_group(bb[0:64, :], [(ones_f[64:65, 0:64], rr[64:65, :])], [rrk, c_tk], bk)
                bs, bsk = bc_pool.next()
                cx.op(ACT, lambda e: e.activation(out=bs[:], in_=bb[0:64, :], func=AF.Copy), reads=[bk], writes=[bsk])
                y, yk = y_pool.next()
                cx.op(DVE, lambda e: e.tensor_tensor(out=y[:], in0=ob[0:64, :], in1=bs[:], op=ALU.mult), reads=[ok, bsk], writes=[yk])
                cx.dma(PL, S["ysrc"][2 * qt:2 * qt + 2, hd * 64:(hd + 1) * 64, :].rearrange("c p n -> p c n"), y[:].rearrange("p (c n) -> p c n", c=2), yk, reads=[yk], writes=[tk["ysrc"]])
        ph.close()

    PI = float(np.pi)

    def phase_s5(l, S):
        ph = Phase(cx, f"s5_{l}")
        tk = S["tk"]
        t_tk = Tk(); ph.tks.append(t_tk)
        tmpi = ph.sb([128, TT], mybir.dt.int32)

        def red_pi(x, w):
            kf = tmpf[:, 0:w]
            cx.op(DVE, lambda e: e.tensor_scalar(out=kf, in0=x, scalar1=1.0 / (2 * PI), scalar2=0.5, op0=ALU.mult, op1=ALU.add),
                  reads=[t_tk], writes=[t_tk])
            cx.op(DVE, lambda e: e.tensor_copy(out=tmpi[:, 0:w], in_=kf), reads=[t_tk], writes=[t_tk])
            cx.op(DVE, lambda e: e.tensor_copy(out=kf, in_=tmpi[:, 0:w]), reads=[t_tk], writes=[t_tk])
            cx.op(DVE, lambda e: e.scalar_tensor_tensor(out=x, in0=kf, scalar=-6.28125, in1=x, op0=ALU.mult, op1=ALU.add),
                  reads=[t_tk], writes=[t_tk])
            cx.op(DVE, lambda e: e.scalar_tensor_tensor(out=x, in0=kf, scalar=-0.0019353071795864769, in1=x, op0=ALU.mult, op1=ALU.add),
                  reads=[t_tk], writes=[t_tk])
            fix_pi(x, w)

        def fix_pi(x, w):
            m = tmpf[:, 0:w]
            cx.op(DVE, lambda e: e.tensor_scalar(out=m, in0=x, scalar1=-PI, scalar2=None, op0=ALU.is_lt), reads=[t_tk], writes=[t_tk])
            cx.op(DVE, lambda e: e.scalar_tensor_tensor(out=x, in0=m, scalar=2 * PI, in1=x, op0=ALU.mult, op1=ALU.add), reads=[t_tk], writes=[t_tk])
            cx.op(DVE, lambda e: e.tensor_scalar(out=m, in0=x, scalar1=PI, scalar2=None, op0=ALU.is_gt), reads=[t_tk], writes=[t_tk])
            cx.op(DVE, lambda e: e.scalar_tensor_tensor(out=x, in0=m, scalar=-2 * PI, in1=x, op0=ALU.mult, op1=ALU.add), reads=[t_tk], writes=[t_tk])
            cx.op(DVE, lambda e: e.tensor_scalar(out=x, in0=x, scalar1=PI, scalar2=-PI, op0=ALU.min, op1=ALU.max), reads=[t_tk], writes=[t_tk])

        def sincos(x, w, sin_out, cos_out):
            red_pi(x, w)
            cx.op(ACT, lambda e: e.activation(out=sin_out, in_=x, func=AF.Sin), reads=[t_tk], writes=[t_tk])
            cx.op(DVE, lambda e: e.tensor_scalar(out=x, in0=x, scalar1=PI / 2, scalar2=None, op0=ALU.add), reads=[t_tk], writes=[t_tk])
            fix_pi(x, w)
            cx.op(ACT, lambda e: e.activation(out=cos_out, in_=x, func=AF.Sin), reads=[t_tk], writes=[t_tk])

        tmpf = ph.sb([128, TT], F32)
        ang = ph.sb([128, TT], F32)
        jv = ph.sb([128, TT], F32)
        m16 = ph.sb([128, 8], F32)
        cx.dma(SP, jv[:], jv_in, t_tk, writes=[t_tk])
        cx.dma(SP, m16[:], m16_in, t_tk, writes=[t_tk])
        anat = ph.sb([8, 2, 128], F32)
        for hf in range(2):
            cx.dma(SP, anat[:, 0, hf * 64:(hf + 1) * 64], P["s5_a_re"][l], t_tk, writes=[t_tk])
            cx.dma(SP, anat[:, 1, hf * 64:(hf + 1) * 64], P["s5_a_im"][l], t_tk, writes=[t_tk])
        dt_pr = ph.sb([128, 8], F32)
        cx.dma(SP, dt_pr[:], P["s5_log_dt"][l:l + 1, :].partition_broadcast(128).rearrange("p o g -> p (o g)"), t_tk, writes=[t_tk])
        cx.op(ACT, lambda e: e.activation(out=dt_pr[:], in_=dt_pr[:], func=AF.Exp), reads=[t_tk], writes=[t_tk])
        mag_pr = ph.sb([128, 8], F32)
        th_pr = ph.sb([128, 8], F32)
        c512 = ph.sb([128, 8], F32)
        s512 = ph.sb([128, 8], F32)
        for which, dst in ((0, mag_pr), (1, th_pr)):
            pb, pk = PS.next()
            cx.op(PE, lambda e, which=which: e.transpose(pb[:, 0:8], anat[:, which, :], ident_f[0:8, 0:8]), reads=[t_tk, c_tk], writes=[pk])
            cx.op(DVE, lambda e, dst=dst: e.tensor_tensor(out=dst[:], in0=pb[:, 0:8], in1=dt_pr[:], op=ALU.mult), reads=[pk, t_tk], writes=[t_tk])
        cx.op(ACT, lambda e: e.activation(out=mag_pr[:], in_=mag_pr[:], func=AF.Exp), reads=[t_tk], writes=[t_tk])
        cx.op(DVE, lambda e: e.tensor_scalar(out=ang[:, 0:8], in0=th_pr[:], scalar1=float(TT), scalar2=None, op0=ALU.mult), reads=[t_tk], writes=[t_tk])
        sincos(ang[:, 0:8], 8, s512[:], c512[:])
        cosT = ph.sb([128, 8, TT], BF16)
        sinT = ph.sb([128, 8, TT], BF16)
        for g in range(8):
            cx.op(DVE, lambda e, g=g: e.tensor_scalar(out=ang[:], in0=jv[:], scalar1=th_pr[:, g:g + 1], scalar2=None, op0=ALU.mult),
                  reads=[t_tk], writes=[t_tk])
            sincos(ang[:], TT, sinT[:, g, :], cosT[:, g, :])
        Bst = ph.sb([128, 8, 128], BF16)
        Bsw = ph.sb([128, 8, 128], BF16)
        Cg = ph.sb([128, 8, 128], BF16)
        dsk = load_vec(ph, P["s5_d"][l], 1, t_tk)
        cx.op(PL, lambda e: e.memset(Cg[:], 0.0), writes=[t_tk])
        sm = [ph.sb([128, 64], F32) for _ in range(12)]
        ar_, ai_, mg_, th_, sn_, cs_, nr_, ni_, dn_, cr_, ci_, t_ = sm
        dt_gh = ph.sb([128, 1], F32)
        bnat = ph.sb([64, 2, 128], F32)
        bgh = ph.sb([128, 2, 64], F32)
        ball = ph.sb([128, 2, 128], F32)
        cnat = ph.sb([128, 128], F32)
        call = ph.sb([128, 128], F32)
        for Q in range(1):
            G0 = Q * 8
            for g in range(8):
                cx.dma(SP, ar_[g * 16:(g + 1) * 16, :], P["s5_a_re"][l, G0 + g:G0 + g + 1, :].partition_broadcast(16).rearrange("p o g -> p (o g)"), t_tk, writes=[t_tk])
                cx.dma(SP, ai_[g * 16:(g + 1) * 16, :], P["s5_a_im"][l, G0 + g:G0 + g + 1, :].partition_broadcast(16).rearrange("p o g -> p (o g)"), t_tk, writes=[t_tk])
                cx.dma(SP, dt_gh[g * 16:(g + 1) * 16, :], P["s5_log_dt"][l:l + 1, G0 + g:G0 + g + 1].partition_broadcast(16).rearrange("p o g -> p (o g)"), t_tk, writes=[t_tk])
            cx.op(ACT, lambda e: e.activation(out=dt_gh[:], in_=dt_gh[:], func=AF.Exp), reads=[t_tk], writes=[t_tk])
            cx.op(ACT, lambda e: e.activation(out=mg_[:], in_=ar_[:], func=AF.Exp, scale=dt_gh[:, 0:1]), reads=[t_tk], writes=[t_tk])
            cx.op(DVE, lambda e: e.tensor_scalar(out=th_[:], in0=ai_[:], scalar1=dt_gh[:, 0:1], scalar2=None, op0=ALU.mult), reads=[t_tk], writes=[t_tk])
            sincos(th_[:], 64, sn_[:], cs_[:])
            TTo = lambda o, a, b, op: cx.op(DVE, lambda e: e.tensor_tensor(out=o, in0=a, in1=b, op=op), reads=[t_tk], writes=[t_tk])
            TTo(nr_[:], mg_[:], cs_[:], ALU.mult)
            cx.op(DVE, lambda e: e.tensor_scalar(out=nr_[:], in0=nr_[:], scalar1=-1.0, scalar2=None, op0=ALU.add), reads=[t_tk], writes=[t_tk])
            TTo(ni_[:], mg_[:], sn_[:], ALU.mult)
            TTo(dn_[:], ar_[:], ar_[:], ALU.mult)
            TTo(t_[:], ai_[:], ai_[:], ALU.mult)
            TTo(dn_[:], dn_[:], t_[:], ALU.add)
            cx.op(DVE, lambda e: e.reciprocal(out=dn_[:], in_=dn_[:]), reads=[t_tk], writes=[t_tk])
            TTo(cr_[:], nr_[:], ar_[:], ALU.mult)
            TTo(t_[:], ni_[:], ai_[:], ALU.mult)
            TTo(cr_[:], cr_[:], t_[:], ALU.add)
            TTo(cr_[:], cr_[:], dn_[:], ALU.mult)
            TTo(ci_[:], ni_[:], ar_[:], ALU.mult)
            TTo(t_[:], nr_[:], ai_[:], ALU.mult)
            TTo(ci_[:], ci_[:], t_[:], ALU.subtract)
            TTo(ci_[:], ci_[:], dn_[:], ALU.mult)
            cx.dma(SP, bnat[:, 0, :].rearrange("p (g h) -> p g h", h=16), P["s5_b_re"][l, G0:G0 + 8].rearrange("g p h -> p g h"), t_tk, writes=[t_tk])
            cx.dma(SP, bnat[:, 1, :].rearrange("p (g h) -> p g h", h=16), P["s5_b_im"][l, G0:G0 + 8].rearrange("g p h -> p g h"), t_tk, writes=[t_tk])
            for ri in range(2):
                pb, pk = PS.next()
                cx.op(PE, lambda e, ri=ri, pb=pb: e.transpose(pb[:, 0:64], bnat[:, ri, :], ident_f[0:64, 0:64]), reads=[t_tk, c_tk], writes=[pk])
                cx.op(DVE, lambda e, ri=ri, pb=pb: e.tensor_copy(out=bgh[:, ri, :], in_=pb[:, 0:64]), reads=[pk], writes=[t_tk])
            TTo(ball[:, 0, 0:64], cr_[:], bgh[:, 0, :], ALU.mult)
            TTo(t_[:], ci_[:], bgh[:, 1, :], ALU.mult)
            TTo(ball[:, 0, 0:64], ball[:, 0, 0:64], t_[:], ALU.subtract)
            TTo(ball[:, 0, 64:128], cr_[:], bgh[:, 1, :], ALU.mult)
            TTo(t_[:], ci_[:], bgh[:, 0, :], ALU.mult)
            TTo(ball[:, 0, 64:128], ball[:, 0, 64:128], t_[:], ALU.add)
            cx.op(DVE, lambda e: e.tensor_copy(out=ball[:, 1, 0:64], in_=ball[:, 0, 64:128]), reads=[t_tk], writes=[t_tk])
            cx.op(DVE, lambda e: e.tensor_scalar(out=ball[:, 1, 64:128], in0=ball[:, 0, 0:64], scalar1=-1.0, scalar2=None, op0=ALU.mult),
                  reads=[t_tk], writes=[t_tk])
            for g in range(8):
                cx.op(DVE, lambda e, g=g: e.tensor_scalar(out=Bst[:, G0 + g, :], in0=ball[:, 0, :], scalar1=m16[:, g:g + 1], scalar2=None, op0=ALU.mult),
                      reads=[t_tk], writes=[t_tk])
                cx.op(DVE, lambda e, g=g: e.tensor_scalar(out=Bsw[:, G0 + g, :], in0=ball[:, 1, :], scalar1=m16[:, g:g + 1], scalar2=None, op0=ALU.mult),
                      reads=[t_tk], writes=[t_tk])
            cx.dma(SP, cnat[:, 0:64], P["s5_c_re"][l, G0:G0 + 8].rearrange("g h p -> (g h) p"), t_tk, writes=[t_tk])
            cx.dma(SP, cnat[:, 64:128], P["s5_c_im"][l, G0:G0 + 8].rearrange("g h p -> (g h) p"), t_tk, writes=[t_tk])
            pb, pk = PS.next()
            cx.op(PE, lambda e, pb=pb: e.transpose(pb[:, 0:128], cnat[:], ident_f), reads=[t_tk, c_tk], writes=[pk])
            cx.op(DVE, lambda e, pb=pb: e.tensor_copy(out=call[0:64, :], in_=pb[0:64, 0:128]), reads=[pk], writes=[t_tk])
            cx.op(DVE, lambda e, pb=pb: e.tensor_scalar(out=call[64:128, :], in0=pb[64:128, 0:128], scalar1=-1.0, scalar2=None, op0=ALU.mult),
                  reads=[pk], writes=[t_tk])
            for g in range(8):
                cx.op(DVE, lambda e, g=g: e.tensor_copy(out=Cg[:, G0 + g, g * 16:(g + 1) * 16], in_=call[:, g * 16:(g + 1) * 16]),
                      reads=[t_tk], writes=[t_tk])
        vin = ph.sb([128, 8], F32)
        vsin = ph.sb([128, 8], F32)
        cx.op(DVE, lambda e: e.memset(vin[:], 0.0), writes=[t_tk])
        cx.op(DVE, lambda e: e.memset(vsin[:], 0.0), writes=[t_tk])
        st_tk = Tk(); ph.tks.append(st_tk)
        u_pool = ph.pool(2, [128, 1, TT], BF16)
        f_pool = ph.pool(6, [128, TT], F32)
        v_pool = ph.pool(2, [128, 2, TT], F32)
        h_pool = ph.pool(2, [128, TT], BF16)
        c_pool = ph.pool(2, [128, 2], F32)
        y_pool = ph.pool(2, [128, TT], F32)
        z_pool = ph.pool(2, [128, TT], BF16)
        ur = S["uT"].rearrange("(q p) n -> p q n", p=128)

        def ld(i):
            u, uk = u_pool.next()
            cx.dma(SP, u[:], ur[:, :, i * TT:(i + 1) * TT], uk, reads=[tk["uT"]], writes=[uk])
            return u, uk

        nxt = ld(0)
        for i in range(NT):
            u, uk = nxt
            if i + 1 < NT:
                nxt = ld(i + 1)
            for Q in range(1):
                py, pyk = PSA.next()
                for g in range(8):
                    G_ = Q * 8 + g
                    pd, pdk = PS.next()
                    mm_group(pd[:], [(Bst[:, G_, :], u[:, Q, :])], [t_tk, uk], pdk)
                    pds, pdsk = PS.next()
                    mm_group(pds[:], [(Bsw[:, G_, :], u[:, Q, :])], [t_tk, uk], pdsk)
                    t1, t1k = f_pool.next(); t2, t2k = f_pool.next()
                    cx.op(DVE, lambda e: e.tensor_tensor(out=t1[:], in0=pd[:], in1=cosT[:, G_, :], op=ALU.mult), reads=[pdk, t_tk], writes=[t1k])
                    cx.op(DVE, lambda e: e.tensor_tensor(out=t2[:], in0=pds[:], in1=sinT[:, G_, :], op=ALU.mult), reads=[pdsk, t_tk], writes=[t2k])
                    cx.op(DVE, lambda e: e.tensor_tensor(out=t1[:], in0=t1[:], in1=t2[:], op=ALU.add), reads=[t1k, t2k], writes=[t1k])
                    t3, t3k = f_pool.next(); t4, t4k = f_pool.next()
                    cx.op(DVE, lambda e: e.tensor_tensor(out=t3[:], in0=pds[:], in1=cosT[:, G_, :], op=ALU.mult), reads=[pdsk, t_tk], writes=[t3k])
                    cx.op(DVE, lambda e: e.tensor_tensor(out=t4[:], in0=pd[:], in1=sinT[:, G_, :], op=ALU.mult), reads=[pdk, t_tk], writes=[t4k])
                    cx.op(DVE, lambda e: e.tensor_tensor(out=t3[:], in0=t3[:], in1=t4[:], op=ALU.subtract), reads=[t3k, t4k], writes=[t3k])
                    vv, vvk = v_pool.next()
                    magb = mag_pr[:, G_:G_ + 1].to_broadcast([128, TT])
                    cx.op(DVE, lambda e: e.tensor_tensor_scan(out=vv[:, 0, :], data0=magb, data1=t1[:], initial=vin[:, G_:G_ + 1], op0=ALU.mult, op1=ALU.add),
                          reads=[t1k, t_tk, st_tk], writes=[vvk])
                    cx.op(DVE, lambda e: e.tensor_tensor_scan(out=vv[:, 1, :], data0=magb, data1=t3[:], initial=vsin[:, G_:G_ + 1], op0=ALU.mult, op1=ALU.add),
                          reads=[t3k, t_tk, st_tk], writes=[vvk])
                    cc_, cck = c_pool.next()
                    cx.op(DVE, lambda e: e.tensor_scalar(out=cc_[:, 0:1], in0=vv[:, 1, TT - 1:TT], scalar1=s512[:, G_:G_ + 1], scalar2=None, op0=ALU.mult),
                          reads=[vvk, t_tk], writes=[cck])
                    cx.op(DVE, lambda e: e.tensor_scalar(out=cc_[:, 1:2], in0=vv[:, 0, TT - 1:TT], scalar1=s512[:, G_:G_ + 1], scalar2=None, op0=ALU.mult),
                          reads=[vvk, t_tk], writes=[cck])
                    cx.op(DVE, lambda e: e.scalar_tensor_tensor(out=vin[:, G_:G_ + 1], in0=vv[:, 0, TT - 1:TT], scalar=c512[:, G_:G_ + 1], in1=cc_[:, 0:1],
                                                               op0=ALU.mult, op1=ALU.subtract), reads=[vvk, cck, t_tk], writes=[st_tk])
                    cx.op(DVE, lambda e: e.scalar_tensor_tensor(out=vsin[:, G_:G_ + 1], in0=vv[:, 1, TT - 1:TT], scalar=c512[:, G_:G_ + 1], in1=cc_[:, 1:2],
                                                               op0=ALU.mult, op1=ALU.add), reads=[vvk, cck, t_tk], writes=[st_tk])
                    cx.op(DVE, lambda e: e.tensor_tensor(out=t2[:], in0=vv[:, 0, :], in1=cosT[:, G_, :], op=ALU.mult), reads=[vvk, t_tk], writes=[t2k])
                    cx.op(DVE, lambda e: e.tensor_tensor(out=t4[:], in0=vv[:, 1, :], in1=sinT[:, G_, :], op=ALU.mult), reads=[vvk, t_tk], writes=[t4k])
                    hh, hhk = h_pool.next()
                    cx.op(DVE, lambda e: e.tensor_tensor(out=hh[:], in0=t2[:], in1=t4[:], op=ALU.subtract), reads=[t2k, t4k], writes=[hhk])
                    cx.op(PE, lambda e, g=g: e.matmul(py[:], lhsT=Cg[:, G_, :], rhs=hh[:], start=(g == 0), stop=(g == 7)),
                          reads=[hhk, t_tk], writes=[pyk], inc=True)
                yv, yvk = y_pool.next()
                cx.op(DVE, lambda e: e.scalar_tensor_tensor(out=yv[:], in0=u[:, Q, :], scalar=dsk[:, Q:Q + 1], in1=py[:], op0=ALU.mult, op1=ALU.add),
                      reads=[uk, pyk, t_tk], writes=[yvk])
                g1_, g1k = f_pool.next()
                cx.op(DVE, lambda e: e.tensor_tensor(out=g1_[:], in0=yv[:], in1=yv[:], op=ALU.mult), reads=[yvk], writes=[g1k])
                cx.op(DVE, lambda e: e.tensor_scalar(out=g1_[:], in0=g1_[:], scalar1=0.044715, scalar2=1.0, op0=ALU.mult, op1=ALU.add), reads=[g1k], writes=[g1k])
                cx.op(DVE, lambda e: e.tensor_tensor(out=g1_[:], in0=g1_[:], in1=yv[:], op=ALU.mult), reads=[g1k, yvk], writes=[g1k])
                cx.op(ACT, lambda e: e.activation(out=g1_[:], in_=g1_[:], func=AF.Sigmoid, scale=2.0 * 0.7978845608028654), reads=[g1k], writes=[g1k])
                z, zk = z_pool.next()
                cx.op(DVE, lambda e: e.tensor_tensor(out=z[:], in0=yv[:], in1=g1_[:], op=ALU.mult), reads=[yvk, g1k], writes=[zk])
                cx.dma(PL, S["ysrc"][2 * i:2 * i + 2, 128:256, :].rearrange("c p n -> p c n"), z[:].rearrange("p (c n) -> p c n", c=2), zk, reads=[zk], writes=[tk["ysrc"]])
        ph.close()

    def phase_hg(l, S):
        ph = Phase(cx, f"hg_{l}")
        tk = S["tk"]
        g_tk = Tk(); ph.tks.append(g_tk)
        og = ph.sb([128, 1], F32)
        cx.dma(SP, og[:], P["hg_onorm"][l].rearrange("(p o) -> p o", o=1), g_tk, writes=[g_tk])
        msk = ph.sb([128, TT], F32)
        cx.op(DVE, lambda e: e.memset(msk[:], 1.0), writes=[g_tk])
        cx.op(DVE, lambda e: e.memset(msk[:].rearrange("p (c j) -> p c j", j=HC)[:, :, 0:1], 0.0), writes=[g_tk])
        inb = ph.pool(2, [128, 3, TT], BF16)
        inf = ph.pool(2, [128, TT], F32)
        inv = ph.pool(2, [64, 8, 128], BF16)
        b_pool = ph.pool(2, [128, TT], F32)
        d_pool = ph.pool(2, [128, TT], F32)
        e_pool = ph.pool(3, [128, TT], F32)
        w_pool = ph.pool(2, [128, 4, TT], BF16)
        eb_pool = ph.pool(2, [128, 8], F32)
        am_pool = ph.pool(2, [64, TT], BF16)
        kt_pool = ph.pool(2, [64, 8, 128], BF16)
        stb_pool = ph.pool(2, [128, 9, 128], BF16)
        stf = [ph.sbt([128, 128], F32) for _ in range(2)]
        sq_pool = ph.pool(2, [128, TT], BF16)
        rs_pool = ph.pool(2, [128, TT], F32)
        y_pool = ph.pool(2, [128, TT], BF16)
        nch = TT // HC

        def ld(hd, i):
            ib, ibk = inb.next(); lf, lfk = inf.next(); v, vk = inv.next()
            sl = slice(i * TT, (i + 1) * TT)
            r0 = hd * 128
            cx.dma(SP, ib[:, 0, :], S["hqT"][r0:r0 + 128, sl], ibk, reads=[tk["hqT"]], writes=[ibk])
            cx.dma(SP, ib[:, 1, :], S["khT"][r0:r0 + 128, sl], ibk, reads=[tk["khT"]], writes=[ibk])
            cx.dma(SP, ib[:, 2, :], S["hgT"][r0:r0 + 128, sl], ibk, reads=[tk["hgT"]], writes=[ibk])
            cx.dma(SP, lf[:], S["lfhT"][r0:r0 + 128, sl], lfk, reads=[tk["lfhT"]], writes=[lfk])
            cx.dma(SP, v[:], S["hvP"][hd, :, i * nch:(i + 1) * nch, :], vk, reads=[tk["hvP"]], writes=[vk])
            return ib, ibk, lf, lfk, v, vk

        for hd in range(1):
            cur = 0
            cx.op(DVE, lambda e: e.memset(stf[0][0][:], 0.0), writes=[stf[0][1]])
            stb, stbk = stb_pool.next()
            cx.op(PL, lambda e, stb=stb: e.memset(stb[:, 0, :], 0.0), writes=[stbk])
            nxt = ld(hd, 0)
            for i in range(NT):
                ib, ibk, lf, lfk, v, vk = nxt
                if i + 1 < NT:
                    nxt = ld(hd, i + 1)
                b, bk = b_pool.next()
                cx.op(DVE, lambda e: e.tensor_tensor_scan(out=b[:], data0=msk[:], data1=lf[:], initial=0.0, op0=ALU.mult, op1=ALU.add),
                      reads=[lfk, g_tk], writes=[bk])
                b3 = b[:].rearrange("p (c j) -> p c j", j=HC)
                w, wk = w_pool.next()
                d1, d1k = d_pool.next()
                cx.op(DVE, lambda e: e.tensor_tensor(out=d1[:].rearrange("p (c j) -> p c j", j=HC), in0=b3,
                                                     in1=b3[:, :, HC // 2 - 1:HC // 2].to_broadcast([128, nch, HC]), op=ALU.subtract),
                      reads=[bk], writes=[d1k])
                e1, e1k = e_pool.next()
                cx.op(ACT, lambda e: e.activation(out=e1[:], in_=d1[:], func=AF.Exp), reads=[d1k], writes=[e1k])
                cx.op(DVE, lambda e: e.tensor_tensor(out=w[:, 0, :], in0=ib[:, 0, :], in1=e1[:], op=ALU.mult), reads=[ibk, e1k], writes=[wk])
                e2, e2k = e_pool.next()
                cx.op(ACT, lambda e: e.activation(out=e2[:], in_=d1[:], func=AF.Exp, scale=-1.0), reads=[d1k], writes=[e2k])
                cx.op(DVE, lambda e: e.tensor_tensor(out=w[:, 1, :], in0=ib[:, 1, :], in1=e2[:], op=ALU.mult), reads=[ibk, e2k], writes=[wk])
                e3, e3k = e_pool.next()
                cx.op(ACT, lambda e: e.activation(out=e3[:], in_=b[:], func=AF.Exp), reads=[bk], writes=[e3k])
                cx.op(DVE, lambda e: e.tensor_tensor(out=w[:, 2, :], in0=ib[:, 0, :], in1=e3[:], op=ALU.mult), reads=[ibk, e3k], writes=[wk])
                d2, d2k = d_pool.next()
                cx.op(DVE, lambda e: e.tensor_tensor(out=d2[:].rearrange("p (c j) -> p c j", j=HC), in0=b3,
                                                     in1=b3[:, :, HC - 1:HC].to_broadcast([128, nch, HC]), op=ALU.subtract),
                      reads=[bk], writes=[d2k])
                e4, e4k = e_pool.next()
                cx.op(ACT, lambda e: e.activation(out=e4[:], in_=d2[:], func=AF.Exp, scale=-1.0), reads=[d2k], writes=[e4k])
                cx.op(DVE, lambda e: e.tensor_tensor(out=w[:, 3, :], in0=ib[:, 1, :], in1=e4[:], op=ALU.mult), reads=[ibk, e4k], writes=[wk])
                eb, ebk = eb_pool.next()
                cx.op(ACT, lambda e: e.activation(out=eb[:].unsqueeze(2), in_=b3[:, :, HC - 1:HC], func=AF.Exp), reads=[bk], writes=[ebk])
                pa, pak = PS.next()
                for c in range(nch):
                    cs = slice(c * HC, (c + 1) * HC)
                    cx.op(PE, lambda e, cs=cs: e.matmul(pa[0:HC, cs], lhsT=w[:, 1, cs], rhs=w[:, 0, cs], start=True, stop=True),
                          reads=[wk], writes=[pak], inc=(c == nch - 1))
                ptp, ptk_ = PS.next()
                ptb = ptp[:].bitcast(BF16)
                for c in range(nch):
                    cs = slice(c * HC, (c + 1) * HC)
                    cx.op(PE, lambda e, cs=cs, c=c: e.transpose(ptb[0:HC, c * 128:(c + 1) * 128], w[:, 3, cs], ident_b),
                          reads=[wk, c_tk], writes=[ptk_], inc=(c == nch - 1))
                am, amk = am_pool.next()
                cx.op(DVE, lambda e: e.tensor_tensor(out=am[:].rearrange("p (c j) -> p c j", j=HC),
                                                     in0=pa[0:HC, :].rearrange("p (c j) -> p c j", j=HC),
                                                     in1=tri_b[0:HC, 0:HC].unsqueeze(1).to_broadcast([HC, nch, HC]), op=ALU.mult),
                      reads=[pak, c_tk], writes=[amk])
                kt, ktk = kt_pool.next()
                cx.op(ACT, lambda e: e.activation(out=kt[:].rearrange("p c k -> p (c k)"), in_=ptb[0:HC, 0:nch * 128], func=AF.Copy),
                      reads=[ptk_], writes=[ktk])
                kvs = []
                for half in range(2):
                    pkv, pkvk = PS.next()
                    for c4 in range(4):
                        c = half * 4 + c4
                        cx.op(PE, lambda e, c=c, c4=c4, pkv=pkv: e.matmul(pkv[:, c4 * 128:(c4 + 1) * 128], lhsT=kt[:, c, :], rhs=v[:, c, :],
                                                                        start=True, stop=True),
                              reads=[ktk, vk], writes=[pkvk], inc=(c4 == 3))
                    kvs.append((pkv, pkvk))
                for c in range(nch):
                    pkv, pkvk = kvs[c // 4]
                    src, srck = stf[cur]
                    dst, dstk = stf[1 - cur]
                    cx.op(DVE, lambda e, c=c, pkv=pkv, src=src, dst=dst: e.scalar_tensor_tensor(
                        out=dst[:], in0=src[:], scalar=eb[:, c:c + 1], in1=pkv[:, (c % 4) * 128:(c % 4 + 1) * 128], op0=ALU.mult, op1=ALU.add),
                        reads=[srck, ebk, pkvk], writes=[dstk])
                    cur = 1 - cur
                    if c < nch - 1:
                        cx.op(PL, lambda e, c=c, dst=dst, stb=stb: e.tensor_copy(out=stb[:, c + 1, :], in_=dst[:]), reads=[dstk], writes=[stbk])
                    else:
                        nstb, nstbk = stb_pool.next()
                        cx.op(PL, lambda e, dst=dst, nstb=nstb: e.tensor_copy(out=nstb[:, 0, :], in_=dst[:]), reads=[dstk], writes=[nstbk])
                po, pok = PS.next()
                for c in range(nch):
                    cs = slice(c * HC, (c + 1) * HC)
                    cx.op(PE, lambda e, cs=cs, c=c, stb=stb: e.matmul(po[:, cs], lhsT=stb[:, c, :], rhs=w[:, 2, cs], start=True, stop=False),
                          reads=[stbk, wk], writes=[pok], inc=False)
                    cx.op(PE, lambda e, cs=cs, c=c: e.matmul(po[:, cs], lhsT=v[:, c, :], rhs=am[0:HC, cs], start=False, stop=True),
                          reads=[vk, amk], writes=[pok], inc=(c == nch - 1))
                stb, stbk = nstb, nstbk
                sq, sqk = sq_pool.next()
                cx.op(ACT, lambda e: e.activation(out=sq[:], in_=po[:], func=AF.Square), reads=[pok], writes=[sqk])
                pn, pnk = PS.next()
                mm_group(pn[:], [(ones_b, sq[:])], [sqk, c_tk], pnk)
                rs, rk = rs_pool.next()
                cx.op(ACT, lambda e: e.activation(out=rs[:], in_=pn[:], func=AF.Sqrt, scale=1.0 / 128, bias=eps_t[:, 0:1]),
                      reads=[pnk, c_tk], writes=[rk])
                cx.op(DVE, lambda e: e.reciprocal(out=rs[:], in_=rs[:]), reads=[rk], writes=[rk])
                cx.op(DVE, lambda e: e.scalar_tensor_tensor(out=rs[:], in0=po[:], scalar=og[:, 0:1], in1=rs[:], op0=ALU.mult, op1=ALU.mult),
                      reads=[pok, rk, g_tk], writes=[rk])
                y, yk = y_pool.next()
                cx.op(DVE, lambda e: e.tensor_tensor(out=y[:], in0=rs[:], in1=ib[:, 2, :], op=ALU.mult), reads=[rk, ibk], writes=[yk])
                cx.dma(PL, S["ysrc"][2 * i:2 * i + 2, 256:384, :].rearrange("c p n -> p c n"), y[:].rearrange("p (c n) -> p c n", c=2), yk, reads=[yk], writes=[tk["ysrc"]])
        ph.close()

    def phase3a(l, xT, x_tk, S):
        ph = Phase(cx, f"p3a_{l}")
        tk = S["tk"]
        stg_pool = ph.pool(2, [128, 8, 512], F32)
        g_tk = Tk(); ph.tks.append(g_tk)
        g1 = load_vec(ph, P["norm_mix"][l], 8, g_tk)
        bglu = load_vec(ph, P["s5_b_glu"][l], 4, g_tk)
        Wg, wg_tk = ph.sbt([128, 8, 3072], BF16)
        Wb, wb_tk = ph.sbt([128, 12, 1024], BF16)
        Wo, wo_tk = ph.sbt([128, 8, 1024], BF16)
        Wgl, wgl_tk = ph.sbt([128, 4, 512], BF16)
        load_w(ph, P["w_in"][l], 8, 1026, 4098, gain=(g1, g_tk), stg_pool=stg_pool, w_out=Wg, w_tk=wg_tk)
        load_w(ph, P["w_branch"][l], 12, 0, 1024, stg_pool=stg_pool, w_out=Wb, w_tk=wb_tk)
        load_w(ph, P["w_out"][l], 8, 0, 1024, stg_pool=stg_pool, w_out=Wo, w_tk=wo_tk)
        load_w(ph, P["s5_w_glu"][l], 4, 0, 512, stg_pool=stg_pool, w_out=Wgl, w_tk=wgl_tk)
        TW = 256
        x_pool = ph.pool(2, [128, 8, TW], F32)
        y_pool = ph.pool(2, [128, 12, TW], BF16)
        sq_pool = ph.pool(1, [128, 8, TW], BF16)
        h_pool = ph.pool(1, [128, 8, TW], BF16)
        rs_pool = ph.pool(2, [128, TW], F32)
        mg_pool = ph.pool(1, [128, 8, TW], BF16)
        mgs, mgsk = ph.sbt([128, 4, TW], BF16)
        evf = ph.pool(4, [128, TW], F32)
        xr = xT.rearrange("(kt p) n -> p kt n", p=128)
        x1r = S["x1"].rearrange("(kt p) n -> p kt n", p=128)

        def ld(i):
            xt, xk = x_pool.next()
            cx.dma(SP, xt[:], xr[:, :, i * TW:(i + 1) * TW], xk, reads=[x_tk], writes=[xk])
            yt, yk = y_pool.next()
            for b_ in range(3):
                cx.dma(SP, yt[:, b_ * 4:(b_ + 1) * 4, :], S["ydst"][i].rearrange("(r b p) n -> b p r n", r=4, b=3, p=128)[b_], yk, reads=[tk["ydst"]], writes=[yk])
            return xt, xk, yt, yk

        nxt = ld(0)
        for i in range(N // TW):
            xt, xk, yt, yk = nxt
            if i + 1 < N // TW:
                nxt = ld(i + 1)
            rs, rk = rms_rstd(ph, xt[:], xk, 8, TW, sq_pool, rs_pool)
            h, hk = h_pool.next()
            cx.op(DVE, lambda e: e.tensor_tensor(out=h[:], in0=xt[:], in1=bc(rs[:], [128, 8, TW], 1), op=ALU.mult),
                  reads=[xk, rk], writes=[hk])
            for ct in range(4):
                pb, pk = PS.next()
                mm_group(pb[:, 0:TW], [(Wgl[:, k, ct * 128:(ct + 1) * 128], yt[:, 4 + k, :]) for k in range(4)], [wgl_tk, yk], pk)
                sg, sgk = evf.next()
                cx.op(ACT, lambda e, ct=ct: e.activation(out=sg[:], in_=pb[:, 0:TW], func=AF.Sigmoid, bias=bglu[:, ct:ct + 1]),
                      reads=[pk, g_tk], writes=[sgk])
                cx.op(DVE, lambda e, ct=ct: e.tensor_tensor(out=mgs[:, ct, :], in0=yt[:, 4 + ct, :], in1=sg[:], op=ALU.mult),
                      reads=[yk, sgk], writes=[mgsk])
            mg, mgk = mg_pool.next()
            for ct in range(8):
                acc, acck = evf.next()
                for b in range(3):
                    pg, pgk = PS.next()
                    c0 = b * 1024 + ct * 128
                    mm_group(pg[:, 0:TW], [(Wg[:, k, c0:c0 + 128], h[:, k, :]) for k in range(8)], [wg_tk, hk], pgk)
                    gs, gsk = evf.next()
                    cx.op(ACT, lambda e: e.activation(out=gs[:], in_=pg[:, 0:TW], func=AF.Sigmoid), reads=[pgk], writes=[gsk])
                    pbr, pbk = PS.next()
                    if b == 1:
                        prs = [(Wb[:, 4 + k, ct * 128:(ct + 1) * 128], mgs[:, k, :]) for k in range(4)]
                        rd = [wb_tk, mgsk]
                    else:
                        prs = [(Wb[:, b * 4 + k, ct * 128:(ct + 1) * 128], yt[:, b * 4 + k, :]) for k in range(4)]
                        rd = [wb_tk, yk]
                    mm_group(pbr[:, 0:TW], prs, rd, pbk)
                    if b == 0:
                        cx.op(DVE, lambda e: e.tensor_tensor(out=acc[:], in0=pbr[:, 0:TW], in1=gs[:], op=ALU.mult),
                              reads=[pbk, gsk], writes=[acck])
                    else:
                        cx.op(DVE, lambda e: e.tensor_tensor(out=gs[:], in0=pbr[:, 0:TW], in1=gs[:], op=ALU.mult),
                              reads=[pbk, gsk], writes=[gsk])
                        dstap = mg[:, ct, :] if b == 2 else acc[:]
                        dstk = mgk if b == 2 else acck
                        cx.op(DVE, lambda e, dstap=dstap: e.tensor_tensor(out=dstap, in0=acc[:], in1=gs[:], op=ALU.add),
                              reads=[acck, gsk], writes=[dstk])
            for ct in range(8):
                pb, pk = PS.next()
                mm_group(pb[:, 0:TW], [(Wo[:, k, ct * 128:(ct + 1) * 128], mg[:, k, :]) for k in range(8)], [wo_tk, mgk], pk)
                cx.op(DVE, lambda e, ct=ct: e.tensor_tensor(out=xt[:, ct, :], in0=xt[:, ct, :], in1=pb[:, 0:TW], op=ALU.add),
                      reads=[pk, xk], writes=[xk])
            cx.dma(PL, x1r[:, :, i * TW:(i + 1) * TW], xt[:], xk, reads=[xk], writes=[tk["x1"]])
        ph.close()

    def phase3b(l, S):
        ph = Phase(cx, f"p3b_{l}")
        tk = S["tk"]
        stg_pool = ph.pool(2, [128, 8, 512], F32)
        g_tk = Tk(); ph.tks.append(g_tk)
        gx = load_vec(ph, P["norm_x"][l], 8, g_tk)
        gm = load_vec(ph, P["norm_mem"][l], 8, g_tk)
        gq = load_vec(ph, P["x_qnorm"][l], 2, g_tk)
        gk = load_vec(ph, P["x_knorm"][l], 2, g_tk)
        cx.op(DVE, lambda e: e.tensor_scalar(out=gq[:], in0=gq[:], scalar1=1.0 / 16, scalar2=None, op0=ALU.mult), reads=[g_tk], writes=[g_tk])
        Wq, wq_tk = ph.sbt([128, 8, 1024], BF16)
        Wo, wo_tk = ph.sbt([128, 8, 1024], BF16)
        Wkv, wkv_tk = ph.sbt([128, 8, 1024], BF16)
        load_w(ph, P["xq"][l], 8, 0, 1024, gain=(gx, g_tk), stg_pool=stg_pool, w_out=Wq, w_tk=wq_tk)
        load_w(ph, P["xo"][l], 8, 0, 1024, stg_pool=stg_pool, w_out=Wo, w_tk=wo_tk)
        TW = 256
        sq_pool = ph.pool(1, [128, 8, TW], BF16)
        rs_pool = ph.pool(2, [128, TW], F32)
        evb = ph.pool(4, [128, TW], BF16)
        evf = ph.pool(4, [128, TW], F32)
        mt_, mk = ph.sbt([128, 8, NMEM], F32)
        cx.dma(SP, mt_[:], memT_in.rearrange("(kt p) n -> p kt n", p=128), mk, writes=[mk])
        rs, rk = rms_rstd(ph, mt_[:], mk, 8, NMEM, sq_pool, rs_pool)
        hm, hmk = ph.sbt([128, 8, NMEM], BF16)
        cx.op(DVE, lambda e: e.tensor_tensor(out=hm[:], in0=mt_[:], in1=bc(rs[:], [128, 8, NMEM], 1), op=ALU.mult),
              reads=[mk, rk], writes=[hmk])
        kx, kxk = ph.sbt([128, 8, NMEM], BF16)
        vx, vxk = ph.sbt([128, 2, 1024], BF16)
        load_w(ph, P["xk"][l], 8, 0, 1024, gain=(gm, g_tk), stg_pool=stg_pool, w_out=Wkv, w_tk=wkv_tk)
        for hd in range(4):
            pbs = []
            for dt_ in range(2):
                ct = hd * 2 + dt_
                pb, pk = PS.next()
                mm_group(pb[:, 0:NMEM], [(Wkv[:, k, ct * 128:(ct + 1) * 128], hm[:, k, :]) for k in range(8)], [wkv_tk, hmk], pk)
                pbs.append((pb, pk))
            sqs = []
            for (pb, pk) in pbs:
                sq, sqk = evb.next()
                cx.op(ACT, lambda e, pb=pb, sq=sq: e.activation(out=sq[:, 0:NMEM], in_=pb[:, 0:NMEM], func=AF.Square), reads=[pk], writes=[sqk])
                sqs.append((sq, sqk))
            p2, p2k = PS.next()
            mm_group(p2[:, 0:NMEM], [(ones_b, sq[:, 0:NMEM]) for (sq, _) in sqs], [sqs[0][1], sqs[1][1], c_tk], p2k)
            rh, rhk = evf.next()
            cx.op(ACT, lambda e: e.activation(out=rh[:, 0:NMEM], in_=p2[:, 0:NMEM], func=AF.Sqrt, scale=1.0 / 256, bias=eps_t[:, 0:1]),
                  reads=[p2k, c_tk], writes=[rhk])
            cx.op(DVE, lambda e: e.reciprocal(out=rh[:, 0:NMEM], in_=rh[:, 0:NMEM]), reads=[rhk], writes=[rhk])
            for dt_, (pb, pk) in enumerate(pbs):
                cx.op(DVE, lambda e, pb=pb, dt_=dt_: e.scalar_tensor_tensor(out=kx[:, hd * 2 + dt_, :], in0=pb[:, 0:NMEM], scalar=gk[:, dt_:dt_ + 1],
                                                                       in1=rh[:, 0:NMEM], op0=ALU.mult, op1=ALU.mult),
                      reads=[pk, rhk, g_tk], writes=[kxk])
        load_w(ph, P["xv"][l], 8, 0, 1024, gain=(gm, g_tk), stg_pool=stg_pool, w_out=Wkv, w_tk=wkv_tk)
        for mt2 in range(2):
            for cc_ in range(2):
                pb, pk = PS.next()
                mm_group(pb[:], [(hm[:, k, mt2 * 128:(mt2 + 1) * 128], Wkv[:, k, cc_ * 512:(cc_ + 1) * 512]) for k in range(8)], [wkv_tk, hmk], pk)
                cx.op(ACT, lambda e, pb=pb, mt2=mt2, cc_=cc_: e.activation(out=vx[:, mt2, cc_ * 512:(cc_ + 1) * 512], in_=pb[:], func=AF.Copy),
                      reads=[pk], writes=[vxk])
        x_pool = ph.pool(2, [128, 8, TW], F32)
        h_pool = ph.pool(1, [128, 8, TW], BF16)
        qh_pool = ph.pool(2, [128, 2, TW], BF16)
        pt_pool = ph.pool(2, [128, 2, TW], BF16)
        o_pool = ph.pool(1, [128, 8, TW], BF16)
        x1r = S["x1"].rearrange("(kt p) n -> p kt n", p=128)
        x2r = S["x2"].rearrange("(kt p) n -> p kt n", p=128)

        def ld(i):
            xt, xk = x_pool.next()
            cx.dma(SP, xt[:], x1r[:, :, i * TW:(i + 1) * TW], xk, reads=[tk["x1"]], writes=[xk])
            return xt, xk

        nxt = ld(0)
        for i in range(N // TW):
            xt, xk = nxt
            if i + 1 < N // TW:
                nxt = ld(i + 1)
            rs, rk = rms_rstd(ph, xt[:], xk, 8, TW, sq_pool, rs_pool)
            h, hk = h_pool.next()
            cx.op(DVE, lambda e: e.tensor_tensor(out=h[:], in0=xt[:], in1=bc(rs[:], [128, 8, TW], 1), op=ALU.mult),
                  reads=[xk, rk], writes=[hk])
            oT, oTk = o_pool.next()
            for hd in range(4):
                pbs = []
                for dt_ in range(2):
                    ct = hd * 2 + dt_
                    pb, pk = PS.next()
                    mm_group(pb[:, 0:TW], [(Wq[:, k, ct * 128:(ct + 1) * 128], h[:, k, :]) for k in range(8)], [wq_tk, hk], pk)
                    pbs.append((pb, pk))
                sqs = []
                for (pb, pk) in pbs:
                    sq, sqk = evb.next()
                    cx.op(ACT, lambda e, pb=pb, sq=sq: e.activation(out=sq[:], in_=pb[:, 0:TW], func=AF.Square), reads=[pk], writes=[sqk])
                    sqs.append((sq, sqk))
                p2, p2k = PS.next()
                mm_group(p2[:, 0:TW], [(ones_b, sq[:]) for (sq, _) in sqs], [sqs[0][1], sqs[1][1], c_tk], p2k)
                rh, rhk = evf.next()
                cx.op(ACT, lambda e: e.activation(out=rh[:], in_=p2[:, 0:TW], func=AF.Sqrt, scale=1.0 / 256, bias=eps_t[:, 0:1]),
                      reads=[p2k, c_tk], writes=[rhk])
                cx.op(DVE, lambda e: e.reciprocal(out=rh[:], in_=rh[:]), reads=[rhk], writes=[rhk])
                qh, qhk = qh_pool.next()
                for dt_, (pb, pk) in enumerate(pbs):
                    cx.op(DVE, lambda e, pb=pb, dt_=dt_: e.scalar_tensor_tensor(out=qh[:, dt_, :], in0=pb[:, 0:TW], scalar=gq[:, dt_:dt_ + 1],
                                                                           in1=rh[:], op0=ALU.mult, op1=ALU.mult),
                          reads=[pk, rhk, g_tk], writes=[qhk])
                pt, ptk = pt_pool.next()
                for mt2 in range(2):
                    pb, pk = PS.next()
                    mm_group(pb[:, 0:TW], [(kx[:, hd * 2 + dt_, mt2 * 128:(mt2 + 1) * 128], qh[:, dt_, :]) for dt_ in range(2)], [kxk, qhk], pk)
                    cx.op(ACT, lambda e, pb=pb, mt2=mt2: e.activation(out=pt[:, mt2, :], in_=pb[:, 0:TW], func=AF.Exp), reads=[pk], writes=[ptk])
                p3, p3k = PS.next()
                mm_group(p3[:, 0:TW], [(ones_b, pt[:, mt2, :]) for mt2 in range(2)], [ptk, c_tk], p3k)
                ri, rik = evf.next()
                cx.op(DVE, lambda e: e.reciprocal(out=ri[:], in_=p3[:, 0:TW]), reads=[p3k], writes=[rik])
                for dt_ in range(2):
                    pb, pk = PS.next()
                    c0 = hd * 256 + dt_ * 128
                    mm_group(pb[:, 0:TW], [(vx[:, mt2, c0:c0 + 128], pt[:, mt2, :]) for mt2 in range(2)], [vxk, ptk], pk)
                    cx.op(DVE, lambda e, pb=pb, dt_=dt_: e.tensor_tensor(out=oT[:, hd * 2 + dt_, :], in0=pb[:, 0:TW], in1=ri[:], op=ALU.mult),
                          reads=[pk, rik], writes=[oTk])
            for ct in range(8):
                pb, pk = PS.next()
                mm_group(pb[:, 0:TW], [(Wo[:, k, ct * 128:(ct + 1) * 128], oT[:, k, :]) for k in range(8)], [wo_tk, oTk], pk)
                cx.op(DVE, lambda e, ct=ct, pb=pb: e.tensor_tensor(out=xt[:, ct, :], in0=xt[:, ct, :], in1=pb[:, 0:TW], op=ALU.add),
                      reads=[pk, xk], writes=[xk])
            cx.dma(PL, x2r[:, :, i * TW:(i + 1) * TW], xt[:], xk, reads=[xk], writes=[tk["x2"]])
        ph.close()

    def phase3c(l, S):
        ph = Phase(cx, f"p3c_{l}")
        tk = S["tk"]
        stg_pool = ph.pool(1, [128, 8, 512], F32)
        g_tk = Tk(); ph.tks.append(g_tk)
        gf = load_vec(ph, P["norm_ffn"][l], 8, g_tk)
        Wgu, wgu_tk = ph.sbt([128, 8, 2 * DFF], BF16)
        Wd, wd_tk = ph.sbt([128, 22, 1024], BF16)
        load_w(ph, P["w_gate_up"][l], 8, 0, 2 * DFF, gain=(gf, g_tk), stg_pool=stg_pool, w_out=Wgu, w_tk=wgu_tk)
        load_w(ph, P["w_down"][l], 22, 0, 1024, stg_pool=stg_pool, w_out=Wd, w_tk=wd_tk)
        TW = 256
        sq_pool = ph.pool(1, [128, 8, TW], BF16)
        rs_pool = ph.pool(2, [128, TW], F32)
        evf = ph.pool(3, [128, TW], F32)
        x_pool = ph.pool(2, [128, 8, TW], F32)
        h_pool = ph.pool(1, [128, 8, TW], BF16)
        hid_pool = ph.pool(1, [128, 22, TW], BF16)
        x2r = S["x2"].rearrange("(kt p) n -> p kt n", p=128)
        x3r = S["x3"].rearrange("(kt p) n -> p kt n", p=128)

        def ld(i):
            xt, xk = x_pool.next()
            cx.dma(SP, xt[:], x2r[:, :, i * TW:(i + 1) * TW], xk, reads=[tk["x2"]], writes=[xk])
            return xt, xk

        nxt = ld(0)
        for i in range(N // TW):
            xt, xk = nxt
            if i + 1 < N // TW:
                nxt = ld(i + 1)
            rs, rk = rms_rstd(ph, xt[:], xk, 8, TW, sq_pool, rs_pool)
            h, hk = h_pool.next()
            cx.op(DVE, lambda e: e.tensor_tensor(out=h[:], in0=xt[:], in1=bc(rs[:], [128, 8, TW], 1), op=ALU.mult),
                  reads=[xk, rk], writes=[hk])
            hid, hidk = hid_pool.next()
            for j in range(22):
                pa, pak = PS.next()
                mm_group(pa[:, 0:TW], [(Wgu[:, k, j * 128:(j + 1) * 128], h[:, k, :]) for k in range(8)], [wgu_tk, hk], pak)
                sa, sak = evf.next()
                cx.op(ACT, lambda e, pa=pa, sa=sa: e.activation(out=sa[:], in_=pa[:, 0:TW], func=AF.Silu), reads=[pak], writes=[sak])
                pb, pk = PS.next()
                mm_group(pb[:, 0:TW], [(Wgu[:, k, DFF + j * 128:DFF + (j + 1) * 128], h[:, k, :]) for k in range(8)], [wgu_tk, hk], pk)
                cx.op(DVE, lambda e, pb=pb, sa=sa, j=j: e.tensor_tensor(out=hid[:, j, :], in0=pb[:, 0:TW], in1=sa[:], op=ALU.mult),
                      reads=[pk, sak], writes=[hidk])
            for ct in range(8):
                pb, pk = PS.next()
                mm_group(pb[:, 0:TW], [(Wd[:, j, ct * 128:(ct + 1) * 128], hid[:, j, :]) for j in range(22)], [wd_tk, hidk], pk)
                cx.op(DVE, lambda e, ct=ct, pb=pb: e.tensor_tensor(out=xt[:, ct, :], in0=xt[:, ct, :], in1=pb[:, 0:TW], op=ALU.add),
                      reads=[pk, xk], writes=[xk])
            cx.dma(PL, x3r[:, :, i * TW:(i + 1) * TW], xt[:], xk, reads=[xk], writes=[S["tk"]["x3"]])
        ph.close()

    ccsem = cx.get_dsem()

    def gather_y(l, S):
        tk = S["tk"]
        cx._need(PL, cx._deps([tk["ysrc"]], [tk["ydst"]]))
        for c in range(N // 256):
            ccsem.count += 1
            nc.gpsimd.collective_compute("AllGather", ALU.bypass, replica_groups=[[0, 1, 2, 3], [4, 5, 6, 7]],
                                         ins=[S["ysrc"][c]], outs=[S["ydst"][c]]).then_inc(ccsem.h, 1)
            cx.ninstr += 1
        tk["ydst"].w[ccsem] = ccsem.count
        tk["ysrc"].r[ccsem] = ccsem.count

    xT, x_tk = xT_in, NOTK
    for l in range(depth):
        S = mk_scratch(l)
        if "p1" in stages:
            phase1(l, xT, x_tk, S)
        if "fox" in stages:
            phase_fox(l, S)
        if "s5" in stages:
            phase_s5(l, S)
        if "hg" in stages:
            phase_hg(l, S)
        if "cc" in stages:
            gather_y(l, S)
        if "p3a" in stages:
            phase3a(l, xT, x_tk, S)
        if "p3b" in stages:
            phase3b(l, S)
        if "p3c" in stages:
            phase3c(l, S)
        xT, x_tk = S["x3"], S["tk"]["x3"]
    G.close()
    return nc, cx, dbg_out


def _consts():
    c = np.zeros((128, 6, 128), np.float32)
    p = np.arange(128)
    c[:, 0, :] = np.eye(128)
    c[:, 1, :] = (p[None, :] >= p[:, None])
    c[:, 2, :] = (p[:, None] // 64 == p[None, :] // 64)
    c[:, 3, :] = np.eye(128)[(p + 64) % 128]
    c[:, 4, :] = 1.0
    c[:, 5, :] = (p[:, None] // 64 == p[None, :] // 64) & (p[:, None] % 64 < p[None, :] % 64)
    return c


_SHARED = ("norm_mix", "fox_qnorm", "fox_knorm", "s5_w_glu", "s5_b_glu", "hg_onorm", "w_branch", "w_out",
           "norm_x", "norm_mem", "xq", "xk", "xv", "xo", "x_qnorm", "x_knorm", "norm_ffn", "w_gate_up", "w_down")
_STAGES = ("p1", "fox", "s5", "hg", "cc", "p3a", "p3b", "p3c")


def _local_params(inputs, j):
    f = lambda k: np.asarray(inputs[k], np.float32)
    w = f("w_in")
    sl = lambda a, b: np.arange(a, b)
    cols = np.concatenate([
        sl(128 * j, 128 * j + 128), sl(512 + 128 * j, 512 + 128 * j + 128), sl(1024 + 128 * j, 1024 + 128 * j + 128),
        sl(1536 + 2 * j, 1536 + 2 * j + 2), sl(1544 + 128 * j, 1544 + 128 * j + 128), sl(2056 + 128 * j, 2056 + 128 * j + 128),
        sl(2568 + 128 * j, 2568 + 128 * j + 128), sl(3080 + 128 * j, 3080 + 128 * j + 128), sl(3592 + 128 * j, 3592 + 128 * j + 128),
        sl(4104, 7176)])
    m = {"w_in": np.ascontiguousarray(w[:, :, cols])}
    m["fox_fbias"] = np.ascontiguousarray(f("fox_fbias")[:, 2 * j:2 * j + 2])
    for k in ("s5_a_re", "s5_a_im", "s5_b_re", "s5_b_im", "s5_c_re", "s5_c_im", "s5_log_dt"):
        m[k] = np.ascontiguousarray(f(k)[:, 8 * j:8 * j + 8])
    m["s5_d"] = np.ascontiguousarray(f("s5_d")[:, 128 * j:128 * j + 128])
    m["hg_lb"] = np.ascontiguousarray(f("hg_lb")[:, 128 * j:128 * j + 128])
    return m


def _run(inputs, N, depth):
    x = np.asarray(inputs["x"], np.float32)
    mem = np.asarray(inputs["mem"], np.float32)
    B = x.shape[0]
    nc, cx, _ = build(N, depth, debug=False, stages=_STAGES)
    shared = {k: np.ascontiguousarray(np.asarray(inputs[k], np.float32)) for k in _SHARED}
    shared["consts"] = _consts()
    shared["nvec"] = np.zeros((128, 80), np.float32)
    shared["jv"] = np.tile(np.arange(TT, dtype=np.float32)[None], (128, 1))
    shared["m16"] = (np.arange(128)[:, None] // 16 == np.arange(8)[None, :]).astype(np.float32)
    loc = [_local_params(inputs, j) for j in range(4)]
    maps = []
    for c in range(8):
        b, j = c // 4, c % 4
        m = dict(shared)
        m.update(loc[j])
        m["xT"] = np.ascontiguousarray(x[b, :N].T)
        m["memT"] = np.ascontiguousarray(mem[b].T)
        maps.append(m)
    res = run_bass_kernel_spmd(nc, maps, core_ids=list(range(8)))
    out = np.stack([np.asarray(res.results[4 * b]["outT"]).T for b in range(B)], axis=0)
    return np.ascontiguousarray(out.astype(np.float32))


def kernel(**inputs):
    return _run(inputs, 16384, DEPTH)
```

```python
import contextlib
import numpy as np
import ml_dtypes
import concourse.bass as bass
import concourse.mybir as mybir
from concourse.bass_utils import run_bass_kernel_spmd

F32 = mybir.dt.float32
BF16 = mybir.dt.bfloat16
AF = mybir.ActivationFunctionType
ALU = mybir.AluOpType
AX = mybir.AxisListType

D = 1024
DEPTH = 4
NMEM = 256
DFF = 2816
INC = 7176
EPS = 1e-6
TT = 512
SL = 64
HC = 64


class Sem:
    def __init__(self, h):
        self.h = h
        self.count = 0


class Tk:
    __slots__ = ("w", "r", "dsem")

    def __init__(self):
        self.w = {}
        self.r = {}
        self.dsem = None


class Eng:
    def __init__(self, name, e, sem, is_pe=False):
        self.name, self.e, self.sem, self.is_pe = name, e, sem, is_pe
        self.waited = {}


class Ctx:
    def __init__(self, nc):
        self.nc = nc
        self.st = contextlib.ExitStack()
        self.sem_free = []
        self.nsem = 0
        self.dma_sems = []
        self.pe = Eng("pe", nc.tensor, self._newsem("pe"), True)
        self.act = Eng("act", nc.scalar, self._newsem("act"))
        self.dve = Eng("dve", nc.vector, self._newsem("dve"))
        self.pool = Eng("pool", nc.gpsimd, self._newsem("pool"))
        self.sp = Eng("sp", nc.sync, self._newsem("sp"))
        self.engs = [self.pe, self.act, self.dve, self.pool, self.sp]
        self.ninstr = 0

    def _newsem(self, name):
        self.nsem += 1
        return Sem(self.st.enter_context(self.nc.semaphore(f"s{self.nsem}_{name}")))

    def get_dsem(self):
        if self.sem_free:
            return self.sem_free.pop()
        s = self._newsem("d")
        self.dma_sems.append(s)
        return s

    def _need(self, E, tokens):
        for S, v in tokens.items():
            if v <= 0 or (E.is_pe and S is E.sem) or E.waited.get(S, 0) >= v:
                continue
            E.e.wait_ge(S.h, v)
            E.waited[S] = v
            self.ninstr += 1

    def _deps(self, reads, writes):
        tok = {}
        for t in reads:
            for S, v in t.w.items():
                if tok.get(S, 0) < v:
                    tok[S] = v
        for t in writes:
            for S, v in t.w.items():
                if tok.get(S, 0) < v:
                    tok[S] = v
            for S, v in t.r.items():
                if tok.get(S, 0) < v:
                    tok[S] = v
        return tok

    def _commit(self, S, v, reads, writes):
        for t in writes:
            t.w = {S: v}
            t.r = {}
        for t in reads:
            if t.r.get(S, 0) < v:
                t.r[S] = v

    def op(self, E, fn, reads=(), writes=(), inc=True):
        self._need(E, self._deps(reads, writes))
        ins = fn(E.e)
        self.ninstr += 1
        if inc:
            E.sem.count += 1
            ins.then_inc(E.sem.h, 1)
            self._commit(E.sem, E.sem.count, reads, writes)
        else:
            self._commit(E.sem, E.sem.count + 1, reads, writes)
        return ins

    def dma(self, Q, out, in_, sb_tk, reads=(), writes=(), **kw):
        self._need(Q, self._deps(reads, writes))
        if sb_tk.dsem is None:
            sb_tk.dsem = self.get_dsem()
        S = sb_tk.dsem
        S.count += 16
        Q.e.dma_start(out=out, in_=in_, **kw).then_inc(S.h, 16)
        self.ninstr += 1
        for t in writes:
            if t is sb_tk:
                t.w = {S: S.count}
                t.r = {}
            else:
                t.w[S] = S.count
        for t in reads:
            if t.r.get(S, 0) < S.count:
                t.r[S] = S.count

    def release(self, tks):
        for t in tks:
            if t.dsem is not None:
                self.sem_free.append(t.dsem)
                t.dsem = None

    def barrier(self):
        tok = {E.sem: E.sem.count for E in self.engs}
        for S in self.dma_sems:
            tok[S] = S.count
        for E in self.engs:
            self._need(E, tok)


class Pool_:
    def __init__(self, tiles):
        self.tiles = [(t, Tk()) for t in tiles]
        self.i = 0

    def next(self):
        t = self.tiles[self.i % len(self.tiles)]
        self.i += 1
        return t

    def tks(self):
        return [t[1] for t in self.tiles]


class Phase:
    def __init__(self, cx, name):
        self.cx, self.name = cx, name
        self.st = contextlib.ExitStack()
        self.tks = []
        self.n = 0

    def sb(self, shape, dt):
        self.n += 1
        return self.st.enter_context(self.cx.nc.sbuf_tensor(f"{self.name}_{self.n}", list(shape), dt))

    def sbt(self, shape, dt):
        tk = Tk()
        self.tks.append(tk)
        return self.sb(shape, dt), tk

    def pool(self, n, shape, dt):
        p = Pool_([self.sb(shape, dt) for _ in range(n)])
        self.tks += p.tks()
        return p

    def close(self):
        self.cx.barrier()
        self.cx.release(self.tks)
        self.st.close()


def bc(ap, shape, axis):
    return ap.unsqueeze(axis).to_broadcast(list(shape))


def build(N, depth, debug=False, stages=("p1", "fox")):
    nc = bass.Bass("TRN2", target_bir_lowering=False)
    cx = Ctx(nc)
    NT = N // TT
    NK = N // 128
    NCH = N // HC

    def din(name, shape, dt=F32):
        return nc.dram_tensor(name, list(shape), dt, kind="ExternalInput").ap()

    dbg_out = {}

    def dscr(name, shape, dt):
        if debug:
            t = nc.dram_tensor(name, list(shape), dt, kind="ExternalOutput").ap()
            dbg_out[name] = t
            return t
        return nc.dram_tensor(name, list(shape), dt).ap()

    xT_in = din("xT", [D, N])
    memT_in = din("memT", [D, NMEM])
    consts = din("consts", [128, 6, 128])
    nvec_in = din("nvec", [128, 80])
    jv_in = din("jv", [128, TT])
    m16_in = din("m16", [128, 8])
    P = {}
    pshapes = dict(
        norm_mix=[DEPTH, D], w_in=[DEPTH, D, 4098], fox_fbias=[DEPTH, 2], fox_qnorm=[DEPTH, 64],
        fox_knorm=[DEPTH, 64], s5_a_re=[DEPTH, 8, 64], s5_a_im=[DEPTH, 8, 64],
        s5_b_re=[DEPTH, 8, 64, 16], s5_b_im=[DEPTH, 8, 64, 16], s5_c_re=[DEPTH, 8, 16, 64],
        s5_c_im=[DEPTH, 8, 16, 64], s5_d=[DEPTH, 128], s5_log_dt=[DEPTH, 8],
        s5_w_glu=[DEPTH, 512, 512], s5_b_glu=[DEPTH, 512], hg_lb=[DEPTH, 128], hg_onorm=[DEPTH, 128],
        w_branch=[DEPTH, 1536, D], w_out=[DEPTH, D, D], norm_x=[DEPTH, D], norm_mem=[DEPTH, D],
        xq=[DEPTH, D, D], xk=[DEPTH, D, D], xv=[DEPTH, D, D], xo=[DEPTH, D, D],
        x_qnorm=[DEPTH, 256], x_knorm=[DEPTH, 256], norm_ffn=[DEPTH, D],
        w_gate_up=[DEPTH, D, 2 * DFF], w_down=[DEPTH, DFF, D])
    for k, s in pshapes.items():
        P[k] = din(k, s)
    out_T = nc.dram_tensor("outT", [D, N], F32, kind="ExternalOutput").ap()

    SP, PL, PE, ACT, DVE = cx.sp, cx.pool, cx.pe, cx.act, cx.dve
    NOTK = Tk()

    G = Phase(cx, "g")
    cst_f = G.sb([128, 6, 128], F32)
    cst_b = G.sb([128, 6, 128], BF16)
    nvec = G.sb([128, 80], F32)
    lb_all = G.sb([128, 1, DEPTH], F32)
    c_tk = Tk()
    cx.dma(SP, cst_f[:], consts, c_tk, writes=[c_tk])
    cx.dma(SP, nvec[:], nvec_in, c_tk, writes=[c_tk])
    cx.op(DVE, lambda e: e.tensor_copy(out=cst_b[:], in_=cst_f[:]), reads=[c_tk], writes=[c_tk])
    ident_b, tri_b, blk64_b, ones_b = cst_b[:, 0, :], cst_b[:, 1, :], cst_b[:, 2, :], cst_b[:, 4, :]
    ident_f, swap_f, ones_f = cst_f[:, 0, :], cst_f[:, 3, :], cst_f[:, 4, :]
    psb = [cx.st.enter_context(nc.psum_tensor(f"ps{i}", [128, 512], F32)) for i in range(8)]
    PS = Pool_(psb[2:8])
    PSA = Pool_(psb[0:2])

    with contextlib.nullcontext():
        lbt = G.sb([128, 1, DEPTH], F32)
        lbs = G.sb([128, 1], F32)
        for l_ in range(DEPTH):
            cx.dma(SP, lbt[:, :, l_], P["hg_lb"][l_].rearrange("(h p) -> p h", p=128), c_tk, writes=[c_tk],
                   allow_slow_non_contiguous=True)
        cx.op(ACT, lambda e: e.activation(out=lbt[:], in_=lbt[:], func=AF.Exp), reads=[c_tk], writes=[c_tk])
        cx.op(DVE, lambda e: e.tensor_reduce(out=lbs[:], in_=lbt[:], axis=AX.X, op=ALU.add), reads=[c_tk], writes=[c_tk])
        cx.op(DVE, lambda e: e.reciprocal(out=lbs[:], in_=lbs[:]), reads=[c_tk], writes=[c_tk])
        cx.op(DVE, lambda e: e.tensor_tensor(out=lbt[:], in0=lbt[:], in1=bc(lbs[:], [128, 1, DEPTH], 2), op=ALU.mult),
              reads=[c_tk], writes=[c_tk])
        cx.op(DVE, lambda e: e.memset(lb_all[:, :, 0:1], 0.0), writes=[c_tk])
        for l in range(1, DEPTH):
            cx.op(DVE, lambda e, l=l: e.tensor_tensor(out=lb_all[:, :, l:l + 1], in0=lb_all[:, :, l - 1:l],
                                                     in1=lbt[:, :, l:l + 1], op=ALU.add), reads=[c_tk], writes=[c_tk])

    def mm_group(out_ap, pairs, reads, out_tk):
        n = len(pairs)
        for i, (lt, rh) in enumerate(pairs):
            cx.op(PE, lambda e, lt=lt, rh=rh, i=i: e.matmul(out_ap, lhsT=lt, rhs=rh, start=(i == 0), stop=(i == n - 1)),
                  reads=reads, writes=[out_tk], inc=(i == n - 1))

    def load_w(ph, src_rows_ap, nk, c0, c1, gain=None, stg_pool=None, w_out=None, w_tk=None, wc0=0):
        CH = 512
        for cc in range(c0, c1, CH):
            ce = min(c1, cc + CH)
            for k0 in range(0, nk, 8):
                k1 = min(nk, k0 + 8)
                stg, stk = stg_pool.next()
                cx.dma(SP, stg[:, 0:k1 - k0, 0:ce - cc],
                       src_rows_ap.rearrange("(kt p) c -> p kt c", p=128)[:, k0:k1, cc:ce], stk, writes=[stk])
                for kt in range(k0, k1):
                    dst = w_out[:, kt, wc0 + cc - c0: wc0 + ce - c0]
                    if gain is not None:
                        cx.op(ACT, lambda e, dst=dst, kt=kt, stg=stg, k0=k0, ce=ce, cc=cc: e.activation(
                            out=dst, in_=stg[:, kt - k0, 0:ce - cc], func=AF.Copy, scale=gain[0][:, kt:kt + 1]),
                            reads=[stk, gain[1]], writes=[w_tk])
                    else:
                        cx.op(PL, lambda e, dst=dst, kt=kt, stg=stg, k0=k0, ce=ce, cc=cc: e.tensor_copy(
                            out=dst, in_=stg[:, kt - k0, 0:ce - cc]), reads=[stk], writes=[w_tk])

    def load_vec(ph, src1d, nk, tk):
        t = ph.sb([128, nk], F32)
        cx.dma(SP, t[:], src1d.rearrange("(kt p) -> p kt", p=128), tk, writes=[tk], allow_slow_non_contiguous=True)
        return t

    def rms_rstd(ph, xt, x_tk, nkt, width, sq_pool, rs_pool):
        sq, sqk = sq_pool.next()
        cx.op(ACT, lambda e: e.activation(out=sq[:, 0:nkt, 0:width], in_=xt, func=AF.Square), reads=[x_tk], writes=[sqk])
        pb, pk = PS.next()
        mm_group(pb[:, 0:width], [(ones_b, sq[:, k, 0:width]) for k in range(nkt)], [sqk, c_tk], pk)
        rs, rk = rs_pool.next()
        cx.op(ACT, lambda e: e.activation(out=rs[:, 0:width], in_=pb[:, 0:width], func=AF.Sqrt, scale=1.0 / (nkt * 128), bias=eps_t[:, 0:1]),
              reads=[pk, c_tk], writes=[rk])
        cx.op(DVE, lambda e: e.reciprocal(out=rs[:, 0:width], in_=rs[:, 0:width]), reads=[rk], writes=[rk])
        return rs, rk

    eps_t = G.sb([128, 1], F32)
    cx.op(DVE, lambda e: e.memset(eps_t[:], EPS), writes=[c_tk])

    def mk_scratch(l):
        S = {}
        S["qT"] = dscr(f"qT{l}", [128, N], BF16)
        S["kT"] = dscr(f"kT{l}", [128, N], BF16)
        S["vP"] = dscr(f"vP{l}", [2, 128, NK, 64], BF16)
        S["lfT"] = dscr(f"lfT{l}", [2, N], F32)
        S["aug"] = dscr(f"aug{l}", [2, 6, 2, N], BF16)
        S["uT"] = dscr(f"uT{l}", [128, N], BF16)
        S["hqT"] = dscr(f"hqT{l}", [128, N], BF16)
        S["lfhT"] = dscr(f"lfhT{l}", [128, N], F32)
        S["khT"] = dscr(f"khT{l}", [128, N], BF16)
        S["hvP"] = dscr(f"hvP{l}", [1, 64, NCH, 128], BF16)
        S["hgT"] = dscr(f"hgT{l}", [128, N], BF16)
        S["ysrc"] = nc.dram_tensor(f"ysrc{l}", [N // 256, 384, 256], BF16).ap()
        S["ydst"] = nc.dram_tensor(f"ydst{l}", [N // 256, 1536, 256], BF16).ap()
        S["x1"] = dscr(f"x1_{l}", [D, N], F32)
        S["x2"] = dscr(f"x2_{l}", [D, N], F32)
        S["x3"] = out_T if l == depth - 1 else dscr(f"x3_{l}", [D, N], F32)
        S["tk"] = {k: Tk() for k in list(S.keys())}
        return S

    def phase1(l, xT, x_tk, S):
        ph = Phase(cx, f"p1_{l}")
        tk = S["tk"]
        W, w_tk = ph.sbt([128, 8, 1026], BF16)
        stg_pool = ph.pool(2, [128, 8, 512], F32)
        g_tk = Tk()
        ph.tks.append(g_tk)
        g1 = load_vec(ph, P["norm_mix"][l], 8, g_tk)
        load_w(ph, P["w_in"][l], 8, 0, 1026, gain=(g1, g_tk), stg_pool=stg_pool, w_out=W, w_tk=w_tk)
        gqk = ph.sb([128, 2], F32)
        for hf in range(2):
            cx.dma(SP, gqk[hf * 64:(hf + 1) * 64, 0:1], P["fox_qnorm"][l].rearrange("(p o) -> p o", o=1), g_tk, writes=[g_tk])
            cx.dma(SP, gqk[hf * 64:(hf + 1) * 64, 1:2], P["fox_knorm"][l].rearrange("(p o) -> p o", o=1), g_tk, writes=[g_tk])
        cx.op(DVE, lambda e: e.tensor_scalar(out=gqk[:, 0:1], in0=gqk[:, 0:1], scalar1=0.125, scalar2=None, op0=ALU.mult),
              reads=[g_tk], writes=[g_tk])
        nfb = ph.sb([2, 1], F32)
        cx.dma(SP, nfb[:], P["fox_fbias"][l].rearrange("(p o) -> p o", o=1), g_tk, writes=[g_tk])
        cx.op(DVE, lambda e: e.tensor_scalar(out=nfb[:], in0=nfb[:], scalar1=-1.0, scalar2=None, op0=ALU.mult), reads=[g_tk], writes=[g_tk])
        oml = ph.sb([128, 1], F32)
        noml = ph.sb([128, 1], F32)
        cx.op(DVE, lambda e: e.tensor_scalar(out=oml[:], in0=lb_all[:, :, l], scalar1=-1.0, scalar2=1.0, op0=ALU.mult, op1=ALU.add),
              reads=[c_tk], writes=[g_tk])
        cx.op(DVE, lambda e: e.tensor_scalar(out=noml[:], in0=oml[:], scalar1=-1.0, scalar2=None, op0=ALU.mult), reads=[g_tk], writes=[g_tk])

        x_pool = ph.pool(2, [128, 8, TT], F32)
        sq_pool = ph.pool(1, [128, 8, TT], BF16)
        h_pool = ph.pool(2, [128, 8, TT], BF16)
        rs_pool = ph.pool(2, [128, TT], F32)
        evb = ph.pool(4, [128, TT], BF16)
        evf = ph.pool(3, [128, TT], F32)
        xr = xT.rearrange("(kt p) n -> p kt n", p=128)

        def ld(i):
            xt, xk = x_pool.next()
            cx.dma(SP, xt[:], xr[:, :, i * TT:(i + 1) * TT], xk, reads=[x_tk], writes=[xk])
            return xt, xk

        nxt = ld(0)
        for i in range(NT):
            xt, xk = nxt
            if i + 1 < NT:
                nxt = ld(i + 1)
            sl = slice(i * TT, (i + 1) * TT)
            rs, rk = rms_rstd(ph, xt[:], xk, 8, TT, sq_pool, rs_pool)
            h, hk = h_pool.next()
            cx.op(DVE, lambda e: e.tensor_tensor(out=h[:], in0=xt[:], in1=bc(rs[:], [128, 8, TT], 1), op=ALU.mult),
                  reads=[xk, rk], writes=[hk])

            def proj(c0, m):
                pb, pk = PS.next()
                mm_group(pb[0:m, :], [(W[:, k, c0:c0 + m], h[:, k, :]) for k in range(8)], [w_tk, hk], pk)
                return pb, pk

            for ct in range(2):
                pb, pk = proj(ct * 128, 128)
                sqh, sk = evb.next()
                cx.op(ACT, lambda e: e.activation(out=sqh[:], in_=pb[:], func=AF.Square), reads=[pk], writes=[sk])
                pb2, pk2 = PS.next()
                mm_group(pb2[:], [(blk64_b, sqh[:])], [sk, c_tk], pk2)
                rh, rhk = evf.next()
                cx.op(ACT, lambda e: e.activation(out=rh[:], in_=pb2[:], func=AF.Sqrt, scale=1.0 / 64, bias=eps_t[:, 0:1]),
                      reads=[pk2, c_tk], writes=[rhk])
                cx.op(DVE, lambda e: e.reciprocal(out=rh[:], in_=rh[:]), reads=[rhk], writes=[rhk])
                qn, qk_ = evb.next()
                gcol = gqk[:, 0:1] if ct < 1 else gqk[:, 1:2]
                cx.op(DVE, lambda e: e.scalar_tensor_tensor(out=qn[:], in0=pb[:], scalar=gcol, in1=rh[:], op0=ALU.mult, op1=ALU.mult),
                      reads=[pk, rhk, g_tk], writes=[qk_])
                dst, dk = (S["qT"], tk["qT"]) if ct < 1 else (S["kT"], tk["kT"])
                r0 = 0
                cx.dma(PL, dst[r0:r0 + 128, sl], qn[:], qk_, reads=[qk_], writes=[dk])
            pb, pk = proj(384, 2)
            ef, ek = evf.next()
            cx.op(ACT, lambda e: e.activation(out=ef[0:2, :], in_=pb[0:2, :], func=AF.Exp, scale=-1.0, bias=nfb[:, 0:1]),
                  reads=[pk, g_tk], writes=[ek])
            cx.op(ACT, lambda e: e.activation(out=ef[0:2, :], in_=ef[0:2, :], func=AF.Ln, bias=1.0), reads=[ek], writes=[ek])
            cx.op(DVE, lambda e: e.tensor_scalar(out=ef[0:2, :], in0=ef[0:2, :], scalar1=-1.0, scalar2=None, op0=ALU.mult),
                  reads=[ek], writes=[ek])
            cx.dma(PL, S["lfT"][:, sl], ef[0:2, :], ek, reads=[ek], writes=[tk["lfT"]])
            for (c0, name, fn) in ((386, "uT", AF.Copy), (514, "hqT", AF.Silu), (898, "hgT", AF.Silu)):
                for ct in range(1):
                    pb, pk = proj(c0 + ct * 128, 128)
                    ob, ok = evb.next()
                    cx.op(ACT, lambda e, fn=fn: e.activation(out=ob[:], in_=pb[:], func=fn), reads=[pk], writes=[ok])
                    cx.dma(PL, S[name][ct * 128:(ct + 1) * 128, sl], ob[:], ok, reads=[ok], writes=[tk[name]])
            for ct in range(1):
                pb, pk = proj(642 + ct * 128, 128)
                sg, sgk = evf.next()
                cx.op(ACT, lambda e: e.activation(out=sg[:], in_=pb[:], func=AF.Sigmoid), reads=[pk], writes=[sgk])
                lf, lfk = evf.next()
                cx.op(ACT, lambda e, ct=ct: e.activation(out=lf[:], in_=sg[:], func=AF.Ln, scale=oml[:, ct:ct + 1], bias=lb_all[:, ct, l:l + 1]),
                      reads=[sgk, g_tk, c_tk], writes=[lfk])
                cx.dma(PL, S["lfhT"][ct * 128:(ct + 1) * 128, sl], lf[:], lfk, reads=[lfk], writes=[tk["lfhT"]])
                kh, khk = evb.next()
                cx.op(DVE, lambda e, ct=ct: e.tensor_scalar(out=kh[:], in0=sg[:], scalar1=noml[:, ct:ct + 1], scalar2=oml[:, ct:ct + 1],
                                                           op0=ALU.mult, op1=ALU.add), reads=[sgk, g_tk], writes=[khk])
                cx.dma(PL, S["khT"][ct * 128:(ct + 1) * 128, sl], kh[:], khk, reads=[khk], writes=[tk["khT"]])
            for ts in range(4):
                ktg = i * 4 + ts
                for which in range(2):
                    c0 = 256 if which == 0 else 770
                    pb, pk = PS.next()
                    mm_group(pb[:, 0:128], [(h[:, k, ts * 128:(ts + 1) * 128], W[:, k, c0:c0 + 128]) for k in range(8)], [w_tk, hk], pk)
                    ob, ok = evb.next()
                    cx.op(ACT, lambda e: e.activation(out=ob[:, 0:128], in_=pb[:, 0:128], func=AF.Copy), reads=[pk], writes=[ok])
                    if which == 0:
                        cx.dma(PL, S["vP"][:, :, ktg, :].rearrange("h p d -> p h d"), ob[:, 0:128].rearrange("p (h d) -> p h d", h=2),
                               ok, reads=[ok], writes=[tk["vP"]])
                    else:
                        for hf in range(2):
                            cx.dma(PL, S["hvP"][:, :, ktg * 2 + hf, :].rearrange("h p d -> p h d"),
                                   ob[hf * 64:(hf + 1) * 64, 0:128].rearrange("p (h d) -> p h d", h=1), ok, reads=[ok], writes=[tk["hvP"]])
        ph.close()

    def phase_fox(l, S):
        ph = Phase(cx, f"fx_{l}")
        tk = S["tk"]
        SEG = N // 64
        lf, lk = ph.sbt([128, SEG], F32)
        cc, ck = ph.sbt([128, SEG], F32)
        r1, r1k = ph.sbt([128, SEG], F32)
        spl = ph.sb([128, 2, 3, SEG], BF16)
        onb = ph.sb([128, SEG], BF16)
        sk_ = Tk(); ph.tks.append(sk_)
        cx.dma(SP, lf[:], S["lfT"].rearrange("h (s j) -> (h s) j", s=64), lk, reads=[tk["lfT"]], writes=[lk])
        cx.op(DVE, lambda e: e.memset(r1[:], 1.0), writes=[r1k])
        cx.op(DVE, lambda e: e.memset(onb[:], 1.0), writes=[sk_])
        cx.op(DVE, lambda e: e.tensor_tensor_scan(out=cc[:], data0=r1[:], data1=lf[:], initial=0.0, op0=ALU.mult, op1=ALU.add),
              reads=[lk, r1k], writes=[ck])
        pb, pk = PS.next()
        mm_group(pb[:, 0:2], [(cst_f[:, 5, :], cc[:, SEG - 2:SEG])], [ck, c_tk], pk)
        off = ph.sb([128, 1], F32)
        cx.op(DVE, lambda e: e.tensor_copy(out=off[:], in_=pb[:, 1:2]), reads=[pk], writes=[lk])
        cx.op(DVE, lambda e: e.tensor_scalar(out=cc[:], in0=cc[:], scalar1=off[:, 0:1], scalar2=None, op0=ALU.add),
              reads=[ck, lk], writes=[ck])
        cx.op(DVE, lambda e: e.tensor_copy(out=spl[:, 0, 0, :], in_=cc[:]), reads=[ck], writes=[sk_])
        cx.op(DVE, lambda e: e.tensor_tensor(out=r1[:], in0=cc[:], in1=spl[:, 0, 0, :], op=ALU.subtract), reads=[ck, sk_], writes=[r1k])
        cx.op(DVE, lambda e: e.tensor_copy(out=spl[:, 0, 1, :], in_=r1[:]), reads=[r1k], writes=[sk_])
        cx.op(DVE, lambda e: e.tensor_tensor(out=r1[:], in0=r1[:], in1=spl[:, 0, 1, :], op=ALU.subtract), reads=[r1k, sk_], writes=[r1k])
        cx.op(DVE, lambda e: e.tensor_copy(out=spl[:, 0, 2, :], in_=r1[:]), reads=[r1k], writes=[sk_])
        cx.op(DVE, lambda e: e.tensor_scalar(out=spl[:, 1, :, :], in0=spl[:, 0, :, :], scalar1=-1.0, scalar2=None, op0=ALU.mult),
              reads=[sk_], writes=[sk_])
        for j in range(3):
            for (side, row, src) in ((0, j, onb[:]), (0, 3 + j, spl[:, 1, j, :]), (1, j, spl[:, 0, j, :]), (1, 3 + j, onb[:])):
                cx.dma(PL, S["aug"][side, row].rearrange("h (s j) -> (h s) j", s=64), src, sk_, reads=[sk_], writes=[tk["aug"]])

        QT = ph.pool(1, [70, N], BF16)
        KT = ph.pool(1, [70, N], BF16)
        VP = ph.pool(1, [128, NK, 65], BF16)
        for (vt, vk) in VP.tiles:
            cx.op(PL, lambda e, vt=vt: e.memset(vt[:, :, 64:65], 1.0), writes=[vk])
        pt_pool = ph.pool(3, [128, TT], BF16)
        rr_pool = ph.pool(2, [65, TT], F32)
        bc_pool = ph.pool(2, [64, TT], F32)
        y_pool = ph.pool(2, [64, TT], BF16)

        def ldh(hd):
            q, qk = QT.next(); k, kk = KT.next(); v, vk = VP.next()
            cx.dma(SP, q[0:64, :], S["qT"][hd * 64:(hd + 1) * 64, :], qk, reads=[tk["qT"]], writes=[qk])
            cx.dma(SP, q[64:70, :], S["aug"][1, :, hd, :], qk, reads=[tk["aug"]], writes=[qk])
            cx.dma(SP, k[0:64, :], S["kT"][hd * 64:(hd + 1) * 64, :], kk, reads=[tk["kT"]], writes=[kk])
            cx.dma(SP, k[64:70, :], S["aug"][0, :, hd, :], kk, reads=[tk["aug"]], writes=[kk])
            cx.dma(SP, v[:, :, 0:64], S["vP"][hd], vk, reads=[tk["vP"]], writes=[vk])
            return (q, qk, k, kk, v, vk)

        for hd in range(2):
            q, qk, k, kk, v, vk = ldh(hd)
            for qt in range(NT):
                ob, ok = PSA.next()
                nkb = 4 * qt + 4
                for kb in range(nkb):
                    r = kb - 4 * qt
                    f0 = 128 * r if r > 0 else 0
                    w = TT - f0
                    sb_, sk2 = PS.next()
                    mm_group(sb_[:, 0:w], [(k[0:70, kb * 128:(kb + 1) * 128], q[0:70, qt * TT + f0:(qt + 1) * TT])], [qk, kk], sk2)
                    pt, ptk = pt_pool.next()
                    cx.op(ACT, lambda e, pt=pt, sb_=sb_, w=w: e.activation(out=pt[:, 0:w], in_=sb_[:, 0:w], func=AF.Exp), reads=[sk2], writes=[ptk])
                    if r >= 0:
                        cx.op(DVE, lambda e, pt=pt: e.tensor_tensor(out=pt[:, 0:128], in0=pt[:, 0:128], in1=tri_b, op=ALU.mult),
                              reads=[ptk, c_tk], writes=[ptk])
                    cx.op(PE, lambda e, pt=pt, w=w, f0=f0, kb=kb, ob=ob, v=v, nkb=nkb: e.matmul(
                        ob[0:65, f0:TT], lhsT=v[:, kb, 0:65], rhs=pt[:, 0:w], start=(kb == 0), stop=(kb == nkb - 1)),
                        reads=[ptk, vk], writes=[ok], inc=True)
                rr, rrk = rr_pool.next()
                cx.op(DVE, lambda e: e.reciprocal(out=rr[64:65, :], in_=ob[64:65, :]), reads=[ok], writes=[rrk])
                bb, bk = PS.next()
                mm_group(bb[0:64, :], [(ones_f[64:65, 0:64], rr[64:65, :])], [rrk, c_tk], bk)
                bs, bsk = bc_pool.next()
                cx.op(ACT, lambda e: e.activation(out=bs[:], in_=bb[0:64, :], func=AF.Copy), reads=[bk], writes=[bsk])
                y, yk = y_pool.next()
                cx.op(DVE, lambda e: e.tensor_tensor(out=y[:], in0=ob[0:64, :], in1=bs[:], op=ALU.mult), reads=[ok, bsk], writes=[yk])
                cx.dma(PL, S["ysrc"][2 * qt:2 * qt + 2, hd * 64:(hd + 1) * 64, :].rearrange("c p n -> p c n"), y[:].rearrange("p (c n) -> p c n", c=2), yk, reads=[yk], writes=[tk["ysrc"]])
        ph.close()

    PI = float(np.pi)

    def phase_s5(l, S):
        ph = Phase(cx, f"s5_{l}")
        tk = S["tk"]
        t_tk = Tk(); ph.tks.append(t_tk)
        tmpi = ph.sb([128, TT], mybir.dt.int32)

        def red_pi(x, w):
            kf = tmpf[:, 0:w]
            cx.op(DVE, lambda e: e.tensor_scalar(out=kf, in0=x, scalar1=1.0 / (2 * PI), scalar2=0.5, op0=ALU.mult, op1=ALU.add),
                  reads=[t_tk], writes=[t_tk])
            cx.op(DVE, lambda e: e.tensor_copy(out=tmpi[:, 0:w], in_=kf), reads=[t_tk], writes=[t_tk])
            cx.op(DVE, lambda e: e.tensor_copy(out=kf, in_=tmpi[:, 0:w]), reads=[t_tk], writes=[t_tk])
            cx.op(DVE, lambda e: e.scalar_tensor_tensor(out=x, in0=kf, scalar=-6.28125, in1=x, op0=ALU.mult, op1=ALU.add),
                  reads=[t_tk], writes=[t_tk])
            cx.op(DVE, lambda e: e.scalar_tensor_tensor(out=x, in0=kf, scalar=-0.0019353071795864769, in1=x, op0=ALU.mult, op1=ALU.add),
                  reads=[t_tk], writes=[t_tk])
            fix_pi(x, w)

        def fix_pi(x, w):
            m = tmpf[:, 0:w]
            cx.op(DVE, lambda e: e.tensor_scalar(out=m, in0=x, scalar1=-PI, scalar2=None, op0=ALU.is_lt), reads=[t_tk], writes=[t_tk])
            cx.op(DVE, lambda e: e.scalar_tensor_tensor(out=x, in0=m, scalar=2 * PI, in1=x, op0=ALU.mult, op1=ALU.add), reads=[t_tk], writes=[t_tk])
            cx.op(DVE, lambda e: e.tensor_scalar(out=m, in0=x, scalar1=PI, scalar2=None, op0=ALU.is_gt), reads=[t_tk], writes=[t_tk])
            cx.op(DVE, lambda e: e.scalar_tensor_tensor(out=x, in0=m, scalar=-2 * PI, in1=x, op0=ALU.mult, op1=ALU.add), reads=[t_tk], writes=[t_tk])
            cx.op(DVE, lambda e: e.tensor_scalar(out=x, in0=x, scalar1=PI, scalar2=-PI, op0=ALU.min, op1=ALU.max), reads=[t_tk], writes=[t_tk])

        def sincos(x, w, sin_out, cos_out):
            red_pi(x, w)
            cx.op(ACT, lambda e: e.activation(out=sin_out, in_=x, func=AF.Sin), reads=[t_tk], writes=[t_tk])
            cx.op(DVE, lambda e: e.tensor_scalar(out=x, in0=x, scalar1=PI / 2, scalar2=None, op0=ALU.add), reads=[t_tk], writes=[t_tk])
            fix_pi(x, w)
            cx.op(ACT, lambda e: e.activation(out=cos_out, in_=x, func=AF.Sin), reads=[t_tk], writes=[t_tk])

        tmpf = ph.sb([128, TT], F32)
        ang = ph.sb([128, TT], F32)
        jv = ph.sb([128, TT], F32)
        m16 = ph.sb([128, 8], F32)
        cx.dma(SP, jv[:], jv_in, t_tk, writes=[t_tk])
        cx.dma(SP, m16[:], m16_in, t_tk, writes=[t_tk])
        anat = ph.sb([8, 2, 128], F32)
        for hf in range(2):
            cx.dma(SP, anat[:, 0, hf * 64:(hf + 1) * 64], P["s5_a_re"][l], t_tk, writes=[t_tk])
            cx.dma(SP, anat[:, 1, hf * 64:(hf + 1) * 64], P["s5_a_im"][l], t_tk, writes=[t_tk])
        dt_pr = ph.sb([128, 8], F32)
        cx.dma(SP, dt_pr[:], P["s5_log_dt"][l:l + 1, :].partition_broadcast(128).rearrange("p o g -> p (o g)"), t_tk, writes=[t_tk])
        cx.op(ACT, lambda e: e.activation(out=dt_pr[:], in_=dt_pr[:], func=AF.Exp), reads=[t_tk], writes=[t_tk])
        mag_pr = ph.sb([128, 8], F32)
        th_pr = ph.sb([128, 8], F32)
        c512 = ph.sb([128, 8], F32)
        s512 = ph.sb([128, 8], F32)
        for which, dst in ((0, mag_pr), (1, th_pr)):
            pb, pk = PS.next()
            cx.op(PE, lambda e, which=which: e.transpose(pb[:, 0:8], anat[:, which, :], ident_f[0:8, 0:8]), reads=[t_tk, c_tk], writes=[pk])
            cx.op(DVE, lambda e, dst=dst: e.tensor_tensor(out=dst[:], in0=pb[:, 0:8], in1=dt_pr[:], op=ALU.mult), reads=[pk, t_tk], writes=[t_tk])
        cx.op(ACT, lambda e: e.activation(out=mag_pr[:], in_=mag_pr[:], func=AF.Exp), reads=[t_tk], writes=[t_tk])
        cx.op(DVE, lambda e: e.tensor_scalar(out=ang[:, 0:8], in0=th_pr[:], scalar1=float(TT), scalar2=None, op0=ALU.mult), reads=[t_tk], writes=[t_tk])
        sincos(ang[:, 0:8], 8, s512[:], c512[:])
        cosT = ph.sb([128, 8, TT], BF16)
        sinT = ph.sb([128, 8, TT], BF16)
        for g in range(8):
            cx.op(DVE, lambda e, g=g: e.tensor_scalar(out=ang[:], in0=jv[:], scalar1=th_pr[:, g:g + 1], scalar2=None, op0=ALU.mult),
                  reads=[t_tk], writes=[t_tk])
            sincos(ang[:], TT, sinT[:, g, :], cosT[:, g, :])
        Bst = ph.sb([128, 8, 128], BF16)
        Bsw = ph.sb([128, 8, 128], BF16)
        Cg = ph.sb([128, 8, 128], BF16)
        dsk = load_vec(ph, P["s5_d"][l], 1, t_tk)
        cx.op(PL, lambda e: e.memset(Cg[:], 0.0), writes=[t_tk])
        sm = [ph.sb([128, 64], F32) for _ in range(12)]
        ar_, ai_, mg_, th_, sn_, cs_, nr_, ni_, dn_, cr_, ci_, t_ = sm
        dt_gh = ph.sb([128, 1], F32)
        bnat = ph.sb([64, 2, 128], F32)
        bgh = ph.sb([128, 2, 64], F32)
        ball = ph.sb([128, 2, 128], F32)
        cnat = ph.sb([128, 128], F32)
        call = ph.sb([128, 128], F32)
        for Q in range(1):
            G0 = Q * 8
            for g in range(8):
                cx.dma(SP, ar_[g * 16:(g + 1) * 16, :], P["s5_a_re"][l, G0 + g:G0 + g + 1, :].partition_broadcast(16).rearrange("p o g -> p (o g)"), t_tk, writes=[t_tk])
                cx.dma(SP, ai_[g * 16:(g + 1) * 16, :], P["s5_a_im"][l, G0 + g:G0 + g + 1, :].partition_broadcast(16).rearrange("p o g -> p (o g)"), t_tk, writes=[t_tk])
                cx.dma(SP, dt_gh[g * 16:(g + 1) * 16, :], P["s5_log_dt"][l:l + 1, G0 + g:G0 + g + 1].partition_broadcast(16).rearrange("p o g -> p (o g)"), t_tk, writes=[t_tk])
            cx.op(ACT, lambda e: e.activation(out=dt_gh[:], in_=dt_gh[:], func=AF.Exp), reads=[t_tk], writes=[t_tk])
            cx.op(ACT, lambda e: e.activation(out=mg_[:], in_=ar_[:], func=AF.Exp, scale=dt_gh[:, 0:1]), reads=[t_tk], writes=[t_tk])
            cx.op(DVE, lambda e: e.tensor_scalar(out=th_[:], in0=ai_[:], scalar1=dt_gh[:, 0:1], scalar2=None, op0=ALU.mult), reads=[t_tk], writes=[t_tk])
            sincos(th_[:], 64, sn_[:], cs_[:])
            TTo = lambda o, a, b, op: cx.op(DVE, lambda e: e.tensor_tensor(out=o, in0=a, in1=b, op=op), reads=[t_tk], writes=[t_tk])
            TTo(nr_[:], mg_[:], cs_[:], ALU.mult)
            cx.op(DVE, lambda e: e.tensor_scalar(out=nr_[:], in0=nr_[:], scalar1=-1.0, scalar2=None, op0=ALU.add), reads=[t_tk], writes=[t_tk])
            TTo(ni_[:], mg_[:], sn_[:], ALU.mult)
            TTo(dn_[:], ar_[:], ar_[:], ALU.mult)
            TTo(t_[:], ai_[:], ai_[:], ALU.mult)
            TTo(dn_[:], dn_[:], t_[:], ALU.add)
            cx.op(DVE, lambda e: e.reciprocal(out=dn_[:], in_=dn_[:]), reads=[t_tk], writes=[t_tk])
            TTo(cr_[:], nr_[:], ar_[:], ALU.mult)
            TTo(t_[:], ni_[:], ai_[:], ALU.mult)
            TTo(cr_[:], cr_[:], t_[:], ALU.add)
            TTo(cr_[:], cr_[:], dn_[:], ALU.mult)
            TTo(ci_[:], ni_[:], ar_[:], ALU.mult)
            TTo(t_[:], nr_[:], ai_[:], ALU.mult)
            TTo(ci_[:], ci_[:], t_[:], ALU.subtract)
            TTo(ci_[:], ci_[:], dn_[:], ALU.mult)
            cx.dma(SP, bnat[:, 0, :].rearrange("p (g h) -> p g h", h=16), P["s5_b_re"][l, G0:G0 + 8].rearrange("g p h -> p g h"), t_tk, writes=[t_tk])
            cx.dma(SP, bnat[:, 1, :].rearrange("p (g h) -> p g h", h=16), P["s5_b_im"][l, G0:G0 + 8].rearrange("g p h -> p g h"), t_tk, writes=[t_tk])
            for ri in range(2):
                pb, pk = PS.next()
                cx.op(PE, lambda e, ri=ri, pb=pb: e.transpose(pb[:, 0:64], bnat[:, ri, :], ident_f[0:64, 0:64]), reads=[t_tk, c_tk], writes=[pk])
                cx.op(DVE, lambda e, ri=ri, pb=pb: e.tensor_copy(out=bgh[:, ri, :], in_=pb[:, 0:64]), reads=[pk], writes=[t_tk])
            TTo(ball[:, 0, 0:64], cr_[:], bgh[:, 0, :], ALU.mult)
            TTo(t_[:], ci_[:], bgh[:, 1, :], ALU.mult)
            TTo(ball[:, 0, 0:64], ball[:, 0, 0:64], t_[:], ALU.subtract)
            TTo(ball[:, 0, 64:128], cr_[:], bgh[:, 1, :], ALU.mult)
            TTo(t_[:], ci_[:], bgh[:, 0, :], ALU.mult)
            TTo(ball[:, 0, 64:128], ball[:, 0, 64:128], t_[:], ALU.add)
            cx.op(DVE, lambda e: e.tensor_copy(out=ball[:, 1, 0:64], in_=ball[:, 0, 64:128]), reads=[t_tk], writes=[t_tk])
            cx.op(DVE, lambda e: e.tensor_scalar(out=ball[:, 1, 64:128], in0=ball[:, 0, 0:64], scalar1=-1.0, scalar2=None, op0=ALU.mult),
                  reads=[t_tk], writes=[t_tk])
            for g in range(8):
                cx.op(DVE, lambda e, g=g: e.tensor_scalar(out=Bst[:, G0 + g, :], in0=ball[:, 0, :], scalar1=m16[:, g:g + 1], scalar2=None, op0=ALU.mult),
                      reads=[t_tk], writes=[t_tk])
                cx.op(DVE, lambda e, g=g: e.tensor_scalar(out=Bsw[:, G0 + g, :], in0=ball[:, 1, :], scalar1=m16[:, g:g + 1], scalar2=None, op0=ALU.mult),
                      reads=[t_tk], writes=[t_tk])
            cx.dma(SP, cnat[:, 0:64], P["s5_c_re"][l, G0:G0 + 8].rearrange("g h p -> (g h) p"), t_tk, writes=[t_tk])
            cx.dma(SP, cnat[:, 64:128], P["s5_c_im"][l, G0:G0 + 8].rearrange("g h p -> (g h) p"), t_tk, writes=[t_tk])
            pb, pk = PS.next()
            cx.op(PE, lambda e, pb=pb: e.transpose(pb[:, 0:128], cnat[:], ident_f), reads=[t_tk, c_tk], writes=[pk])
            cx.op(DVE, lambda e, pb=pb: e.tensor_copy(out=call[0:64, :], in_=pb[0:64, 0:128]), reads=[pk], writes=[t_tk])
            cx.op(DVE, lambda e, pb=pb: e.tensor_scalar(out=call[64:128, :], in0=pb[64:128, 0:128], scalar1=-1.0, scalar2=None, op0=ALU.mult),
                  reads=[pk], writes=[t_tk])
            for g in range(8):
                cx.op(DVE, lambda e, g=g: e.tensor_copy(out=Cg[:, G0 + g, g * 16:(g + 1) * 16], in_=call[:, g * 16:(g + 1) * 16]),
                      reads=[t_tk], writes=[t_tk])
        vin = ph.sb([128, 8], F32)
        vsin = ph.sb([128, 8], F32)
        cx.op(DVE, lambda e: e.memset(vin[:], 0.0), writes=[t_tk])
        cx.op(DVE, lambda e: e.memset(vsin[:], 0.0), writes=[t_tk])
        st_tk = Tk(); ph.tks.append(st_tk)
        u_pool = ph.pool(2, [128, 1, TT], BF16)
        f_pool = ph.pool(6, [128, TT], F32)
        v_pool = ph.pool(2, [128, 2, TT], F32)
        h_pool = ph.pool(2, [128, TT], BF16)
        c_pool = ph.pool(2, [128, 2], F32)
        y_pool = ph.pool(2, [128, TT], F32)
        z_pool = ph.pool(2, [128, TT], BF16)
        ur = S["uT"].rearrange("(q p) n -> p q n", p=128)

        def ld(i):
            u, uk = u_pool.next()
            cx.dma(SP, u[:], ur[:, :, i * TT:(i + 1) * TT], uk, reads=[tk["uT"]], writes=[uk])
            return u, uk

        nxt = ld(0)
        for i in range(NT):
            u, uk = nxt
            if i + 1 < NT:
                nxt = ld(i + 1)
            for Q in range(1):
                py, pyk = PSA.next()
                for g in range(8):
                    G_ = Q * 8 + g
                    pd, pdk = PS.next()
                    mm_group(pd[:], [(Bst[:, G_, :], u[:, Q, :])], [t_tk, uk], pdk)
                    pds, pdsk = PS.next()
                    mm_group(pds[:], [(Bsw[:, G_, :], u[:, Q, :])], [t_tk, uk], pdsk)
                    t1, t1k = f_pool.next(); t2, t2k = f_pool.next()
                    cx.op(DVE, lambda e: e.tensor_tensor(out=t1[:], in0=pd[:], in1=cosT[:, G_, :], op=ALU.mult), reads=[pdk, t_tk], writes=[t1k])
                    cx.op(DVE, lambda e: e.tensor_tensor(out=t2[:], in0=pds[:], in1=sinT[:, G_, :], op=ALU.mult), reads=[pdsk, t_tk], writes=[t2k])
                    cx.op(DVE, lambda e: e.tensor_tensor(out=t1[:], in0=t1[:], in1=t2[:], op=ALU.add), reads=[t1k, t2k], writes=[t1k])
                    t3, t3k = f_pool.next(); t4, t4k = f_pool.next()
                    cx.op(DVE, lambda e: e.tensor_tensor(out=t3[:], in0=pds[:], in1=cosT[:, G_, :], op=ALU.mult), reads=[pdsk, t_tk], writes=[t3k])
                    cx.op(DVE, lambda e: e.tensor_tensor(out=t4[:], in0=pd[:], in1=sinT[:, G_, :], op=ALU.mult), reads=[pdk, t_tk], writes=[t4k])
                    cx.op(DVE, lambda e: e.tensor_tensor(out=t3[:], in0=t3[:], in1=t4[:], op=ALU.subtract), reads=[t3k, t4k], writes=[t3k])
                    vv, vvk = v_pool.next()
                    magb = mag_pr[:, G_:G_ + 1].to_broadcast([128, TT])
                    cx.op(DVE, lambda e: e.tensor_tensor_scan(out=vv[:, 0, :], data0=magb, data1=t1[:], initial=vin[:, G_:G_ + 1], op0=ALU.mult, op1=ALU.add),
                          reads=[t1k, t_tk, st_tk], writes=[vvk])
                    cx.op(DVE, lambda e: e.tensor_tensor_scan(out=vv[:, 1, :], data0=magb, data1=t3[:], initial=vsin[:, G_:G_ + 1], op0=ALU.mult, op1=ALU.add),
                          reads=[t3k, t_tk, st_tk], writes=[vvk])
                    cc_, cck = c_pool.next()
                    cx.op(DVE, lambda e: e.tensor_scalar(out=cc_[:, 0:1], in0=vv[:, 1, TT - 1:TT], scalar1=s512[:, G_:G_ + 1], scalar2=None, op0=ALU.mult),
                          reads=[vvk, t_tk], writes=[cck])
                    cx.op(DVE, lambda e: e.tensor_scalar(out=cc_[:, 1:2], in0=vv[:, 0, TT - 1:TT], scalar1=s512[:, G_:G_ + 1], scalar2=None, op0=ALU.mult),
                          reads=[vvk, t_tk], writes=[cck])
                    cx.op(DVE, lambda e: e.scalar_tensor_tensor(out=vin[:, G_:G_ + 1], in0=vv[:, 0, TT - 1:TT], scalar=c512[:, G_:G_ + 1], in1=cc_[:, 0:1],
                                                               op0=ALU.mult, op1=ALU.subtract), reads=[vvk, cck, t_tk], writes=[st_tk])
                    cx.op(DVE, lambda e: e.scalar_tensor_tensor(out=vsin[:, G_:G_ + 1], in0=vv[:, 1, TT - 1:TT], scalar=c512[:, G_:G_ + 1], in1=cc_[:, 1:2],
                                                               op0=ALU.mult, op1=ALU.add), reads=[vvk, cck, t_tk], writes=[st_tk])
                    cx.op(DVE, lambda e: e.tensor_tensor(out=t2[:], in0=vv[:, 0, :], in1=cosT[:, G_, :], op=ALU.mult), reads=[vvk, t_tk], writes=[t2k])
                    cx.op(DVE, lambda e: e.tensor_tensor(out=t4[:], in0=vv[:, 1, :], in1=sinT[:, G_, :], op=ALU.mult), reads=[vvk, t_tk], writes=[t4k])
                    hh, hhk = h_pool.next()
                    cx.op(DVE, lambda e: e.tensor_tensor(out=hh[:], in0=t2[:], in1=t4[:], op=ALU.subtract), reads=[t2k, t4k], writes=[hhk])
                    cx.op(PE, lambda e, g=g: e.matmul(py[:], lhsT=Cg[:, G_, :], rhs=hh[:], start=(g == 0), stop=(g == 7)),
                          reads=[hhk, t_tk], writes=[pyk], inc=True)
                yv, yvk = y_pool.next()
                cx.op(DVE, lambda e: e.scalar_tensor_tensor(out=yv[:], in0=u[:, Q, :], scalar=dsk[:, Q:Q + 1], in1=py[:], op0=ALU.mult, op1=ALU.add),
                      reads=[uk, pyk, t_tk], writes=[yvk])
                g1_, g1k = f_pool.next()
                cx.op(DVE, lambda e: e.tensor_tensor(out=g1_[:], in0=yv[:], in1=yv[:], op=ALU.mult), reads=[yvk], writes=[g1k])
                cx.op(DVE, lambda e: e.tensor_scalar(out=g1_[:], in0=g1_[:], scalar1=0.044715, scalar2=1.0, op0=ALU.mult, op1=ALU.add), reads=[g1k], writes=[g1k])
                cx.op(DVE, lambda e: e.tensor_tensor(out=g1_[:], in0=g1_[:], in1=yv[:], op=ALU.mult), reads=[g1k, yvk], writes=[g1k])
                cx.op(ACT, lambda e: e.activation(out=g1_[:], in_=g1_[:], func=AF.Sigmoid, scale=2.0 * 0.7978845608028654), reads=[g1k], writes=[g1k])
                z, zk = z_pool.next()
                cx.op(DVE, lambda e: e.tensor_tensor(out=z[:], in0=yv[:], in1=g1_[:], op=ALU.mult), reads=[yvk, g1k], writes=[zk])
                cx.dma(PL, S["ysrc"][2 * i:2 * i + 2, 128:256, :].rearrange("c p n -> p c n"), z[:].rearrange("p (c n) -> p c n", c=2), zk, reads=[zk], writes=[tk["ysrc"]])
        ph.close()

    def phase_hg(l, S):
        ph = Phase(cx, f"hg_{l}")
        tk = S["tk"]
        g_tk = Tk(); ph.tks.append(g_tk)
        og = ph.sb([128, 1], F32)
        cx.dma(SP, og[:], P["hg_onorm"][l].rearrange("(p o) -> p o", o=1), g_tk, writes=[g_tk])
        msk = ph.sb([128, TT], F32)
        cx.op(DVE, lambda e: e.memset(msk[:], 1.0), writes=[g_tk])
        cx.op(DVE, lambda e: e.memset(msk[:].rearrange("p (c j) -> p c j", j=HC)[:, :, 0:1], 0.0), writes=[g_tk])
        inb = ph.pool(2, [128, 3, TT], BF16)
        inf = ph.pool(2, [128, TT], F32)
        inv = ph.pool(2, [64, 8, 128], BF16)
        b_pool = ph.pool(2, [128, TT], F32)
        d_pool = ph.pool(2, [128, TT], F32)
        e_pool = ph.pool(3, [128, TT], F32)
        w_pool = ph.pool(2, [128, 4, TT], BF16)
        eb_pool = ph.pool(2, [128, 8], F32)
        am_pool = ph.pool(2, [64, TT], BF16)
        kt_pool = ph.pool(2, [64, 8, 128], BF16)
        stb_pool = ph.pool(2, [128, 9, 128], BF16)
        stf = [ph.sbt([128, 128], F32) for _ in range(2)]
        sq_pool = ph.pool(2, [128, TT], BF16)
        rs_pool = ph.pool(2, [128, TT], F32)
        y_pool = ph.pool(2, [128, TT], BF16)
        nch = TT // HC

        def ld(hd, i):
            ib, ibk = inb.next(); lf, lfk = inf.next(); v, vk = inv.next()
            sl = slice(i * TT, (i + 1) * TT)
            r0 = hd * 128
            cx.dma(SP, ib[:, 0, :], S["hqT"][r0:r0 + 128, sl], ibk, reads=[tk["hqT"]], writes=[ibk])
            cx.dma(SP, ib[:, 1, :], S["khT"][r0:r0 + 128, sl], ibk, reads=[tk["khT"]], writes=[ibk])
            cx.dma(SP, ib[:, 2, :], S["hgT"][r0:r0 + 128, sl], ibk, reads=[tk["hgT"]], writes=[ibk])
            cx.dma(SP, lf[:], S["lfhT"][r0:r0 + 128, sl], lfk, reads=[tk["lfhT"]], writes=[lfk])
            cx.dma(SP, v[:], S["hvP"][hd, :, i * nch:(i + 1) * nch, :], vk, reads=[tk["hvP"]], writes=[vk])
            return ib, ibk, lf, lfk, v, vk

        for hd in range(1):
            cur = 0
            cx.op(DVE, lambda e: e.memset(stf[0][0][:], 0.0), writes=[stf[0][1]])
            stb, stbk = stb_pool.next()
            cx.op(PL, lambda e, stb=stb: e.memset(stb[:, 0, :], 0.0), writes=[stbk])
            nxt = ld(hd, 0)
            for i in range(NT):
                ib, ibk, lf, lfk, v, vk = nxt
                if i + 1 < NT:
                    nxt = ld(hd, i + 1)
                b, bk = b_pool.next()
                cx.op(DVE, lambda e: e.tensor_tensor_scan(out=b[:], data0=msk[:], data1=lf[:], initial=0.0, op0=ALU.mult, op1=ALU.add),
                      reads=[lfk, g_tk], writes=[bk])
                b3 = b[:].rearrange("p (c j) -> p c j", j=HC)
                w, wk = w_pool.next()
                d1, d1k = d_pool.next()
                cx.op(DVE, lambda e: e.tensor_tensor(out=d1[:].rearrange("p (c j) -> p c j", j=HC), in0=b3,
                                                     in1=b3[:, :, HC // 2 - 1:HC // 2].to_broadcast([128, nch, HC]), op=ALU.subtract),
                      reads=[bk], writes=[d1k])
                e1, e1k = e_pool.next()
                cx.op(ACT, lambda e: e.activation(out=e1[:], in_=d1[:], func=AF.Exp), reads=[d1k], writes=[e1k])
                cx.op(DVE, lambda e: e.tensor_tensor(out=w[:, 0, :], in0=ib[:, 0, :], in1=e1[:], op=ALU.mult), reads=[ibk, e1k], writes=[wk])
                e2, e2k = e_pool.next()
                cx.op(ACT, lambda e: e.activation(out=e2[:], in_=d1[:], func=AF.Exp, scale=-1.0), reads=[d1k], writes=[e2k])
                cx.op(DVE, lambda e: e.tensor_tensor(out=w[:, 1, :], in0=ib[:, 1, :], in1=e2[:], op=ALU.mult), reads=[ibk, e2k], writes=[wk])
                e3, e3k = e_pool.next()
                cx.op(ACT, lambda e: e.activation(out=e3[:], in_=b[:], func=AF.Exp), reads=[bk], writes=[e3k])
                cx.op(DVE, lambda e: e.tensor_tensor(out=w[:, 2, :], in0=ib[:, 0, :], in1=e3[:], op=ALU.mult), reads=[ibk, e3k], writes=[wk])
                d2, d2k = d_pool.next()
                cx.op(DVE, lambda e: e.tensor_tensor(out=d2[:].rearrange("p (c j) -> p c j", j=HC), in0=b3,
                                                     in1=b3[:, :, HC - 1:HC].to_broadcast([128, nch, HC]), op=ALU.subtract),
                      reads=[bk], writes=[d2k])
                e4, e4k = e_pool.next()
                cx.op(ACT, lambda e: e.activation(out=e4[:], in_=d2[:], func=AF.Exp, scale=-1.0), reads=[d2k], writes=[e4k])
                cx.op(DVE, lambda e: e.tensor_tensor(out=w[:, 3, :], in0=ib[:, 1, :], in1=e4[:], op=ALU.mult), reads=[ibk, e4k], writes=[wk])
                eb, ebk = eb_pool.next()
                cx.op(ACT, lambda e: e.activation(out=eb[:].unsqueeze(2), in_=b3[:, :, HC - 1:HC], func=AF.Exp), reads=[bk], writes=[ebk])
                pa, pak = PS.next()
                for c in range(nch):
                    cs = slice(c * HC, (c + 1) * HC)
                    cx.op(PE, lambda e, cs=cs: e.matmul(pa[0:HC, cs], lhsT=w[:, 1, cs], rhs=w[:, 0, cs], start=True, stop=True),
                          reads=[wk], writes=[pak], inc=(c == nch - 1))
                ptp, ptk_ = PS.next()
                ptb = ptp[:].bitcast(BF16)
                for c in range(nch):
                    cs = slice(c * HC, (c + 1) * HC)
                    cx.op(PE, lambda e, cs=cs, c=c: e.transpose(ptb[0:HC, c * 128:(c + 1) * 128], w[:, 3, cs], ident_b),
                          reads=[wk, c_tk], writes=[ptk_], inc=(c == nch - 1))
                am, amk = am_pool.next()
                cx.op(DVE, lambda e: e.tensor_tensor(out=am[:].rearrange("p (c j) -> p c j", j=HC),
                                                     in0=pa[0:HC, :].rearrange("p (c j) -> p c j", j=HC),
                                                     in1=tri_b[0:HC, 0:HC].unsqueeze(1).to_broadcast([HC, nch, HC]), op=ALU.mult),
                      reads=[pak, c_tk], writes=[amk])
                kt, ktk = kt_pool.next()
                cx.op(ACT, lambda e: e.activation(out=kt[:].rearrange("p c k -> p (c k)"), in_=ptb[0:HC, 0:nch * 128], func=AF.Copy),
                      reads=[ptk_], writes=[ktk])
                kvs = []
                for half in range(2):
                    pkv, pkvk = PS.next()
                    for c4 in range(4):
                        c = half * 4 + c4
                        cx.op(PE, lambda e, c=c, c4=c4, pkv=pkv: e.matmul(pkv[:, c4 * 128:(c4 + 1) * 128], lhsT=kt[:, c, :], rhs=v[:, c, :],
                                                                        start=True, stop=True),
                              reads=[ktk, vk], writes=[pkvk], inc=(c4 == 3))
                    kvs.append((pkv, pkvk))
                for c in range(nch):
                    pkv, pkvk = kvs[c // 4]
                    src, srck = stf[cur]
                    dst, dstk = stf[1 - cur]
                    cx.op(DVE, lambda e, c=c, pkv=pkv, src=src, dst=dst: e.scalar_tensor_tensor(
                        out=dst[:], in0=src[:], scalar=eb[:, c:c + 1], in1=pkv[:, (c % 4) * 128:(c % 4 + 1) * 128], op0=ALU.mult, op1=ALU.add),
                        reads=[srck, ebk, pkvk], writes=[dstk])
                    cur = 1 - cur
                    if c < nch - 1:
                        cx.op(PL, lambda e, c=c, dst=dst, stb=stb: e.tensor_copy(out=stb[:, c + 1, :], in_=dst[:]), reads=[dstk], writes=[stbk])
                    else:
                        nstb, nstbk = stb_pool.next()
                        cx.op(PL, lambda e, dst=dst, nstb=nstb: e.tensor_copy(out=nstb[:, 0, :], in_=dst[:]), reads=[dstk], writes=[nstbk])
                po, pok = PS.next()
                for c in range(nch):
                    cs = slice(c * HC, (c + 1) * HC)
                    cx.op(PE, lambda e, cs=cs, c=c, stb=stb: e.matmul(po[:, cs], lhsT=stb[:, c, :], rhs=w[:, 2, cs], start=True, stop=False),
                          reads=[stbk, wk], writes=[pok], inc=False)
                    cx.op(PE, lambda e, cs=cs, c=c: e.matmul(po[:, cs], lhsT=v[:, c, :], rhs=am[0:HC, cs], start=False, stop=True),
                          reads=[vk, amk], writes=[pok], inc=(c == nch - 1))
                stb, stbk = nstb, nstbk
                sq, sqk = sq_pool.next()
                cx.op(ACT, lambda e: e.activation(out=sq[:], in_=po[:], func=AF.Square), reads=[pok], writes=[sqk])
                pn, pnk = PS.next()
                mm_group(pn[:], [(ones_b, sq[:])], [sqk, c_tk], pnk)
                rs, rk = rs_pool.next()
                cx.op(ACT, lambda e: e.activation(out=rs[:], in_=pn[:], func=AF.Sqrt, scale=1.0 / 128, bias=eps_t[:, 0:1]),
                      reads=[pnk, c_tk], writes=[rk])
                cx.op(DVE, lambda e: e.reciprocal(out=rs[:], in_=rs[:]), reads=[rk], writes=[rk])
                cx.op(DVE, lambda e: e.scalar_tensor_tensor(out=rs[:], in0=po[:], scalar=og[:, 0:1], in1=rs[:], op0=ALU.mult, op1=ALU.mult),
                      reads=[pok, rk, g_tk], writes=[rk])
                y, yk = y_pool.next()
                cx.op(DVE, lambda e: e.tensor_tensor(out=y[:], in0=rs[:], in1=ib[:, 2, :], op=ALU.mult), reads=[rk, ibk], writes=[yk])
                cx.dma(PL, S["ysrc"][2 * i:2 * i + 2, 256:384, :].rearrange("c p n -> p c n"), y[:].rearrange("p (c n) -> p c n", c=2), yk, reads=[yk], writes=[tk["ysrc"]])
        ph.close()

    def phase3a(l, xT, x_tk, S):
        ph = Phase(cx, f"p3a_{l}")
        tk = S["tk"]
        stg_pool = ph.pool(2, [128, 8, 512], F32)
        g_tk = Tk(); ph.tks.append(g_tk)
        g1 = load_vec(ph, P["norm_mix"][l], 8, g_tk)
        bglu = load_vec(ph, P["s5_b_glu"][l], 4, g_tk)
        Wg, wg_tk = ph.sbt([128, 8, 3072], BF16)
        Wb, wb_tk = ph.sbt([128, 12, 1024], BF16)
        Wo, wo_tk = ph.sbt([128, 8, 1024], BF16)
        Wgl, wgl_tk = ph.sbt([128, 4, 512], BF16)
        load_w(ph, P["w_in"][l], 8, 1026, 4098, gain=(g1, g_tk), stg_pool=stg_pool, w_out=Wg, w_tk=wg_tk)
        load_w(ph, P["w_branch"][l], 12, 0, 1024, stg_pool=stg_pool, w_out=Wb, w_tk=wb_tk)
        load_w(ph, P["w_out"][l], 8, 0, 1024, stg_pool=stg_pool, w_out=Wo, w_tk=wo_tk)
        load_w(ph, P["s5_w_glu"][l], 4, 0, 512, stg_pool=stg_pool, w_out=Wgl, w_tk=wgl_tk)
        TW = 512
        x_pool = stg_pool
        y_pool = ph.pool(2, [128, 12, TW], BF16)
        sq_pool = ph.pool(1, [128, 8, TW], BF16)
        h_pool = ph.pool(1, [128, 8, TW], BF16)
        rs_pool = ph.pool(2, [128, TW], F32)
        mg_pool = ph.pool(1, [128, 8, TW], BF16)
        mgs, mgsk = ph.sbt([128, 4, TW], BF16)
        evf = ph.pool(4, [128, TW], F32)
        xr = xT.rearrange("(kt p) n -> p kt n", p=128)
        x1r = S["x1"].rearrange("(kt p) n -> p kt n", p=128)

        def ld(i):
            xt, xk = x_pool.next()
            cx.dma(SP, xt[:], xr[:, :, i * TW:(i + 1) * TW], xk, reads=[x_tk], writes=[xk])
            yt, yk = y_pool.next()
            for hf_ in range(2):
                for b_ in range(3):
                    cx.dma(SP, yt[:, b_ * 4:(b_ + 1) * 4, hf_ * 256:(hf_ + 1) * 256],
                           S["ydst"][2 * i + hf_].rearrange("(r b p) n -> b p r n", r=4, b=3, p=128)[b_], yk, reads=[tk["ydst"]], writes=[yk])
            return xt, xk, yt, yk

        nxt = ld(0)
        for i in range(N // TW):
            xt, xk, yt, yk = nxt
            if i + 1 < N // TW:
                nxt = ld(i + 1)
            rs, rk = rms_rstd(ph, xt[:], xk, 8, TW, sq_pool, rs_pool)
            h, hk = h_pool.next()
            cx.op(DVE, lambda e: e.tensor_tensor(out=h[:], in0=xt[:], in1=bc(rs[:], [128, 8, TW], 1), op=ALU.mult),
                  reads=[xk, rk], writes=[hk])
            for ct in range(4):
                pb, pk = PS.next()
                mm_group(pb[:, 0:TW], [(Wgl[:, k, ct * 128:(ct + 1) * 128], yt[:, 4 + k, :]) for k in range(4)], [wgl_tk, yk], pk)
                sg, sgk = evf.next()
                cx.op(ACT, lambda e, ct=ct: e.activation(out=sg[:], in_=pb[:, 0:TW], func=AF.Sigmoid, bias=bglu[:, ct:ct + 1]),
                      reads=[pk, g_tk], writes=[sgk])
                cx.op(DVE, lambda e, ct=ct: e.tensor_tensor(out=mgs[:, ct, :], in0=yt[:, 4 + ct, :], in1=sg[:], op=ALU.mult),
                      reads=[yk, sgk], writes=[mgsk])
            mg, mgk = mg_pool.next()
            for ct in range(8):
                acc, acck = evf.next()
                for b in range(3):
                    pg, pgk = PS.next()
                    c0 = b * 1024 + ct * 128
                    mm_group(pg[:, 0:TW], [(Wg[:, k, c0:c0 + 128], h[:, k, :]) for k in range(8)], [wg_tk, hk], pgk)
                    gs, gsk = evf.next()
                    cx.op(ACT, lambda e: e.activation(out=gs[:], in_=pg[:, 0:TW], func=AF.Sigmoid), reads=[pgk], writes=[gsk])
                    pbr, pbk = PS.next()
                    if b == 1:
                        prs = [(Wb[:, 4 + k, ct * 128:(ct + 1) * 128], mgs[:, k, :]) for k in range(4)]
                        rd = [wb_tk, mgsk]
                    else:
                        prs = [(Wb[:, b * 4 + k, ct * 128:(ct + 1) * 128], yt[:, b * 4 + k, :]) for k in range(4)]
                        rd = [wb_tk, yk]
                    mm_group(pbr[:, 0:TW], prs, rd, pbk)
                    if b == 0:
                        cx.op(DVE, lambda e: e.tensor_tensor(out=acc[:], in0=pbr[:, 0:TW], in1=gs[:], op=ALU.mult),
                              reads=[pbk, gsk], writes=[acck])
                    else:
                        cx.op(DVE, lambda e: e.tensor_tensor(out=gs[:], in0=pbr[:, 0:TW], in1=gs[:], op=ALU.mult),
                              reads=[pbk, gsk], writes=[gsk])
                        dstap = mg[:, ct, :] if b == 2 else acc[:]
                        dstk = mgk if b == 2 else acck
                        cx.op(DVE, lambda e, dstap=dstap: e.tensor_tensor(out=dstap, in0=acc[:], in1=gs[:], op=ALU.add),
                              reads=[acck, gsk], writes=[dstk])
            for ct in range(8):
                pb, pk = PS.next()
                mm_group(pb[:, 0:TW], [(Wo[:, k, ct * 128:(ct + 1) * 128], mg[:, k, :]) for k in range(8)], [wo_tk, mgk], pk)
                cx.op(DVE, lambda e, ct=ct: e.tensor_tensor(out=xt[:, ct, :], in0=xt[:, ct, :], in1=pb[:, 0:TW], op=ALU.add),
                      reads=[pk, xk], writes=[xk])
            cx.dma(PL, x1r[:, :, i * TW:(i + 1) * TW], xt[:], xk, reads=[xk], writes=[tk["x1"]])
        ph.close()

    def phase3b(l, S):
        ph = Phase(cx, f"p3b_{l}")
        tk = S["tk"]
        stg_pool = ph.pool(2, [128, 8, 512], F32)
        g_tk = Tk(); ph.tks.append(g_tk)
        gx = load_vec(ph, P["norm_x"][l], 8, g_tk)
        gm = load_vec(ph, P["norm_mem"][l], 8, g_tk)
        gq = load_vec(ph, P["x_qnorm"][l], 2, g_tk)
        gk = load_vec(ph, P["x_knorm"][l], 2, g_tk)
        cx.op(DVE, lambda e: e.tensor_scalar(out=gq[:], in0=gq[:], scalar1=1.0 / 16, scalar2=None, op0=ALU.mult), reads=[g_tk], writes=[g_tk])
        Wq, wq_tk = ph.sbt([128, 8, 1024], BF16)
        Wo, wo_tk = ph.sbt([128, 8, 1024], BF16)
        Wkv, wkv_tk = ph.sbt([128, 8, 1024], BF16)
        load_w(ph, P["xq"][l], 8, 0, 1024, gain=(gx, g_tk), stg_pool=stg_pool, w_out=Wq, w_tk=wq_tk)
        load_w(ph, P["xo"][l], 8, 0, 1024, stg_pool=stg_pool, w_out=Wo, w_tk=wo_tk)
        TW = 512
        sq_pool = ph.pool(1, [128, 8, TW], BF16)
        rs_pool = ph.pool(2, [128, TW], F32)
        evb = ph.pool(4, [128, TW], BF16)
        evf = ph.pool(4, [128, TW], F32)
        mt_, mk = ph.sbt([128, 8, NMEM], F32)
        cx.dma(SP, mt_[:], memT_in.rearrange("(kt p) n -> p kt n", p=128), mk, writes=[mk])
        rs, rk = rms_rstd(ph, mt_[:], mk, 8, NMEM, sq_pool, rs_pool)
        hm, hmk = ph.sbt([128, 8, NMEM], BF16)
        cx.op(DVE, lambda e: e.tensor_tensor(out=hm[:], in0=mt_[:], in1=bc(rs[:, 0:NMEM], [128, 8, NMEM], 1), op=ALU.mult),
              reads=[mk, rk], writes=[hmk])
        kx, kxk = ph.sbt([128, 8, NMEM], BF16)
        vx, vxk = ph.sbt([128, 2, 1024], BF16)
        load_w(ph, P["xk"][l], 8, 0, 1024, gain=(gm, g_tk), stg_pool=stg_pool, w_out=Wkv, w_tk=wkv_tk)
        for hd in range(4):
            pbs = []
            for dt_ in range(2):
                ct = hd * 2 + dt_
                pb, pk = PS.next()
                mm_group(pb[:, 0:NMEM], [(Wkv[:, k, ct * 128:(ct + 1) * 128], hm[:, k, :]) for k in range(8)], [wkv_tk, hmk], pk)
                pbs.append((pb, pk))
            sqs = []
            for (pb, pk) in pbs:
                sq, sqk = evb.next()
                cx.op(ACT, lambda e, pb=pb, sq=sq: e.activation(out=sq[:, 0:NMEM], in_=pb[:, 0:NMEM], func=AF.Square), reads=[pk], writes=[sqk])
                sqs.append((sq, sqk))
            p2, p2k = PS.next()
            mm_group(p2[:, 0:NMEM], [(ones_b, sq[:, 0:NMEM]) for (sq, _) in sqs], [sqs[0][1], sqs[1][1], c_tk], p2k)
            rh, rhk = evf.next()
            cx.op(ACT, lambda e: e.activation(out=rh[:, 0:NMEM], in_=p2[:, 0:NMEM], func=AF.Sqrt, scale=1.0 / 256, bias=eps_t[:, 0:1]),
                  reads=[p2k, c_tk], writes=[rhk])
            cx.op(DVE, lambda e: e.reciprocal(out=rh[:, 0:NMEM], in_=rh[:, 0:NMEM]), reads=[rhk], writes=[rhk])
            for dt_, (pb, pk) in enumerate(pbs):
                cx.op(DVE, lambda e, pb=pb, dt_=dt_: e.scalar_tensor_tensor(out=kx[:, hd * 2 + dt_, :], in0=pb[:, 0:NMEM], scalar=gk[:, dt_:dt_ + 1],
                                                                       in1=rh[:, 0:NMEM], op0=ALU.mult, op1=ALU.mult),
                      reads=[pk, rhk, g_tk], writes=[kxk])
        load_w(ph, P["xv"][l], 8, 0, 1024, gain=(gm, g_tk), stg_pool=stg_pool, w_out=Wkv, w_tk=wkv_tk)
        for mt2 in range(2):
            for cc_ in range(2):
                pb, pk = PS.next()
                mm_group(pb[:], [(hm[:, k, mt2 * 128:(mt2 + 1) * 128], Wkv[:, k, cc_ * 512:(cc_ + 1) * 512]) for k in range(8)], [wkv_tk, hmk], pk)
                cx.op(ACT, lambda e, pb=pb, mt2=mt2, cc_=cc_: e.activation(out=vx[:, mt2, cc_ * 512:(cc_ + 1) * 512], in_=pb[:], func=AF.Copy),
                      reads=[pk], writes=[vxk])
        x_pool = stg_pool
        h_pool = ph.pool(1, [128, 8, TW], BF16)
        qh_pool = ph.pool(2, [128, 2, TW], BF16)
        pt_pool = ph.pool(2, [128, 2, TW], BF16)
        o_pool = ph.pool(1, [128, 8, TW], BF16)
        x1r = S["x1"].rearrange("(kt p) n -> p kt n", p=128)
        x2r = S["x2"].rearrange("(kt p) n -> p kt n", p=128)

        def ld(i):
            xt, xk = x_pool.next()
            cx.dma(SP, xt[:], x1r[:, :, i * TW:(i + 1) * TW], xk, reads=[tk["x1"]], writes=[xk])
            return xt, xk

        nxt = ld(0)
        for i in range(N // TW):
            xt, xk = nxt
            if i + 1 < N // TW:
                nxt = ld(i + 1)
            rs, rk = rms_rstd(ph, xt[:], xk, 8, TW, sq_pool, rs_pool)
            h, hk = h_pool.next()
            cx.op(DVE, lambda e: e.tensor_tensor(out=h[:], in0=xt[:], in1=bc(rs[:], [128, 8, TW], 1), op=ALU.mult),
                  reads=[xk, rk], writes=[hk])
            oT, oTk = o_pool.next()
            for hd in range(4):
                pbs = []
                for dt_ in range(2):
                    ct = hd * 2 + dt_
                    pb, pk = PS.next()
                    mm_group(pb[:, 0:TW], [(Wq[:, k, ct * 128:(ct + 1) * 128], h[:, k, :]) for k in range(8)], [wq_tk, hk], pk)
                    pbs.append((pb, pk))
                sqs = []
                for (pb, pk) in pbs:
                    sq, sqk = evb.next()
                    cx.op(ACT, lambda e, pb=pb, sq=sq: e.activation(out=sq[:], in_=pb[:, 0:TW], func=AF.Square), reads=[pk], writes=[sqk])
                    sqs.append((sq, sqk))
                p2, p2k = PS.next()
                mm_group(p2[:, 0:TW], [(ones_b, sq[:]) for (sq, _) in sqs], [sqs[0][1], sqs[1][1], c_tk], p2k)
                rh, rhk = evf.next()
                cx.op(ACT, lambda e: e.activation(out=rh[:], in_=p2[:, 0:TW], func=AF.Sqrt, scale=1.0 / 256, bias=eps_t[:, 0:1]),
                      reads=[p2k, c_tk], writes=[rhk])
                cx.op(DVE, lambda e: e.reciprocal(out=rh[:], in_=rh[:]), reads=[rhk], writes=[rhk])
                qh, qhk = qh_pool.next()
                for dt_, (pb, pk) in enumerate(pbs):
                    cx.op(DVE, lambda e, pb=pb, dt_=dt_: e.scalar_tensor_tensor(out=qh[:, dt_, :], in0=pb[:, 0:TW], scalar=gq[:, dt_:dt_ + 1],
                                                                           in1=rh[:], op0=ALU.mult, op1=ALU.mult),
                          reads=[pk, rhk, g_tk], writes=[qhk])
                pt, ptk = pt_pool.next()
                for mt2 in range(2):
                    pb, pk = PS.next()
                    mm_group(pb[:, 0:TW], [(kx[:, hd * 2 + dt_, mt2 * 128:(mt2 + 1) * 128], qh[:, dt_, :]) for dt_ in range(2)], [kxk, qhk], pk)
                    cx.op(ACT, lambda e, pb=pb, mt2=mt2: e.activation(out=pt[:, mt2, :], in_=pb[:, 0:TW], func=AF.Exp), reads=[pk], writes=[ptk])
                p3, p3k = PS.next()
                mm_group(p3[:, 0:TW], [(ones_b, pt[:, mt2, :]) for mt2 in range(2)], [ptk, c_tk], p3k)
                ri, rik = evf.next()
                cx.op(DVE, lambda e: e.reciprocal(out=ri[:], in_=p3[:, 0:TW]), reads=[p3k], writes=[rik])
                for dt_ in range(2):
                    pb, pk = PS.next()
                    c0 = hd * 256 + dt_ * 128
                    mm_group(pb[:, 0:TW], [(vx[:, mt2, c0:c0 + 128], pt[:, mt2, :]) for mt2 in range(2)], [vxk, ptk], pk)
                    cx.op(DVE, lambda e, pb=pb, dt_=dt_: e.tensor_tensor(out=oT[:, hd * 2 + dt_, :], in0=pb[:, 0:TW], in1=ri[:], op=ALU.mult),
                          reads=[pk, rik], writes=[oTk])
            for ct in range(8):
                pb, pk = PS.next()
                mm_group(pb[:, 0:TW], [(Wo[:, k, ct * 128:(ct + 1) * 128], oT[:, k, :]) for k in range(8)], [wo_tk, oTk], pk)
                cx.op(DVE, lambda e, ct=ct, pb=pb: e.tensor_tensor(out=xt[:, ct, :], in0=xt[:, ct, :], in1=pb[:, 0:TW], op=ALU.add),
                      reads=[pk, xk], writes=[xk])
            cx.dma(PL, x2r[:, :, i * TW:(i + 1) * TW], xt[:], xk, reads=[xk], writes=[tk["x2"]])
        ph.close()

    def phase3c(l, S):
        ph = Phase(cx, f"p3c_{l}")
        tk = S["tk"]
        stg_pool = ph.pool(2, [128, 8, 512], F32)
        g_tk = Tk(); ph.tks.append(g_tk)
        gf = load_vec(ph, P["norm_ffn"][l], 8, g_tk)
        Wgu, wgu_tk = ph.sbt([128, 8, 2 * DFF], BF16)
        Wd, wd_tk = ph.sbt([128, 22, 1024], BF16)
        load_w(ph, P["w_gate_up"][l], 8, 0, 2 * DFF, gain=(gf, g_tk), stg_pool=stg_pool, w_out=Wgu, w_tk=wgu_tk)
        load_w(ph, P["w_down"][l], 22, 0, 1024, stg_pool=stg_pool, w_out=Wd, w_tk=wd_tk)
        TW = 512
        hid_pool = ph.pool(1, [128, 22, TW], BF16)
        sq_pool = hid_pool
        rs_pool = ph.pool(2, [128, TW], F32)
        evf = ph.pool(2, [128, TW], F32)
        x_pool = stg_pool
        h_pool = ph.pool(1, [128, 8, TW], BF16)
        x2r = S["x2"].rearrange("(kt p) n -> p kt n", p=128)
        x3r = S["x3"].rearrange("(kt p) n -> p kt n", p=128)

        def ld(i):
            xt, xk = x_pool.next()
            cx.dma(SP, xt[:], x2r[:, :, i * TW:(i + 1) * TW], xk, reads=[tk["x2"]], writes=[xk])
            return xt, xk

        nxt = ld(0)
        for i in range(N // TW):
            xt, xk = nxt
            if i + 1 < N // TW:
                nxt = ld(i + 1)
            rs, rk = rms_rstd(ph, xt[:], xk, 8, TW, sq_pool, rs_pool)
            h, hk = h_pool.next()
            cx.op(DVE, lambda e: e.tensor_tensor(out=h[:], in0=xt[:], in1=bc(rs[:], [128, 8, TW], 1), op=ALU.mult),
                  reads=[xk, rk], writes=[hk])
            hid, hidk = hid_pool.next()
            for j in range(22):
                pa, pak = PS.next()
                mm_group(pa[:, 0:TW], [(Wgu[:, k, j * 128:(j + 1) * 128], h[:, k, :]) for k in range(8)], [wgu_tk, hk], pak)
                sa, sak = evf.next()
                cx.op(ACT, lambda e, pa=pa, sa=sa: e.activation(out=sa[:], in_=pa[:, 0:TW], func=AF.Silu), reads=[pak], writes=[sak])
                pb, pk = PS.next()
                mm_group(pb[:, 0:TW], [(Wgu[:, k, DFF + j * 128:DFF + (j + 1) * 128], h[:, k, :]) for k in range(8)], [wgu_tk, hk], pk)
                cx.op(DVE, lambda e, pb=pb, sa=sa, j=j: e.tensor_tensor(out=hid[:, j, :], in0=pb[:, 0:TW], in1=sa[:], op=ALU.mult),
                      reads=[pk, sak], writes=[hidk])
            for ct in range(8):
                pb, pk = PS.next()
                mm_group(pb[:, 0:TW], [(Wd[:, j, ct * 128:(ct + 1) * 128], hid[:, j, :]) for j in range(22)], [wd_tk, hidk], pk)
                cx.op(DVE, lambda e, ct=ct, pb=pb: e.tensor_tensor(out=xt[:, ct, :], in0=xt[:, ct, :], in1=pb[:, 0:TW], op=ALU.add),
                      reads=[pk, xk], writes=[xk])
            cx.dma(PL, x3r[:, :, i * TW:(i + 1) * TW], xt[:], xk, reads=[xk], writes=[S["tk"]["x3"]])
        ph.close()

    ccsem = cx.get_dsem()

    def gather_y(l, S):
        tk = S["tk"]
        cx._need(PL, cx._deps([tk["ysrc"]], [tk["ydst"]]))
        for c in range(N // 256):
            ccsem.count += 1
            nc.gpsimd.collective_compute("AllGather", ALU.bypass, replica_groups=[[0, 1, 2, 3], [4, 5, 6, 7]],
                                         ins=[S["ysrc"][c]], outs=[S["ydst"][c]]).then_inc(ccsem.h, 1)
            cx.ninstr += 1
        tk["ydst"].w[ccsem] = ccsem.count
        tk["ysrc"].r[ccsem] = ccsem.count

    xT, x_tk = xT_in, NOTK
    for l in range(depth):
        S = mk_scratch(l)
        if "p1" in stages:
            phase1(l, xT, x_tk, S)
        if "fox" in stages:
            phase_fox(l, S)
        if "s5" in stages:
            phase_s5(l, S)
        if "hg" in stages:
            phase_hg(l, S)
        if "cc" in stages:
            gather_y(l, S)
        if "p3a" in stages:
            phase3a(l, xT, x_tk, S)
        if "p3b" in stages:
            phase3b(l, S)
        if "p3c" in stages:
            phase3c(l, S)
        xT, x_tk = S["x3"], S["tk"]["x3"]
    G.close()
    return nc, cx, dbg_out


def _consts():
    c = np.zeros((128, 6, 128), np.float32)
    p = np.arange(128)
    c[:, 0, :] = np.eye(128)
    c[:, 1, :] = (p[None, :] >= p[:, None])
    c[:, 2, :] = (p[:, None] // 64 == p[None, :] // 64)
    c[:, 3, :] = np.eye(128)[(p + 64) % 128]
    c[:, 4, :] = 1.0
    c[:, 5, :] = (p[:, None] // 64 == p[None, :] // 64) & (p[:, None] % 64 < p[None, :] % 64)
    return c


_SHARED = ("norm_mix", "fox_qnorm", "fox_knorm", "s5_w_glu", "s5_b_glu", "hg_onorm", "w_branch", "w_out",
           "norm_x", "norm_mem", "xq", "xk", "xv", "xo", "x_qnorm", "x_knorm", "norm_ffn", "w_gate_up", "w_down")
_STAGES = ("p1", "fox", "s5", "hg", "cc", "p3a", "p3b", "p3c")


def _local_params(inputs, j):
    f = lambda k: np.asarray(inputs[k], np.float32)
    w = f("w_in")
    sl = lambda a, b: np.arange(a, b)
    cols = np.concatenate([
        sl(128 * j, 128 * j + 128), sl(512 + 128 * j, 512 + 128 * j + 128), sl(1024 + 128 * j, 1024 + 128 * j + 128),
        sl(1536 + 2 * j, 1536 + 2 * j + 2), sl(1544 + 128 * j, 1544 + 128 * j + 128), sl(2056 + 128 * j, 2056 + 128 * j + 128),
        sl(2568 + 128 * j, 2568 + 128 * j + 128), sl(3080 + 128 * j, 3080 + 128 * j + 128), sl(3592 + 128 * j, 3592 + 128 * j + 128),
        sl(4104, 7176)])
    m = {"w_in": np.ascontiguousarray(w[:, :, cols])}
    m["fox_fbias"] = np.ascontiguousarray(f("fox_fbias")[:, 2 * j:2 * j + 2])
    for k in ("s5_a_re", "s5_a_im", "s5_b_re", "s5_b_im", "s5_c_re", "s5_c_im", "s5_log_dt"):
        m[k] = np.ascontiguousarray(f(k)[:, 8 * j:8 * j + 8])
    m["s5_d"] = np.ascontiguousarray(f("s5_d")[:, 128 * j:128 * j + 128])
    m["hg_lb"] = np.ascontiguousarray(f("hg_lb")[:, 128 * j:128 * j + 128])
    return m


def _run(inputs, N, depth):
    x = np.asarray(inputs["x"], np.float32)
    mem = np.asarray(inputs["mem"], np.float32)
    B = x.shape[0]
    nc, cx, _ = build(N, depth, debug=False, stages=_STAGES)
    shared = {k: np.ascontiguousarray(np.asarray(inputs[k], np.float32)) for k in _SHARED}
    shared["consts"] = _consts()
    shared["nvec"] = np.zeros((128, 80), np.float32)
    shared["jv"] = np.tile(np.arange(TT, dtype=np.float32)[None], (128, 1))
    shared["m16"] = (np.arange(128)[:, None] // 16 == np.arange(8)[None, :]).astype(np.float32)
    loc = [_local_params(inputs, j) for j in range(4)]
    maps = []
    for c in range(8):
        b, j = c // 4, c % 4
        m = dict(shared)
        m.update(loc[j])
        m["xT"] = np.ascontiguousarray(x[b, :N].T)
        m["memT"] = np.ascontiguousarray(mem[b].T)
        maps.append(m)
    res = run_bass_kernel_spmd(nc, maps, core_ids=list(range(8)))
    out = np.stack([np.asarray(res.results[4 * b]["outT"]).T for b in range(B)], axis=0)
    return np.ascontiguousarray(out.astype(np.float32))


def kernel(**inputs):
    return _run(inputs, 16384, DEPTH)
```

```python
import contextlib
import numpy as np
import ml_dtypes
import concourse.bass as bass
import concourse.mybir as mybir
from concourse.bass_utils import run_bass_kernel_spmd

F32 = mybir.dt.float32
BF16 = mybir.dt.bfloat16
AF = mybir.ActivationFunctionType
ALU = mybir.AluOpType
AX = mybir.AxisListType

D = 1024
DEPTH = 4
NMEM = 256
DFF = 2816
INC = 7176
EPS = 1e-6
TT = 512
SL = 64
HC = 64


class Sem:
    def __init__(self, h):
        self.h = h
        self.count = 0


class Tk:
    __slots__ = ("w", "r", "dsem")

    def __init__(self):
        self.w = {}
        self.r = {}
        self.dsem = None


class Eng:
    def __init__(self, name, e, sem, is_pe=False):
        self.name, self.e, self.sem, self.is_pe = name, e, sem, is_pe
        self.waited = {}


class Ctx:
    def __init__(self, nc):
        self.nc = nc
        self.st = contextlib.ExitStack()
        self.sem_free = []
        self.nsem = 0
        self.dma_sems = []
        self.pe = Eng("pe", nc.tensor, self._newsem("pe"), True)
        self.act = Eng("act", nc.scalar, self._newsem("act"))
        self.dve = Eng("dve", nc.vector, self._newsem("dve"))
        self.pool = Eng("pool", nc.gpsimd, self._newsem("pool"))
        self.sp = Eng("sp", nc.sync, self._newsem("sp"))
        self.engs = [self.pe, self.act, self.dve, self.pool, self.sp]
        self.ninstr = 0

    def _newsem(self, name):
        self.nsem += 1
        return Sem(self.st.enter_context(self.nc.semaphore(f"s{self.nsem}_{name}")))

    def get_dsem(self):
        if self.sem_free:
            return self.sem_free.pop()
        s = self._newsem("d")
        self.dma_sems.append(s)
        return s

    def _need(self, E, tokens):
        for S, v in tokens.items():
            if v <= 0 or (E.is_pe and S is E.sem) or E.waited.get(S, 0) >= v:
                continue
            E.e.wait_ge(S.h, v)
            E.waited[S] = v
            self.ninstr += 1

    def _deps(self, reads, writes):
        tok = {}
        for t in reads:
            for S, v in t.w.items():
                if tok.get(S, 0) < v:
                    tok[S] = v
        for t in writes:
            for S, v in t.w.items():
                if tok.get(S, 0) < v:
                    tok[S] = v
            for S, v in t.r.items():
                if tok.get(S, 0) < v:
                    tok[S] = v
        return tok

    def _commit(self, S, v, reads, writes):
        for t in writes:
            t.w = {S: v}
            t.r = {}
        for t in reads:
            if t.r.get(S, 0) < v:
                t.r[S] = v

    def op(self, E, fn, reads=(), writes=(), inc=True):
        self._need(E, self._deps(reads, writes))
        ins = fn(E.e)
        self.ninstr += 1
        if inc:
            E.sem.count += 1
            ins.then_inc(E.sem.h, 1)
            self._commit(E.sem, E.sem.count, reads, writes)
        else:
            self._commit(E.sem, E.sem.count + 1, reads, writes)
        return ins

    def dma(self, Q, out, in_, sb_tk, reads=(), writes=(), **kw):
        self._need(Q, self._deps(reads, writes))
        if sb_tk.dsem is None:
            sb_tk.dsem = self.get_dsem()
        S = sb_tk.dsem
        S.count += 16
        Q.e.dma_start(out=out, in_=in_, **kw).then_inc(S.h, 16)
        self.ninstr += 1
        for t in writes:
            if t is sb_tk:
                t.w = {S: S.count}
                t.r = {}
            else:
                t.w[S] = S.count
        for t in reads:
            if t.r.get(S, 0) < S.count:
                t.r[S] = S.count

    def release(self, tks):
        for t in tks:
            if t.dsem is not None:
                self.sem_free.append(t.dsem)
                t.dsem = None

    def barrier(self):
        tok = {E.sem: E.sem.count for E in self.engs}
        for S in self.dma_sems:
            tok[S] = S.count
        for E in self.engs:
            self._need(E, tok)


class Pool_:
    def __init__(self, tiles):
        self.tiles = [(t, Tk()) for t in tiles]
        self.i = 0

    def next(self):
        t = self.tiles[self.i % len(self.tiles)]
        self.i += 1
        return t

    def tks(self):
        return [t[1] for t in self.tiles]


class Phase:
    def __init__(self, cx, name):
        self.cx, self.name = cx, name
        self.st = contextlib.ExitStack()
        self.tks = []
        self.n = 0

    def sb(self, shape, dt):
        self.n += 1
        return self.st.enter_context(self.cx.nc.sbuf_tensor(f"{self.name}_{self.n}", list(shape), dt))

    def sbt(self, shape, dt):
        tk = Tk()
        self.tks.append(tk)
        return self.sb(shape, dt), tk

    def pool(self, n, shape, dt):
        p = Pool_([self.sb(shape, dt) for _ in range(n)])
        self.tks += p.tks()
        return p

    def close(self):
        self.cx.barrier()
        self.cx.release(self.tks)
        self.st.close()


def bc(ap, shape, axis):
    return ap.unsqueeze(axis).to_broadcast(list(shape))


def build(N, depth, debug=False, stages=("p1", "fox")):
    nc = bass.Bass("TRN2", target_bir_lowering=False)
    cx = Ctx(nc)
    NT = N // TT
    NK = N // 128
    NCH = N // HC

    def din(name, shape, dt=F32):
        return nc.dram_tensor(name, list(shape), dt, kind="ExternalInput").ap()

    dbg_out = {}

    def dscr(name, shape, dt):
        if debug:
            t = nc.dram_tensor(name, list(shape), dt, kind="ExternalOutput").ap()
            dbg_out[name] = t
            return t
        return nc.dram_tensor(name, list(shape), dt).ap()

    xT_in = din("xT", [D, N])
    memT_in = din("memT", [D, NMEM])
    consts = din("consts", [128, 6, 128])
    nvec_in = din("nvec", [128, 80])
    jv_in = din("jv", [128, TT])
    m16_in = din("m16", [128, 8])
    P = {}
    pshapes = dict(
        norm_mix=[DEPTH, D], w_in=[DEPTH, D, 4098], fox_fbias=[DEPTH, 2], fox_qnorm=[DEPTH, 64],
        fox_knorm=[DEPTH, 64], s5_a_re=[DEPTH, 8, 64], s5_a_im=[DEPTH, 8, 64],
        s5_b_re=[DEPTH, 8, 64, 16], s5_b_im=[DEPTH, 8, 64, 16], s5_c_re=[DEPTH, 8, 16, 64],
        s5_c_im=[DEPTH, 8, 16, 64], s5_d=[DEPTH, 128], s5_log_dt=[DEPTH, 8],
        s5_w_glu=[DEPTH, 512, 512], s5_b_glu=[DEPTH, 512], hg_lb=[DEPTH, 128], hg_onorm=[DEPTH, 128],
        w_branch=[DEPTH, 1536, D], w_out=[DEPTH, D, D], norm_x=[DEPTH, D], norm_mem=[DEPTH, D],
        xq=[DEPTH, D, D], xk=[DEPTH, D, D], xv=[DEPTH, D, D], xo=[DEPTH, D, D],
        x_qnorm=[DEPTH, 256], x_knorm=[DEPTH, 256], norm_ffn=[DEPTH, D],
        w_gate_up=[DEPTH, D, 2 * DFF], w_down=[DEPTH, DFF, D])
    for k, s in pshapes.items():
        P[k] = din(k, s)
    out_T = nc.dram_tensor("outT", [D, N], F32, kind="ExternalOutput").ap()

    SP, PL, PE, ACT, DVE = cx.sp, cx.pool, cx.pe, cx.act, cx.dve
    NOTK = Tk()

    G = Phase(cx, "g")
    cst_f = G.sb([128, 6, 128], F32)
    cst_b = G.sb([128, 6, 128], BF16)
    nvec = G.sb([128, 80], F32)
    lb_all = G.sb([128, 1, DEPTH], F32)
    c_tk = Tk()
    cx.dma(SP, cst_f[:], consts, c_tk, writes=[c_tk])
    cx.dma(SP, nvec[:], nvec_in, c_tk, writes=[c_tk])
    cx.op(DVE, lambda e: e.tensor_copy(out=cst_b[:], in_=cst_f[:]), reads=[c_tk], writes=[c_tk])
    ident_b, tri_b, blk64_b, ones_b = cst_b[:, 0, :], cst_b[:, 1, :], cst_b[:, 2, :], cst_b[:, 4, :]
    ident_f, swap_f, ones_f = cst_f[:, 0, :], cst_f[:, 3, :], cst_f[:, 4, :]
    psb = [cx.st.enter_context(nc.psum_tensor(f"ps{i}", [128, 512], F32)) for i in range(8)]
    PS = Pool_(psb[2:8])
    PSA = Pool_(psb[0:2])

    with contextlib.nullcontext():
        lbt = G.sb([128, 1, DEPTH], F32)
        lbs = G.sb([128, 1], F32)
        for l_ in range(DEPTH):
            cx.dma(SP, lbt[:, :, l_], P["hg_lb"][l_].rearrange("(h p) -> p h", p=128), c_tk, writes=[c_tk],
                   allow_slow_non_contiguous=True)
        cx.op(ACT, lambda e: e.activation(out=lbt[:], in_=lbt[:], func=AF.Exp), reads=[c_tk], writes=[c_tk])
        cx.op(DVE, lambda e: e.tensor_reduce(out=lbs[:], in_=lbt[:], axis=AX.X, op=ALU.add), reads=[c_tk], writes=[c_tk])
        cx.op(DVE, lambda e: e.reciprocal(out=lbs[:], in_=lbs[:]), reads=[c_tk], writes=[c_tk])
        cx.op(DVE, lambda e: e.tensor_tensor(out=lbt[:], in0=lbt[:], in1=bc(lbs[:], [128, 1, DEPTH], 2), op=ALU.mult),
              reads=[c_tk], writes=[c_tk])
        cx.op(DVE, lambda e: e.memset(lb_all[:, :, 0:1], 0.0), writes=[c_tk])
        for l in range(1, DEPTH):
            cx.op(DVE, lambda e, l=l: e.tensor_tensor(out=lb_all[:, :, l:l + 1], in0=lb_all[:, :, l - 1:l],
                                                     in1=lbt[:, :, l:l + 1], op=ALU.add), reads=[c_tk], writes=[c_tk])

    def mm_group(out_ap, pairs, reads, out_tk):
        n = len(pairs)
        for i, (lt, rh) in enumerate(pairs):
            cx.op(PE, lambda e, lt=lt, rh=rh, i=i: e.matmul(out_ap, lhsT=lt, rhs=rh, start=(i == 0), stop=(i == n - 1)),
                  reads=reads, writes=[out_tk], inc=(i == n - 1))

    def load_w(ph, src_rows_ap, nk, c0, c1, gain=None, stg_pool=None, w_out=None, w_tk=None, wc0=0):
        CH = 512
        for cc in range(c0, c1, CH):
            ce = min(c1, cc + CH)
            for k0 in range(0, nk, 8):
                k1 = min(nk, k0 + 8)
                stg, stk = stg_pool.next()
                cx.dma(SP, stg[:, 0:k1 - k0, 0:ce - cc],
                       src_rows_ap.rearrange("(kt p) c -> p kt c", p=128)[:, k0:k1, cc:ce], stk, writes=[stk])
                for kt in range(k0, k1):
                    dst = w_out[:, kt, wc0 + cc - c0: wc0 + ce - c0]
                    if gain is not None:
                        cx.op(ACT, lambda e, dst=dst, kt=kt, stg=stg, k0=k0, ce=ce, cc=cc: e.activation(
                            out=dst, in_=stg[:, kt - k0, 0:ce - cc], func=AF.Copy, scale=gain[0][:, kt:kt + 1]),
                            reads=[stk, gain[1]], writes=[w_tk])
                    else:
                        cx.op(PL, lambda e, dst=dst, kt=kt, stg=stg, k0=k0, ce=ce, cc=cc: e.tensor_copy(
                            out=dst, in_=stg[:, kt - k0, 0:ce - cc]), reads=[stk], writes=[w_tk])

    def load_vec(ph, src1d, nk, tk):
        t = ph.sb([128, nk], F32)
        cx.dma(SP, t[:], src1d.rearrange("(kt p) -> p kt", p=128), tk, writes=[tk], allow_slow_non_contiguous=True)
        return t

    def rms_rstd(ph, xt, x_tk, nkt, width, sq_pool, rs_pool):
        sq, sqk = sq_pool.next()
        cx.op(ACT, lambda e: e.activation(out=sq[:, 0:nkt, 0:width], in_=xt, func=AF.Square), reads=[x_tk], writes=[sqk])
        pb, pk = PS.next()
        mm_group(pb[:, 0:width], [(ones_b, sq[:, k, 0:width]) for k in range(nkt)], [sqk, c_tk], pk)
        rs, rk = rs_pool.next()
        cx.op(ACT, lambda e: e.activation(out=rs[:, 0:width], in_=pb[:, 0:width], func=AF.Sqrt, scale=1.0 / (nkt * 128), bias=eps_t[:, 0:1]),
              reads=[pk, c_tk], writes=[rk])
        cx.op(DVE, lambda e: e.reciprocal(out=rs[:, 0:width], in_=rs[:, 0:width]), reads=[rk], writes=[rk])
        return rs, rk

    eps_t = G.sb([128, 1], F32)
    cx.op(DVE, lambda e: e.memset(eps_t[:], EPS), writes=[c_tk])

    def mk_scratch(l):
        S = {}
        S["qT"] = dscr(f"qT{l}", [128, N], BF16)
        S["kT"] = dscr(f"kT{l}", [128, N], BF16)
        S["vP"] = dscr(f"vP{l}", [2, 128, NK, 64], BF16)
        S["lfT"] = dscr(f"lfT{l}", [2, N], F32)
        S["aug"] = dscr(f"aug{l}", [2, 6, 2, N], BF16)
        S["uT"] = dscr(f"uT{l}", [128, N], BF16)
        S["hqT"] = dscr(f"hqT{l}", [128, N], BF16)
        S["lfhT"] = dscr(f"lfhT{l}", [128, N], F32)
        S["khT"] = dscr(f"khT{l}", [128, N], BF16)
        S["hvP"] = dscr(f"hvP{l}", [1, 64, NCH, 128], BF16)
        S["hgT"] = dscr(f"hgT{l}", [128, N], BF16)
        S["ysrc"] = nc.dram_tensor(f"ysrc{l}", [N // 256, 384, 256], BF16).ap()
        S["ydst"] = nc.dram_tensor(f"ydst{l}", [N // 256, 1536, 256], BF16).ap()
        S["x1"] = dscr(f"x1_{l}", [D, N], F32)
        S["x2"] = dscr(f"x2_{l}", [D, N], F32)
        S["x3"] = out_T if l == depth - 1 else dscr(f"x3_{l}", [D, N], F32)
        S["tk"] = {k: Tk() for k in list(S.keys())}
        return S

    def phase1(l, xT, x_tk, S):
        ph = Phase(cx, f"p1_{l}")
        tk = S["tk"]
        W, w_tk = ph.sbt([128, 8, 1026], BF16)
        stg_pool = ph.pool(2, [128, 8, 512], F32)
        g_tk = Tk()
        ph.tks.append(g_tk)
        g1 = load_vec(ph, P["norm_mix"][l], 8, g_tk)
        load_w(ph, P["w_in"][l], 8, 0, 1026, gain=(g1, g_tk), stg_pool=stg_pool, w_out=W, w_tk=w_tk)
        gqk = ph.sb([128, 2], F32)
        for hf in range(2):
            cx.dma(SP, gqk[hf * 64:(hf + 1) * 64, 0:1], P["fox_qnorm"][l].rearrange("(p o) -> p o", o=1), g_tk, writes=[g_tk])
            cx.dma(SP, gqk[hf * 64:(hf + 1) * 64, 1:2], P["fox_knorm"][l].rearrange("(p o) -> p o", o=1), g_tk, writes=[g_tk])
        cx.op(DVE, lambda e: e.tensor_scalar(out=gqk[:, 0:1], in0=gqk[:, 0:1], scalar1=0.125, scalar2=None, op0=ALU.mult),
              reads=[g_tk], writes=[g_tk])
        nfb = ph.sb([2, 1], F32)
        cx.dma(SP, nfb[:], P["fox_fbias"][l].rearrange("(p o) -> p o", o=1), g_tk, writes=[g_tk])
        cx.op(DVE, lambda e: e.tensor_scalar(out=nfb[:], in0=nfb[:], scalar1=-1.0, scalar2=None, op0=ALU.mult), reads=[g_tk], writes=[g_tk])
        oml = ph.sb([128, 1], F32)
        noml = ph.sb([128, 1], F32)
        cx.op(DVE, lambda e: e.tensor_scalar(out=oml[:], in0=lb_all[:, :, l], scalar1=-1.0, scalar2=1.0, op0=ALU.mult, op1=ALU.add),
              reads=[c_tk], writes=[g_tk])
        cx.op(DVE, lambda e: e.tensor_scalar(out=noml[:], in0=oml[:], scalar1=-1.0, scalar2=None, op0=ALU.mult), reads=[g_tk], writes=[g_tk])

        x_pool = ph.pool(2, [128, 8, TT], F32)
        sq_pool = ph.pool(1, [128, 8, TT], BF16)
        h_pool = ph.pool(2, [128, 8, TT], BF16)
        rs_pool = ph.pool(2, [128, TT], F32)
        evb = ph.pool(4, [128, TT], BF16)
        evf = ph.pool(3, [128, TT], F32)
        xr = xT.rearrange("(kt p) n -> p kt n", p=128)

        def ld(i):
            xt, xk = x_pool.next()
            cx.dma(SP, xt[:], xr[:, :, i * TT:(i + 1) * TT], xk, reads=[x_tk], writes=[xk])
            return xt, xk

        nxt = ld(0)
        for i in range(NT):
            xt, xk = nxt
            if i + 1 < NT:
                nxt = ld(i + 1)
            sl = slice(i * TT, (i + 1) * TT)
            rs, rk = rms_rstd(ph, xt[:], xk, 8, TT, sq_pool, rs_pool)
            h, hk = h_pool.next()
            cx.op(DVE, lambda e: e.tensor_tensor(out=h[:], in0=xt[:], in1=bc(rs[:], [128, 8, TT], 1), op=ALU.mult),
                  reads=[xk, rk], writes=[hk])

            def proj(c0, m):
                pb, pk = PS.next()
                mm_group(pb[0:m, :], [(W[:, k, c0:c0 + m], h[:, k, :]) for k in range(8)], [w_tk, hk], pk)
                return pb, pk

            for ct in range(2):
                pb, pk = proj(ct * 128, 128)
                sqh, sk = evb.next()
                cx.op(ACT, lambda e: e.activation(out=sqh[:], in_=pb[:], func=AF.Square), reads=[pk], writes=[sk])
                pb2, pk2 = PS.next()
                mm_group(pb2[:], [(blk64_b, sqh[:])], [sk, c_tk], pk2)
                rh, rhk = evf.next()
                cx.op(ACT, lambda e: e.activation(out=rh[:], in_=pb2[:], func=AF.Sqrt, scale=1.0 / 64, bias=eps_t[:, 0:1]),
                      reads=[pk2, c_tk], writes=[rhk])
                cx.op(DVE, lambda e: e.reciprocal(out=rh[:], in_=rh[:]), reads=[rhk], writes=[rhk])
                qn, qk_ = evb.next()
                gcol = gqk[:, 0:1] if ct < 1 else gqk[:, 1:2]
                cx.op(DVE, lambda e: e.scalar_tensor_tensor(out=qn[:], in0=pb[:], scalar=gcol, in1=rh[:], op0=ALU.mult, op1=ALU.mult),
                      reads=[pk, rhk, g_tk], writes=[qk_])
                dst, dk = (S["qT"], tk["qT"]) if ct < 1 else (S["kT"], tk["kT"])
                r0 = 0
                cx.dma(PL, dst[r0:r0 + 128, sl], qn[:], qk_, reads=[qk_], writes=[dk])
            pb, pk = proj(384, 2)
            ef, ek = evf.next()
            cx.op(ACT, lambda e: e.activation(out=ef[0:2, :], in_=pb[0:2, :], func=AF.Exp, scale=-1.0, bias=nfb[:, 0:1]),
                  reads=[pk, g_tk], writes=[ek])
            cx.op(ACT, lambda e: e.activation(out=ef[0:2, :], in_=ef[0:2, :], func=AF.Ln, bias=1.0), reads=[ek], writes=[ek])
            cx.op(DVE, lambda e: e.tensor_scalar(out=ef[0:2, :], in0=ef[0:2, :], scalar1=-1.0, scalar2=None, op0=ALU.mult),
                  reads=[ek], writes=[ek])
            cx.dma(PL, S["lfT"][:, sl], ef[0:2, :], ek, reads=[ek], writes=[tk["lfT"]])
            for (c0, name, fn) in ((386, "uT", AF.Copy), (514, "hqT", AF.Silu), (898, "hgT", AF.Silu)):
                for ct in range(1):
                    pb, pk = proj(c0 + ct * 128, 128)
                    ob, ok = evb.next()
                    cx.op(ACT, lambda e, fn=fn: e.activation(out=ob[:], in_=pb[:], func=fn), reads=[pk], writes=[ok])
                    cx.dma(PL, S[name][ct * 128:(ct + 1) * 128, sl], ob[:], ok, reads=[ok], writes=[tk[name]])
            for ct in range(1):
                pb, pk = proj(642 + ct * 128, 128)
                sg, sgk = evf.next()
                cx.op(ACT, lambda e: e.activation(out=sg[:], in_=pb[:], func=AF.Sigmoid), reads=[pk], writes=[sgk])
                lf, lfk = evf.next()
                cx.op(ACT, lambda e, ct=ct: e.activation(out=lf[:], in_=sg[:], func=AF.Ln, scale=oml[:, ct:ct + 1], bias=lb_all[:, ct, l:l + 1]),
                      reads=[sgk, g_tk, c_tk], writes=[lfk])
                cx.dma(PL, S["lfhT"][ct * 128:(ct + 1) * 128, sl], lf[:], lfk, reads=[lfk], writes=[tk["lfhT"]])
                kh, khk = evb.next()
                cx.op(DVE, lambda e, ct=ct: e.tensor_scalar(out=kh[:], in0=sg[:], scalar1=noml[:, ct:ct + 1], scalar2=oml[:, ct:ct + 1],
                                                           op0=ALU.mult, op1=ALU.add), reads=[sgk, g_tk], writes=[khk])
                cx.dma(PL, S["khT"][ct * 128:(ct + 1) * 128, sl], kh[:], khk, reads=[khk], writes=[tk["khT"]])
            for ts in range(4):
                ktg = i * 4 + ts
                for which in range(2):
                    c0 = 256 if which == 0 else 770
                    pb, pk = PS.next()
                    mm_group(pb[:, 0:128], [(h[:, k, ts * 128:(ts + 1) * 128], W[:, k, c0:c0 + 128]) for k in range(8)], [w_tk, hk], pk)
                    ob, ok = evb.next()
                    cx.op(ACT, lambda e: e.activation(out=ob[:, 0:128], in_=pb[:, 0:128], func=AF.Copy), reads=[pk], writes=[ok])
                    if which == 0:
                        cx.dma(PL, S["vP"][:, :, ktg, :].rearrange("h p d -> p h d"), ob[:, 0:128].rearrange("p (h d) -> p h d", h=2),
                               ok, reads=[ok], writes=[tk["vP"]])
                    else:
                        for hf in range(2):
                            cx.dma(PL, S["hvP"][:, :, ktg * 2 + hf, :].rearrange("h p d -> p h d"),
                                   ob[hf * 64:(hf + 1) * 64, 0:128].rearrange("p (h d) -> p h d", h=1), ok, reads=[ok], writes=[tk["hvP"]])
        ph.close()

    def phase_fox(l, S):
        ph = Phase(cx, f"fx_{l}")
        tk = S["tk"]
        SEG = N // 64
        lf, lk = ph.sbt([128, SEG], F32)
        cc, ck = ph.sbt([128, SEG], F32)
        r1, r1k = ph.sbt([128, SEG], F32)
        spl = ph.sb([128, 2, 3, SEG], BF16)
        onb = ph.sb([128, SEG], BF16)
        sk_ = Tk(); ph.tks.append(sk_)
        cx.dma(SP, lf[:], S["lfT"].rearrange("h (s j) -> (h s) j", s=64), lk, reads=[tk["lfT"]], writes=[lk])
        cx.op(DVE, lambda e: e.memset(r1[:], 1.0), writes=[r1k])
        cx.op(DVE, lambda e: e.memset(onb[:], 1.0), writes=[sk_])
        cx.op(DVE, lambda e: e.tensor_tensor_scan(out=cc[:], data0=r1[:], data1=lf[:], initial=0.0, op0=ALU.mult, op1=ALU.add),
              reads=[lk, r1k], writes=[ck])
        pb, pk = PS.next()
        mm_group(pb[:, 0:2], [(cst_f[:, 5, :], cc[:, SEG - 2:SEG])], [ck, c_tk], pk)
        off = ph.sb([128, 1], F32)
        cx.op(DVE, lambda e: e.tensor_copy(out=off[:], in_=pb[:, 1:2]), reads=[pk], writes=[lk])
        cx.op(DVE, lambda e: e.tensor_scalar(out=cc[:], in0=cc[:], scalar1=off[:, 0:1], scalar2=None, op0=ALU.add),
              reads=[ck, lk], writes=[ck])
        cx.op(DVE, lambda e: e.tensor_copy(out=spl[:, 0, 0, :], in_=cc[:]), reads=[ck], writes=[sk_])
        cx.op(DVE, lambda e: e.tensor_tensor(out=r1[:], in0=cc[:], in1=spl[:, 0, 0, :], op=ALU.subtract), reads=[ck, sk_], writes=[r1k])
        cx.op(DVE, lambda e: e.tensor_copy(out=spl[:, 0, 1, :], in_=r1[:]), reads=[r1k], writes=[sk_])
        cx.op(DVE, lambda e: e.tensor_tensor(out=r1[:], in0=r1[:], in1=spl[:, 0, 1, :], op=ALU.subtract), reads=[r1k, sk_], writes=[r1k])
        cx.op(DVE, lambda e: e.tensor_copy(out=spl[:, 0, 2, :], in_=r1[:]), reads=[r1k], writes=[sk_])
        cx.op(DVE, lambda e: e.tensor_scalar(out=spl[:, 1, :, :], in0=spl[:, 0, :, :], scalar1=-1.0, scalar2=None, op0=ALU.mult),
              reads=[sk_], writes=[sk_])
        for j in range(3):
            for (side, row, src) in ((0, j, onb[:]), (0, 3 + j, spl[:, 1, j, :]), (1, j, spl[:, 0, j, :]), (1, 3 + j, onb[:])):
                cx.dma(PL, S["aug"][side, row].rearrange("h (s j) -> (h s) j", s=64), src, sk_, reads=[sk_], writes=[tk["aug"]])

        QT = ph.pool(1, [70, N], BF16)
        KT = ph.pool(1, [70, N], BF16)
        VP = ph.pool(1, [128, NK, 65], BF16)
        for (vt, vk) in VP.tiles:
            cx.op(PL, lambda e, vt=vt: e.memset(vt[:, :, 64:65], 1.0), writes=[vk])
        pt_pool = ph.pool(3, [128, TT], BF16)
        rr_pool = ph.pool(2, [65, TT], F32)
        bc_pool = ph.pool(2, [64, TT], F32)
        y_pool = ph.pool(2, [64, TT], BF16)

        def ldh(hd):
            q, qk = QT.next(); k, kk = KT.next(); v, vk = VP.next()
            cx.dma(SP, q[0:64, :], S["qT"][hd * 64:(hd + 1) * 64, :], qk, reads=[tk["qT"]], writes=[qk])
            cx.dma(SP, q[64:70, :], S["aug"][1, :, hd, :], qk, reads=[tk["aug"]], writes=[qk])
            cx.dma(SP, k[0:64, :], S["kT"][hd * 64:(hd + 1) * 64, :], kk, reads=[tk["kT"]], writes=[kk])
            cx.dma(SP, k[64:70, :], S["aug"][0, :, hd, :], kk, reads=[tk["aug"]], writes=[kk])
            cx.dma(SP, v[:, :, 0:64], S["vP"][hd], vk, reads=[tk["vP"]], writes=[vk])
            return (q, qk, k, kk, v, vk)

        for hd in range(2):
            q, qk, k, kk, v, vk = ldh(hd)
            for qt in range(NT):
                ob, ok = PSA.next()
                nkb = 4 * qt + 4
                for kb in range(nkb):
                    r = kb - 4 * qt
                    f0 = 128 * r if r > 0 else 0
                    w = TT - f0
                    sb_, sk2 = PS.next()
                    mm_group(sb_[:, 0:w], [(k[0:70, kb * 128:(kb + 1) * 128], q[0:70, qt * TT + f0:(qt + 1) * TT])], [qk, kk], sk2)
                    pt, ptk = pt_pool.next()
                    cx.op(ACT, lambda e, pt=pt, sb_=sb_, w=w: e.activation(out=pt[:, 0:w], in_=sb_[:, 0:w], func=AF.Exp), reads=[sk2], writes=[ptk])
                    if r >= 0:
                        cx.op(DVE, lambda e, pt=pt: e.tensor_tensor(out=pt[:, 0:128], in0=pt[:, 0:128], in1=tri_b, op=ALU.mult),
                              reads=[ptk, c_tk], writes=[ptk])
                    cx.op(PE, lambda e, pt=pt, w=w, f0=f0, kb=kb, ob=ob, v=v, nkb=nkb: e.matmul(
                        ob[0:65, f0:TT], lhsT=v[:, kb, 0:65], rhs=pt[:, 0:w], start=(kb == 0), stop=(kb == nkb - 1)),
                        reads=[ptk, vk], writes=[ok], inc=True)
                rr, rrk = rr_pool.next()
                cx.op(DVE, lambda e: e.reciprocal(out=rr[64:65, :], in_=ob[64:65, :]), reads=[ok], writes=[rrk])
                bb, bk = PS.next()
                mm_group(bb[0:64, :], [(ones_f[64:65, 0:64], rr[64:65, :])], [rrk, c_tk], bk)
                bs, bsk = bc_pool.next()
                cx.op(ACT, lambda e: e.activation(out=bs[:], in_=bb[0:64, :], func=AF.Copy), reads=[bk], writes=[bsk])
                y, yk = y_pool.next()
                cx.op(DVE, lambda e: e.tensor_tensor(out=y[:], in0=ob[0:64, :], in1=bs[:], op=ALU.mult), reads=[ok, bsk], writes=[yk])
                cx.dma(PL, S["ysrc"][2 * qt:2 * qt + 2, hd * 64:(hd + 1) * 64, :].rearrange("c p n -> p c n"), y[:].rearrange("p (c n) -> p c n", c=2), yk, reads=[yk], writes=[tk["ysrc"]])
        ph.close()

    PI = float(np.pi)

    def phase_s5(l, S):
        ph = Phase(cx, f"s5_{l}")
        tk = S["tk"]
        t_tk = Tk(); ph.tks.append(t_tk)
        tmpi = ph.sb([128, TT], mybir.dt.int32)

        def red_pi(x, w):
            kf = tmpf[:, 0:w]
            cx.op(DVE, lambda e: e.tensor_scalar(out=kf, in0=x, scalar1=1.0 / (2 * PI), scalar2=0.5, op0=ALU.mult, op1=ALU.add),
                  reads=[t_tk], writes=[t_tk])
            cx.op(DVE, lambda e: e.tensor_copy(out=tmpi[:, 0:w], in_=kf), reads=[t_tk], writes=[t_tk])
            cx.op(DVE, lambda e: e.tensor_copy(out=kf, in_=tmpi[:, 0:w]), reads=[t_tk], writes=[t_tk])
            cx.op(DVE, lambda e: e.scalar_tensor_tensor(out=x, in0=kf, scalar=-6.28125, in1=x, op0=ALU.mult, op1=ALU.add),
                  reads=[t_tk], writes=[t_tk])
            cx.op(DVE, lambda e: e.scalar_tensor_tensor(out=x, in0=kf, scalar=-0.0019353071795864769, in1=x, op0=ALU.mult, op1=ALU.add),
                  reads=[t_tk], writes=[t_tk])
            fix_pi(x, w)

        def fix_pi(x, w):
            m = tmpf[:, 0:w]
            cx.op(DVE, lambda e: e.tensor_scalar(out=m, in0=x, scalar1=-PI, scalar2=None, op0=ALU.is_lt), reads=[t_tk], writes=[t_tk])
            cx.op(DVE, lambda e: e.scalar_tensor_tensor(out=x, in0=m, scalar=2 * PI, in1=x, op0=ALU.mult, op1=ALU.add), reads=[t_tk], writes=[t_tk])
            cx.op(DVE, lambda e: e.tensor_scalar(out=m, in0=x, scalar1=PI, scalar2=None, op0=ALU.is_gt), reads=[t_tk], writes=[t_tk])
            cx.op(DVE, lambda e: e.scalar_tensor_tensor(out=x, in0=m, scalar=-2 * PI, in1=x, op0=ALU.mult, op1=ALU.add), reads=[t_tk], writes=[t_tk])
            cx.op(DVE, lambda e: e.tensor_scalar(out=x, in0=x, scalar1=PI, scalar2=-PI, op0=ALU.min, op1=ALU.max), reads=[t_tk], writes=[t_tk])

        def sincos(x, w, sin_out, cos_out):
            red_pi(x, w)
            cx.op(ACT, lambda e: e.activation(out=sin_out, in_=x, func=AF.Sin), reads=[t_tk], writes=[t_tk])
            cx.op(DVE, lambda e: e.tensor_scalar(out=x, in0=x, scalar1=PI / 2, scalar2=None, op0=ALU.add), reads=[t_tk], writes=[t_tk])
            fix_pi(x, w)
            cx.op(ACT, lambda e: e.activation(out=cos_out, in_=x, func=AF.Sin), reads=[t_tk], writes=[t_tk])

        tmpf = ph.sb([128, TT], F32)
        ang = ph.sb([128, TT], F32)
        jv = ph.sb([128, TT], F32)
        m16 = ph.sb([128, 8], F32)
        cx.dma(SP, jv[:], jv_in, t_tk, writes=[t_tk])
        cx.dma(SP, m16[:], m16_in, t_tk, writes=[t_tk])
        anat = ph.sb([8, 2, 128], F32)
        for hf in range(2):
            cx.dma(SP, anat[:, 0, hf * 64:(hf + 1) * 64], P["s5_a_re"][l], t_tk, writes=[t_tk])
            cx.dma(SP, anat[:, 1, hf * 64:(hf + 1) * 64], P["s5_a_im"][l], t_tk, writes=[t_tk])
        dt_pr = ph.sb([128, 8], F32)
        cx.dma(SP, dt_pr[:], P["s5_log_dt"][l:l + 1, :].partition_broadcast(128).rearrange("p o g -> p (o g)"), t_tk, writes=[t_tk])
        cx.op(ACT, lambda e: e.activation(out=dt_pr[:], in_=dt_pr[:], func=AF.Exp), reads=[t_tk], writes=[t_tk])
        mag_pr = ph.sb([128, 8], F32)
        th_pr = ph.sb([128, 8], F32)
        c512 = ph.sb([128, 8], F32)
        s512 = ph.sb([128, 8], F32)
        for which, dst in ((0, mag_pr), (1, th_pr)):
            pb, pk = PS.next()
            cx.op(PE, lambda e, which=which: e.transpose(pb[:, 0:8], anat[:, which, :], ident_f[0:8, 0:8]), reads=[t_tk, c_tk], writes=[pk])
            cx.op(DVE, lambda e, dst=dst: e.tensor_tensor(out=dst[:], in0=pb[:, 0:8], in1=dt_pr[:], op=ALU.mult), reads=[pk, t_tk], writes=[t_tk])
        cx.op(ACT, lambda e: e.activation(out=mag_pr[:], in_=mag_pr[:], func=AF.Exp), reads=[t_tk], writes=[t_tk])
        cx.op(DVE, lambda e: e.tensor_scalar(out=ang[:, 0:8], in0=th_pr[:], scalar1=float(TT), scalar2=None, op0=ALU.mult), reads=[t_tk], writes=[t_tk])
        sincos(ang[:, 0:8], 8, s512[:], c512[:])
        cosT = ph.sb([128, 8, TT], BF16)
        sinT = ph.sb([128, 8, TT], BF16)
        for g in range(8):
            cx.op(DVE, lambda e, g=g: e.tensor_scalar(out=ang[:], in0=jv[:], scalar1=th_pr[:, g:g + 1], scalar2=None, op0=ALU.mult),
                  reads=[t_tk], writes=[t_tk])
            sincos(ang[:], TT, sinT[:, g, :], cosT[:, g, :])
        Bst = ph.sb([128, 8, 128], BF16)
        Bsw = ph.sb([128, 8, 128], BF16)
        Cg = ph.sb([128, 8, 128], BF16)
        dsk = load_vec(ph, P["s5_d"][l], 1, t_tk)
        cx.op(PL, lambda e: e.memset(Cg[:], 0.0), writes=[t_tk])
        sm = [ph.sb([128, 64], F32) for _ in range(12)]
        ar_, ai_, mg_, th_, sn_, cs_, nr_, ni_, dn_, cr_, ci_, t_ = sm
        dt_gh = ph.sb([128, 1], F32)
        bnat = ph.sb([64, 2, 128], F32)
        bgh = ph.sb([128, 2, 64], F32)
        ball = ph.sb([128, 2, 128], F32)
        cnat = ph.sb([128, 128], F32)
        call = ph.sb([128, 128], F32)
        for Q in range(1):
            G0 = Q * 8
            for g in range(8):
                cx.dma(SP, ar_[g * 16:(g + 1) * 16, :], P["s5_a_re"][l, G0 + g:G0 + g + 1, :].partition_broadcast(16).rearrange("p o g -> p (o g)"), t_tk, writes=[t_tk])
                cx.dma(SP, ai_[g * 16:(g + 1) * 16, :], P["s5_a_im"][l, G0 + g:G0 + g + 1, :].partition_broadcast(16).rearrange("p o g -> p (o g)"), t_tk, writes=[t_tk])
                cx.dma(SP, dt_gh[g * 16:(g + 1) * 16, :], P["s5_log_dt"][l:l + 1, G0 + g:G0 + g + 1].partition_broadcast(16).rearrange("p o g -> p (o g)"), t_tk, writes=[t_tk])
            cx.op(ACT, lambda e: e.activation(out=dt_gh[:], in_=dt_gh[:], func=AF.Exp), reads=[t_tk], writes=[t_tk])
            cx.op(ACT, lambda e: e.activation(out=mg_[:], in_=ar_[:], func=AF.Exp, scale=dt_gh[:, 0:1]), reads=[t_tk], writes=[t_tk])
            cx.op(DVE, lambda e: e.tensor_scalar(out=th_[:], in0=ai_[:], scalar1=dt_gh[:, 0:1], scalar2=None, op0=ALU.mult), reads=[t_tk], writes=[t_tk])
            sincos(th_[:], 64, sn_[:], cs_[:])
            TTo = lambda o, a, b, op: cx.op(DVE, lambda e: e.tensor_tensor(out=o, in0=a, in1=b, op=op), reads=[t_tk], writes=[t_tk])
            TTo(nr_[:], mg_[:], cs_[:], ALU.mult)
            cx.op(DVE, lambda e: e.tensor_scalar(out=nr_[:], in0=nr_[:], scalar1=-1.0, scalar2=None, op0=ALU.add), reads=[t_tk], writes=[t_tk])
            TTo(ni_[:], mg_[:], sn_[:], ALU.mult)
            TTo(dn_[:], ar_[:], ar_[:], ALU.mult)
            TTo(t_[:], ai_[:], ai_[:], ALU.mult)
            TTo(dn_[:], dn_[:], t_[:], ALU.add)
            cx.op(DVE, lambda e: e.reciprocal(out=dn_[:], in_=dn_[:]), reads=[t_tk], writes=[t_tk])
            TTo(cr_[:], nr_[:], ar_[:], ALU.mult)
            TTo(t_[:], ni_[:], ai_[:], ALU.mult)
            TTo(cr_[:], cr_[:], t_[:], ALU.add)
            TTo(cr_[:], cr_[:], dn_[:], ALU.mult)
            TTo(ci_[:], ni_[:], ar_[:], ALU.mult)
            TTo(t_[:], nr_[:], ai_[:], ALU.mult)
            TTo(ci_[:], ci_[:], t_[:], ALU.subtract)
            TTo(ci_[:], ci_[:], dn_[:], ALU.mult)
            cx.dma(SP, bnat[:, 0, :].rearrange("p (g h) -> p g h", h=16), P["s5_b_re"][l, G0:G0 + 8].rearrange("g p h -> p g h"), t_tk, writes=[t_tk])
            cx.dma(SP, bnat[:, 1, :].rearrange("p (g h) -> p g h", h=16), P["s5_b_im"][l, G0:G0 + 8].rearrange("g p h -> p g h"), t_tk, writes=[t_tk])
            for ri in range(2):
                pb, pk = PS.next()
                cx.op(PE, lambda e, ri=ri, pb=pb: e.transpose(pb[:, 0:64], bnat[:, ri, :], ident_f[0:64, 0:64]), reads=[t_tk, c_tk], writes=[pk])
                cx.op(DVE, lambda e, ri=ri, pb=pb: e.tensor_copy(out=bgh[:, ri, :], in_=pb[:, 0:64]), reads=[pk], writes=[t_tk])
            TTo(ball[:, 0, 0:64], cr_[:], bgh[:, 0, :], ALU.mult)
            TTo(t_[:], ci_[:], bgh[:, 1, :], ALU.mult)
            TTo(ball[:, 0, 0:64], ball[:, 0, 0:64], t_[:], ALU.subtract)
            TTo(ball[:, 0, 64:128], cr_[:], bgh[:, 1, :], ALU.mult)
            TTo(t_[:], ci_[:], bgh[:, 0, :], ALU.mult)
            TTo(ball[:, 0, 64:128], ball[:, 0, 64:128], t_[:], ALU.add)
            cx.op(DVE, lambda e: e.tensor_copy(out=ball[:, 1, 0:64], in_=ball[:, 0, 64:128]), reads=[t_tk], writes=[t_tk])
            cx.op(DVE, lambda e: e.tensor_scalar(out=ball[:, 1, 64:128], in0=ball[:, 0, 0:64], scalar1=-1.0, scalar2=None, op0=ALU.mult),
                  reads=[t_tk], writes=[t_tk])
            for g in range(8):
                cx.op(DVE, lambda e, g=g: e.tensor_scalar(out=Bst[:, G0 + g, :], in0=ball[:, 0, :], scalar1=m16[:, g:g + 1], scalar2=None, op0=ALU.mult),
                      reads=[t_tk], writes=[t_tk])
                cx.op(DVE, lambda e, g=g: e.tensor_scalar(out=Bsw[:, G0 + g, :], in0=ball[:, 1, :], scalar1=m16[:, g:g + 1], scalar2=None, op0=ALU.mult),
                      reads=[t_tk], writes=[t_tk])
            cx.dma(SP, cnat[:, 0:64], P["s5_c_re"][l, G0:G0 + 8].rearrange("g h p -> (g h) p"), t_tk, writes=[t_tk])
            cx.dma(SP, cnat[:, 64:128], P["s5_c_im"][l, G0:G0 + 8].rearrange("g h p -> (g h) p"), t_tk, writes=[t_tk])
            pb, pk = PS.next()
            cx.op(PE, lambda e, pb=pb: e.transpose(pb[:, 0:128], cnat[:], ident_f), reads=[t_tk, c_tk], writes=[pk])
            cx.op(DVE, lambda e, pb=pb: e.tensor_copy(out=call[0:64, :], in_=pb[0:64, 0:128]), reads=[pk], writes=[t_tk])
            cx.op(DVE, lambda e, pb=pb: e.tensor_scalar(out=call[64:128, :], in0=pb[64:128, 0:128], scalar1=-1.0, scalar2=None, op0=ALU.mult),
                  reads=[pk], writes=[t_tk])
            for g in range(8):
                cx.op(DVE, lambda e, g=g: e.tensor_copy(out=Cg[:, G0 + g, g * 16:(g + 1) * 16], in_=call[:, g * 16:(g + 1) * 16]),
                      reads=[t_tk], writes=[t_tk])
        vin = ph.sb([128, 8], F32)
        vsin = ph.sb([128, 8], F32)
        cx.op(DVE, lambda e: e.memset(vin[:], 0.0), writes=[t_tk])
        cx.op(DVE, lambda e: e.memset(vsin[:], 0.0), writes=[t_tk])
        st_tk = Tk(); ph.tks.append(st_tk)
        u_pool = ph.pool(2, [128, 1, TT], BF16)
        f_pool = ph.pool(6, [128, TT], F32)
        v_pool = ph.pool(2, [128, 2, TT], F32)
        h_pool = ph.pool(2, [128, TT], BF16)
        c_pool = ph.pool(2, [128, 2], F32)
        y_pool = ph.pool(2, [128, TT], F32)
        z_pool = ph.pool(2, [128, TT], BF16)
        ur = S["uT"].rearrange("(q p) n -> p q n", p=128)

        def ld(i):
            u, uk = u_pool.next()
            cx.dma(SP, u[:], ur[:, :, i * TT:(i + 1) * TT], uk, reads=[tk["uT"]], writes=[uk])
            return u, uk

        nxt = ld(0)
        for i in range(NT):
            u, uk = nxt
            if i + 1 < NT:
                nxt = ld(i + 1)
            for Q in range(1):
                py, pyk = PSA.next()
                for g in range(8):
                    G_ = Q * 8 + g
                    pd, pdk = PS.next()
                    mm_group(pd[:], [(Bst[:, G_, :], u[:, Q, :])], [t_tk, uk], pdk)
                    pds, pdsk = PS.next()
                    mm_group(pds[:], [(Bsw[:, G_, :], u[:, Q, :])], [t_tk, uk], pdsk)
                    t1, t1k = f_pool.next(); t2, t2k = f_pool.next()
                    cx.op(DVE, lambda e: e.tensor_tensor(out=t1[:], in0=pd[:], in1=cosT[:, G_, :], op=ALU.mult), reads=[pdk, t_tk], writes=[t1k])
                    cx.op(DVE, lambda e: e.tensor_tensor(out=t2[:], in0=pds[:], in1=sinT[:, G_, :], op=ALU.mult), reads=[pdsk, t_tk], writes=[t2k])
                    cx.op(DVE, lambda e: e.tensor_tensor(out=t1[:], in0=t1[:], in1=t2[:], op=ALU.add), reads=[t1k, t2k], writes=[t1k])
                    t3, t3k = f_pool.next(); t4, t4k = f_pool.next()
                    cx.op(DVE, lambda e: e.tensor_tensor(out=t3[:], in0=pds[:], in1=cosT[:, G_, :], op=ALU.mult), reads=[pdsk, t_tk], writes=[t3k])
                    cx.op(DVE, lambda e: e.tensor_tensor(out=t4[:], in0=pd[:], in1=sinT[:, G_, :], op=ALU.mult), reads=[pdk, t_tk], writes=[t4k])
                    cx.op(DVE, lambda e: e.tensor_tensor(out=t3[:], in0=t3[:], in1=t4[:], op=ALU.subtract), reads=[t3k, t4k], writes=[t3k])
                    vv, vvk = v_pool.next()
                    magb = mag_pr[:, G_:G_ + 1].to_broadcast([128, TT])
                    cx.op(DVE, lambda e: e.tensor_tensor_scan(out=vv[:, 0, :], data0=magb, data1=t1[:], initial=vin[:, G_:G_ + 1], op0=ALU.mult, op1=ALU.add),
                          reads=[t1k, t_tk, st_tk], writes=[vvk])
                    cx.op(DVE, lambda e: e.tensor_tensor_scan(out=vv[:, 1, :], data0=magb, data1=t3[:], initial=vsin[:, G_:G_ + 1], op0=ALU.mult, op1=ALU.add),
                          reads=[t3k, t_tk, st_tk], writes=[vvk])
                    cc_, cck = c_pool.next()
                    cx.op(DVE, lambda e: e.tensor_scalar(out=cc_[:, 0:1], in0=vv[:, 1, TT - 1:TT], scalar1=s512[:, G_:G_ + 1], scalar2=None, op0=ALU.mult),
                          reads=[vvk, t_tk], writes=[cck])
                    cx.op(DVE, lambda e: e.tensor_scalar(out=cc_[:, 1:2], in0=vv[:, 0, TT - 1:TT], scalar1=s512[:, G_:G_ + 1], scalar2=None, op0=ALU.mult),
                          reads=[vvk, t_tk], writes=[cck])
                    cx.op(DVE, lambda e: e.scalar_tensor_tensor(out=vin[:, G_:G_ + 1], in0=vv[:, 0, TT - 1:TT], scalar=c512[:, G_:G_ + 1], in1=cc_[:, 0:1],
                                                               op0=ALU.mult, op1=ALU.subtract), reads=[vvk, cck, t_tk], writes=[st_tk])
                    cx.op(DVE, lambda e: e.scalar_tensor_tensor(out=vsin[:, G_:G_ + 1], in0=vv[:, 1, TT - 1:TT], scalar=c512[:, G_:G_ + 1], in1=cc_[:, 1:2],
                                                               op0=ALU.mult, op1=ALU.add), reads=[vvk, cck, t_tk], writes=[st_tk])
                    cx.op(DVE, lambda e: e.tensor_tensor(out=t2[:], in0=vv[:, 0, :], in1=cosT[:, G_, :], op=ALU.mult), reads=[vvk, t_tk], writes=[t2k])
                    cx.op(DVE, lambda e: e.tensor_tensor(out=t4[:], in0=vv[:, 1, :], in1=sinT[:, G_, :], op=ALU.mult), reads=[vvk, t_tk], writes=[t4k])
                    hh, hhk = h_pool.next()
                    cx.op(DVE, lambda e: e.tensor_tensor(out=hh[:], in0=t2[:], in1=t4[:], op=ALU.subtract), reads=[t2k, t4k], writes=[hhk])
                    cx.op(PE, lambda e, g=g: e.matmul(py[:], lhsT=Cg[:, G_, :], rhs=hh[:], start=(g == 0), stop=(g == 7)),
                          reads=[hhk, t_tk], writes=[pyk], inc=True)
                yv, yvk = y_pool.next()
                cx.op(DVE, lambda e: e.scalar_tensor_tensor(out=yv[:], in0=u[:, Q, :], scalar=dsk[:, Q:Q + 1], in1=py[:], op0=ALU.mult, op1=ALU.add),
                      reads=[uk, pyk, t_tk], writes=[yvk])
                g1_, g1k = f_pool.next()
                cx.op(DVE, lambda e: e.tensor_tensor(out=g1_[:], in0=yv[:], in1=yv[:], op=ALU.mult), reads=[yvk], writes=[g1k])
                cx.op(DVE, lambda e: e.tensor_scalar(out=g1_[:], in0=g1_[:], scalar1=0.044715, scalar2=1.0, op0=ALU.mult, op1=ALU.add), reads=[g1k], writes=[g1k])
                cx.op(DVE, lambda e: e.tensor_tensor(out=g1_[:], in0=g1_[:], in1=yv[:], op=ALU.mult), reads=[g1k, yvk], writes=[g1k])
                cx.op(ACT, lambda e: e.activation(out=g1_[:], in_=g1_[:], func=AF.Sigmoid, scale=2.0 * 0.7978845608028654), reads=[g1k], writes=[g1k])
                z, zk = z_pool.next()
                cx.op(DVE, lambda e: e.tensor_tensor(out=z[:], in0=yv[:], in1=g1_[:], op=ALU.mult), reads=[yvk, g1k], writes=[zk])
                cx.dma(PL, S["ysrc"][2 * i:2 * i + 2, 128:256, :].rearrange("c p n -> p c n"), z[:].rearrange("p (c n) -> p c n", c=2), zk, reads=[zk], writes=[tk["ysrc"]])
        ph.close()

    def phase_hg(l, S):
        ph = Phase(cx, f"hg_{l}")
        tk = S["tk"]
        g_tk = Tk(); ph.tks.append(g_tk)
        og = ph.sb([128, 1], F32)
        cx.dma(SP, og[:], P["hg_onorm"][l].rearrange("(p o) -> p o", o=1), g_tk, writes=[g_tk])
        msk = ph.sb([128, TT], F32)
        cx.op(DVE, lambda e: e.memset(msk[:], 1.0), writes=[g_tk])
        cx.op(DVE, lambda e: e.memset(msk[:].rearrange("p (c j) -> p c j", j=HC)[:, :, 0:1], 0.0), writes=[g_tk])
        inb = ph.pool(2, [128, 3, TT], BF16)
        inf = ph.pool(2, [128, TT], F32)
        inv = ph.pool(2, [64, 8, 128], BF16)
        b_pool = ph.pool(2, [128, TT], F32)
        d_pool = ph.pool(2, [128, TT], F32)
        e_pool = ph.pool(3, [128, TT], F32)
        w_pool = ph.pool(2, [128, 4, TT], BF16)
        eb_pool = ph.pool(2, [128, 8], F32)
        am_pool = ph.pool(2, [64, TT], BF16)
        kt_pool = ph.pool(2, [64, 8, 128], BF16)
        stb_pool = ph.pool(2, [128, 9, 128], BF16)
        stf = [ph.sbt([128, 128], F32) for _ in range(2)]
        sq_pool = ph.pool(2, [128, TT], BF16)
        rs_pool = ph.pool(2, [128, TT], F32)
        y_pool = ph.pool(2, [128, TT], BF16)
        nch = TT // HC

        def ld(hd, i):
            ib, ibk = inb.next(); lf, lfk = inf.next(); v, vk = inv.next()
            sl = slice(i * TT, (i + 1) * TT)
            r0 = hd * 128
            cx.dma(SP, ib[:, 0, :], S["hqT"][r0:r0 + 128, sl], ibk, reads=[tk["hqT"]], writes=[ibk])
            cx.dma(SP, ib[:, 1, :], S["khT"][r0:r0 + 128, sl], ibk, reads=[tk["khT"]], writes=[ibk])
            cx.dma(SP, ib[:, 2, :], S["hgT"][r0:r0 + 128, sl], ibk, reads=[tk["hgT"]], writes=[ibk])
            cx.dma(SP, lf[:], S["lfhT"][r0:r0 + 128, sl], lfk, reads=[tk["lfhT"]], writes=[lfk])
            cx.dma(SP, v[:], S["hvP"][hd, :, i * nch:(i + 1) * nch, :], vk, reads=[tk["hvP"]], writes=[vk])
            return ib, ibk, lf, lfk, v, vk

        for hd in range(1):
            cur = 0
            cx.op(DVE, lambda e: e.memset(stf[0][0][:], 0.0), writes=[stf[0][1]])
            stb, stbk = stb_pool.next()
            cx.op(PL, lambda e, stb=stb: e.memset(stb[:, 0, :], 0.0), writes=[stbk])
            nxt = ld(hd, 0)
            for i in range(NT):
                ib, ibk, lf, lfk, v, vk = nxt
                if i + 1 < NT:
                    nxt = ld(hd, i + 1)
                b, bk = b_pool.next()
                cx.op(DVE, lambda e: e.tensor_tensor_scan(out=b[:], data0=msk[:], data1=lf[:], initial=0.0, op0=ALU.mult, op1=ALU.add),
                      reads=[lfk, g_tk], writes=[bk])
                b3 = b[:].rearrange("p (c j) -> p c j", j=HC)
                w, wk = w_pool.next()
                d1, d1k = d_pool.next()
                cx.op(DVE, lambda e: e.tensor_tensor(out=d1[:].rearrange("p (c j) -> p c j", j=HC), in0=b3,
                                                     in1=b3[:, :, HC // 2 - 1:HC // 2].to_broadcast([128, nch, HC]), op=ALU.subtract),
                      reads=[bk], writes=[d1k])
                e1, e1k = e_pool.next()
                cx.op(ACT, lambda e: e.activation(out=e1[:], in_=d1[:], func=AF.Exp), reads=[d1k], writes=[e1k])
                cx.op(DVE, lambda e: e.tensor_tensor(out=w[:, 0, :], in0=ib[:, 0, :], in1=e1[:], op=ALU.mult), reads=[ibk, e1k], writes=[wk])
                e2, e2k = e_pool.next()
                cx.op(ACT, lambda e: e.activation(out=e2[:], in_=d1[:], func=AF.Exp, scale=-1.0), reads=[d1k], writes=[e2k])
                cx.op(DVE, lambda e: e.tensor_tensor(out=w[:, 1, :], in0=ib[:, 1, :], in1=e2[:], op=ALU.mult), reads=[ibk, e2k], writes=[wk])
                e3, e3k = e_pool.next()
                cx.op(ACT, lambda e: e.activation(out=e3[:], in_=b[:], func=AF.Exp), reads=[bk], writes=[e3k])
                cx.op(DVE, lambda e: e.tensor_tensor(out=w[:, 2, :], in0=ib[:, 0, :], in1=e3[:], op=ALU.mult), reads=[ibk, e3k], writes=[wk])
                d2, d2k = d_pool.next()
                cx.op(DVE, lambda e: e.tensor_tensor(out=d2[:].rearrange("p (c j) -> p c j", j=HC), in0=b3,
                                                     in1=b3[:, :, HC - 1:HC].to_broadcast([128, nch, HC]), op=ALU.subtract),
                      reads=[bk], writes=[d2k])
                e4, e4k = e_pool.next()
                cx.op(ACT, lambda e: e.activation(out=e4[:], in_=d2[:], func=AF.Exp, scale=-1.0), reads=[d2k], writes=[e4k])
                cx.op(DVE, lambda e: e.tensor_tensor(out=w[:, 3, :], in0=ib[:, 1, :], in1=e4[:], op=ALU.mult), reads=[ibk, e4k], writes=[wk])
                eb, ebk = eb_pool.next()
                cx.op(ACT, lambda e: e.activation(out=eb[:].unsqueeze(2), in_=b3[:, :, HC - 1:HC], func=AF.Exp), reads=[bk], writes=[ebk])
                pa, pak = PS.next()
                for c in range(nch):
                    cs = slice(c * HC, (c + 1) * HC)
                    cx.op(PE, lambda e, cs=cs: e.matmul(pa[0:HC, cs], lhsT=w[:, 1, cs], rhs=w[:, 0, cs], start=True, stop=True),
                          reads=[wk], writes=[pak], inc=(c == nch - 1))
                ptp, ptk_ = PS.next()
                ptb = ptp[:].bitcast(BF16)
                for c in range(nch):
                    cs = slice(c * HC, (c + 1) * HC)
                    cx.op(PE, lambda e, cs=cs, c=c: e.transpose(ptb[0:HC, c * 128:(c + 1) * 128], w[:, 3, cs], ident_b),
                          reads=[wk, c_tk], writes=[ptk_], inc=(c == nch - 1))
                am, amk = am_pool.next()
                cx.op(DVE, lambda e: e.tensor_tensor(out=am[:].rearrange("p (c j) -> p c j", j=HC),
                                                     in0=pa[0:HC, :].rearrange("p (c j) -> p c j", j=HC),
                                                     in1=tri_b[0:HC, 0:HC].unsqueeze(1).to_broadcast([HC, nch, HC]), op=ALU.mult),
                      reads=[pak, c_tk], writes=[amk])
                kt, ktk = kt_pool.next()
                cx.op(ACT, lambda e: e.activation(out=kt[:].rearrange("p c k -> p (c k)"), in_=ptb[0:HC, 0:nch * 128], func=AF.Copy),
                      reads=[ptk_], writes=[ktk])
                kvs = []
                for half in range(2):
                    pkv, pkvk = PS.next()
                    for c4 in range(4):
                        c = half * 4 + c4
                        cx.op(PE, lambda e, c=c, c4=c4, pkv=pkv: e.matmul(pkv[:, c4 * 128:(c4 + 1) * 128], lhsT=kt[:, c, :], rhs=v[:, c, :],
                                                                        start=True, stop=True),
                              reads=[ktk, vk], writes=[pkvk], inc=(c4 == 3))
                    kvs.append((pkv, pkvk))
                for c in range(nch):
                    pkv, pkvk = kvs[c // 4]
                    src, srck = stf[cur]
                    dst, dstk = stf[1 - cur]
                    cx.op(DVE, lambda e, c=c, pkv=pkv, src=src, dst=dst: e.scalar_tensor_tensor(
                        out=dst[:], in0=src[:], scalar=eb[:, c:c + 1], in1=pkv[:, (c % 4) * 128:(c % 4 + 1) * 128], op0=ALU.mult, op1=ALU.add),
                        reads=[srck, ebk, pkvk], writes=[dstk])
                    cur = 1 - cur
                    if c < nch - 1:
                        cx.op(PL, lambda e, c=c, dst=dst, stb=stb: e.tensor_copy(out=stb[:, c + 1, :], in_=dst[:]), reads=[dstk], writes=[stbk])
                    else:
                        nstb, nstbk = stb_pool.next()
                        cx.op(PL, lambda e, dst=dst, nstb=nstb: e.tensor_copy(out=nstb[:, 0, :], in_=dst[:]), reads=[dstk], writes=[nstbk])
                po, pok = PS.next()
                for c in range(nch):
                    cs = slice(c * HC, (c + 1) * HC)
                    cx.op(PE, lambda e, cs=cs, c=c, stb=stb: e.matmul(po[:, cs], lhsT=stb[:, c, :], rhs=w[:, 2, cs], start=True, stop=False),
                          reads=[stbk, wk], writes=[pok], inc=False)
                    cx.op(PE, lambda e, cs=cs, c=c: e.matmul(po[:, cs], lhsT=v[:, c, :], rhs=am[0:HC, cs], start=False, stop=True),
                          reads=[vk, amk], writes=[pok], inc=(c == nch - 1))
                stb, stbk = nstb, nstbk
                sq, sqk = sq_pool.next()
                cx.op(ACT, lambda e: e.activation(out=sq[:], in_=po[:], func=AF.Square), reads=[pok], writes=[sqk])
                pn, pnk = PS.next()
                mm_group(pn[:], [(ones_b, sq[:])], [sqk, c_tk], pnk)
                rs, rk = rs_pool.next()
                cx.op(ACT, lambda e: e.activation(out=rs[:], in_=pn[:], func=AF.Sqrt, scale=1.0 / 128, bias=eps_t[:, 0:1]),
                      reads=[pnk, c_tk], writes=[rk])
                cx.op(DVE, lambda e: e.reciprocal(out=rs[:], in_=rs[:]), reads=[rk], writes=[rk])
                cx.op(DVE, lambda e: e.scalar_tensor_tensor(out=rs[:], in0=po[:], scalar=og[:, 0:1], in1=rs[:], op0=ALU.mult, op1=ALU.mult),
                      reads=[pok, rk, g_tk], writes=[rk])
                y, yk = y_pool.next()
                cx.op(DVE, lambda e: e.tensor_tensor(out=y[:], in0=rs[:], in1=ib[:, 2, :], op=ALU.mult), reads=[rk, ibk], writes=[yk])
                cx.dma(PL, S["ysrc"][2 * i:2 * i + 2, 256:384, :].rearrange("c p n -> p c n"), y[:].rearrange("p (c n) -> p c n", c=2), yk, reads=[yk], writes=[tk["ysrc"]])
        ph.close()

    def phase3a(l, xT, x_tk, S):
        ph = Phase(cx, f"p3a_{l}")
        tk = S["tk"]
        stg_pool = ph.pool(2, [128, 8, 512], F32)
        g_tk = Tk(); ph.tks.append(g_tk)
        g1 = load_vec(ph, P["norm_mix"][l], 8, g_tk)
        bglu = load_vec(ph, P["s5_b_glu"][l], 4, g_tk)
        Wg, wg_tk = ph.sbt([128, 8, 3072], BF16)
        Wb, wb_tk = ph.sbt([128, 12, 1024], BF16)
        Wo, wo_tk = ph.sbt([128, 8, 1024], BF16)
        Wgl, wgl_tk = ph.sbt([128, 4, 512], BF16)
        load_w(ph, P["w_in"][l], 8, 1026, 4098, gain=(g1, g_tk), stg_pool=stg_pool, w_out=Wg, w_tk=wg_tk)
        load_w(ph, P["w_branch"][l], 12, 0, 1024, stg_pool=stg_pool, w_out=Wb, w_tk=wb_tk)
        load_w(ph, P["w_out"][l], 8, 0, 1024, stg_pool=stg_pool, w_out=Wo, w_tk=wo_tk)
        load_w(ph, P["s5_w_glu"][l], 4, 0, 512, stg_pool=stg_pool, w_out=Wgl, w_tk=wgl_tk)
        TW = 512
        x_pool = stg_pool
        y_pool = ph.pool(2, [128, 12, TW], BF16)
        sq_pool = ph.pool(1, [128, 8, TW], BF16)
        h_pool = ph.pool(2, [128, 8, TW], BF16)
        rs_pool = ph.pool(2, [128, TW], F32)
        mg_pool = ph.pool(1, [128, 8, TW], BF16)
        mgs, mgsk = ph.sbt([128, 4, TW], BF16)
        evf = ph.pool(4, [128, TW], F32)
        xr = xT.rearrange("(kt p) n -> p kt n", p=128)
        x1r = S["x1"].rearrange("(kt p) n -> p kt n", p=128)

        def ld(i):
            xt, xk = x_pool.next()
            cx.dma(SP, xt[:], xr[:, :, i * TW:(i + 1) * TW], xk, reads=[x_tk], writes=[xk])
            yt, yk = y_pool.next()
            for hf_ in range(2):
                for b_ in range(3):
                    cx.dma(SP, yt[:, b_ * 4:(b_ + 1) * 4, hf_ * 256:(hf_ + 1) * 256],
                           S["ydst"][2 * i + hf_].rearrange("(r b p) n -> b p r n", r=4, b=3, p=128)[b_], yk, reads=[tk["ydst"]], writes=[yk])
            return xt, xk, yt, yk

        def prep(xt, xk):
            rs, rk = rms_rstd(ph, xt[:], xk, 8, TW, sq_pool, rs_pool)
            h, hk = h_pool.next()
            cx.op(DVE, lambda e: e.tensor_tensor(out=h[:], in0=xt[:], in1=bc(rs[:], [128, 8, TW], 1), op=ALU.mult),
                  reads=[xk, rk], writes=[hk])
            return h, hk

        nxt = ld(0)
        nxt_h = prep(nxt[0], nxt[1])
        for i in range(N // TW):
            xt, xk, yt, yk = nxt
            h, hk = nxt_h
            nxt = ld(i + 1) if i + 1 < N // TW else None
            for ct in range(4):
                pb, pk = PS.next()
                mm_group(pb[:, 0:TW], [(Wgl[:, k, ct * 128:(ct + 1) * 128], yt[:, 4 + k, :]) for k in range(4)], [wgl_tk, yk], pk)
                sg, sgk = evf.next()
                cx.op(ACT, lambda e, ct=ct: e.activation(out=sg[:], in_=pb[:, 0:TW], func=AF.Sigmoid, bias=bglu[:, ct:ct + 1]),
                      reads=[pk, g_tk], writes=[sgk])
                cx.op(DVE, lambda e, ct=ct: e.tensor_tensor(out=mgs[:, ct, :], in0=yt[:, 4 + ct, :], in1=sg[:], op=ALU.mult),
                      reads=[yk, sgk], writes=[mgsk])
            mg, mgk = mg_pool.next()
            for ct in range(8):
                if ct == 4 and nxt is not None:
                    nxt_h = prep(nxt[0], nxt[1])
                acc, acck = evf.next()
                for b in range(3):
                    pg, pgk = PS.next()
                    c0 = b * 1024 + ct * 128
                    mm_group(pg[:, 0:TW], [(Wg[:, k, c0:c0 + 128], h[:, k, :]) for k in range(8)], [wg_tk, hk], pgk)
                    gs, gsk = evf.next()
                    cx.op(ACT, lambda e: e.activation(out=gs[:], in_=pg[:, 0:TW], func=AF.Sigmoid), reads=[pgk], writes=[gsk])
                    pbr, pbk = PS.next()
                    if b == 1:
                        prs = [(Wb[:, 4 + k, ct * 128:(ct + 1) * 128], mgs[:, k, :]) for k in range(4)]
                        rd = [wb_tk, mgsk]
                    else:
                        prs = [(Wb[:, b * 4 + k, ct * 128:(ct + 1) * 128], yt[:, b * 4 + k, :]) for k in range(4)]
                        rd = [wb_tk, yk]
                    mm_group(pbr[:, 0:TW], prs, rd, pbk)
                    if b == 0:
                        cx.op(DVE, lambda e: e.tensor_tensor(out=acc[:], in0=pbr[:, 0:TW], in1=gs[:], op=ALU.mult),
                              reads=[pbk, gsk], writes=[acck])
                    else:
                        cx.op(DVE, lambda e: e.tensor_tensor(out=gs[:], in0=pbr[:, 0:TW], in1=gs[:], op=ALU.mult),
                              reads=[pbk, gsk], writes=[gsk])
                        dstap = mg[:, ct, :] if b == 2 else acc[:]
                        dstk = mgk if b == 2 else acck
                        cx.op(DVE, lambda e, dstap=dstap: e.tensor_tensor(out=dstap, in0=acc[:], in1=gs[:], op=ALU.add),
                              reads=[acck, gsk], writes=[dstk])
            for ct in range(8):
                pb, pk = PS.next()
                mm_group(pb[:, 0:TW], [(Wo[:, k, ct * 128:(ct + 1) * 128], mg[:, k, :]) for k in range(8)], [wo_tk, mgk], pk)
                cx.op(DVE, lambda e, ct=ct: e.tensor_tensor(out=xt[:, ct, :], in0=xt[:, ct, :], in1=pb[:, 0:TW], op=ALU.add),
                      reads=[pk, xk], writes=[xk])
            cx.dma(PL, x1r[:, :, i * TW:(i + 1) * TW], xt[:], xk, reads=[xk], writes=[tk["x1"]])
        ph.close()

    def phase3b(l, S):
        ph = Phase(cx, f"p3b_{l}")
        tk = S["tk"]
        stg_pool = ph.pool(2, [128, 8, 512], F32)
        g_tk = Tk(); ph.tks.append(g_tk)
        gx = load_vec(ph, P["norm_x"][l], 8, g_tk)
        gm = load_vec(ph, P["norm_mem"][l], 8, g_tk)
        gq = load_vec(ph, P["x_qnorm"][l], 2, g_tk)
        gk = load_vec(ph, P["x_knorm"][l], 2, g_tk)
        cx.op(DVE, lambda e: e.tensor_scalar(out=gq[:], in0=gq[:], scalar1=1.0 / 16, scalar2=None, op0=ALU.mult), reads=[g_tk], writes=[g_tk])
        Wq, wq_tk = ph.sbt([128, 8, 1024], BF16)
        Wo, wo_tk = ph.sbt([128, 8, 1024], BF16)
        Wkv, wkv_tk = ph.sbt([128, 8, 1024], BF16)
        load_w(ph, P["xq"][l], 8, 0, 1024, gain=(gx, g_tk), stg_pool=stg_pool, w_out=Wq, w_tk=wq_tk)
        load_w(ph, P["xo"][l], 8, 0, 1024, stg_pool=stg_pool, w_out=Wo, w_tk=wo_tk)
        TW = 512
        sq_pool = ph.pool(1, [128, 8, TW], BF16)
        rs_pool = ph.pool(2, [128, TW], F32)
        evb = ph.pool(4, [128, TW], BF16)
        evf = ph.pool(4, [128, TW], F32)
        mt_, mk = ph.sbt([128, 8, NMEM], F32)
        cx.dma(SP, mt_[:], memT_in.rearrange("(kt p) n -> p kt n", p=128), mk, writes=[mk])
        rs, rk = rms_rstd(ph, mt_[:], mk, 8, NMEM, sq_pool, rs_pool)
        hm, hmk = ph.sbt([128, 8, NMEM], BF16)
        cx.op(DVE, lambda e: e.tensor_tensor(out=hm[:], in0=mt_[:], in1=bc(rs[:, 0:NMEM], [128, 8, NMEM], 1), op=ALU.mult),
              reads=[mk, rk], writes=[hmk])
        kx, kxk = ph.sbt([128, 8, NMEM], BF16)
        vx, vxk = ph.sbt([128, 2, 1024], BF16)
        load_w(ph, P["xk"][l], 8, 0, 1024, gain=(gm, g_tk), stg_pool=stg_pool, w_out=Wkv, w_tk=wkv_tk)
        for hd in range(4):
            pbs = []
            for dt_ in range(2):
                ct = hd * 2 + dt_
                pb, pk = PS.next()
                mm_group(pb[:, 0:NMEM], [(Wkv[:, k, ct * 128:(ct + 1) * 128], hm[:, k, :]) for k in range(8)], [wkv_tk, hmk], pk)
                pbs.append((pb, pk))
            sqs = []
            for (pb, pk) in pbs:
                sq, sqk = evb.next()
                cx.op(ACT, lambda e, pb=pb, sq=sq: e.activation(out=sq[:, 0:NMEM], in_=pb[:, 0:NMEM], func=AF.Square), reads=[pk], writes=[sqk])
                sqs.append((sq, sqk))
            p2, p2k = PS.next()
            mm_group(p2[:, 0:NMEM], [(ones_b, sq[:, 0:NMEM]) for (sq, _) in sqs], [sqs[0][1], sqs[1][1], c_tk], p2k)
            rh, rhk = evf.next()
            cx.op(ACT, lambda e: e.activation(out=rh[:, 0:NMEM], in_=p2[:, 0:NMEM], func=AF.Sqrt, scale=1.0 / 256, bias=eps_t[:, 0:1]),
                  reads=[p2k, c_tk], writes=[rhk])
            cx.op(DVE, lambda e: e.reciprocal(out=rh[:, 0:NMEM], in_=rh[:, 0:NMEM]), reads=[rhk], writes=[rhk])
            for dt_, (pb, pk) in enumerate(pbs):
                cx.op(DVE, lambda e, pb=pb, dt_=dt_: e.scalar_tensor_tensor(out=kx[:, hd * 2 + dt_, :], in0=pb[:, 0:NMEM], scalar=gk[:, dt_:dt_ + 1],
                                                                       in1=rh[:, 0:NMEM], op0=ALU.mult, op1=ALU.mult),
                      reads=[pk, rhk, g_tk], writes=[kxk])
        load_w(ph, P["xv"][l], 8, 0, 1024, gain=(gm, g_tk), stg_pool=stg_pool, w_out=Wkv, w_tk=wkv_tk)
        for mt2 in range(2):
            for cc_ in range(2):
                pb, pk = PS.next()
                mm_group(pb[:], [(hm[:, k, mt2 * 128:(mt2 + 1) * 128], Wkv[:, k, cc_ * 512:(cc_ + 1) * 512]) for k in range(8)], [wkv_tk, hmk], pk)
                cx.op(ACT, lambda e, pb=pb, mt2=mt2, cc_=cc_: e.activation(out=vx[:, mt2, cc_ * 512:(cc_ + 1) * 512], in_=pb[:], func=AF.Copy),
                      reads=[pk], writes=[vxk])
        x_pool = stg_pool
        h_pool = ph.pool(2, [128, 8, TW], BF16)
        qh_pool = ph.pool(2, [128, 2, TW], BF16)
        pt_pool = ph.pool(2, [128, 2, TW], BF16)
        o_pool = ph.pool(1, [128, 8, TW], BF16)
        x1r = S["x1"].rearrange("(kt p) n -> p kt n", p=128)
        x2r = S["x2"].rearrange("(kt p) n -> p kt n", p=128)

        def ld(i):
            xt, xk = x_pool.next()
            cx.dma(SP, xt[:], x1r[:, :, i * TW:(i + 1) * TW], xk, reads=[tk["x1"]], writes=[xk])
            return xt, xk

        def prep(xt, xk):
            rs, rk = rms_rstd(ph, xt[:], xk, 8, TW, sq_pool, rs_pool)
            h, hk = h_pool.next()
            cx.op(DVE, lambda e: e.tensor_tensor(out=h[:], in0=xt[:], in1=bc(rs[:], [128, 8, TW], 1), op=ALU.mult),
                  reads=[xk, rk], writes=[hk])
            return h, hk

        def part_a(h, hk, hd):
            pbs = []
            for dt_ in range(2):
                ct = hd * 2 + dt_
                pb, pk = PS.next()
                mm_group(pb[:, 0:TW], [(Wq[:, k, ct * 128:(ct + 1) * 128], h[:, k, :]) for k in range(8)], [wq_tk, hk], pk)
                pbs.append((pb, pk))
            sqs = []
            for (pb, pk) in pbs:
                sq, sqk = evb.next()
                cx.op(ACT, lambda e, pb=pb, sq=sq: e.activation(out=sq[:], in_=pb[:, 0:TW], func=AF.Square), reads=[pk], writes=[sqk])
                sqs.append((sq, sqk))
            p2, p2k = PS.next()
            mm_group(p2[:, 0:TW], [(ones_b, sq[:]) for (sq, _) in sqs], [sqs[0][1], sqs[1][1], c_tk], p2k)
            rh, rhk = evf.next()
            cx.op(ACT, lambda e: e.activation(out=rh[:], in_=p2[:, 0:TW], func=AF.Sqrt, scale=1.0 / 256, bias=eps_t[:, 0:1]),
                  reads=[p2k, c_tk], writes=[rhk])
            cx.op(DVE, lambda e: e.reciprocal(out=rh[:], in_=rh[:]), reads=[rhk], writes=[rhk])
            qh, qhk = qh_pool.next()
            for dt_, (pb, pk) in enumerate(pbs):
                cx.op(DVE, lambda e, pb=pb, dt_=dt_: e.scalar_tensor_tensor(out=qh[:, dt_, :], in0=pb[:, 0:TW], scalar=gq[:, dt_:dt_ + 1],
                                                                       in1=rh[:], op0=ALU.mult, op1=ALU.mult),
                      reads=[pk, rhk, g_tk], writes=[qhk])
            return qh, qhk

        def part_b(hd, qh, qhk, oT, oTk):
            pt, ptk = pt_pool.next()
            for mt2 in range(2):
                pb, pk = PS.next()
                mm_group(pb[:, 0:TW], [(kx[:, hd * 2 + dt_, mt2 * 128:(mt2 + 1) * 128], qh[:, dt_, :]) for dt_ in range(2)], [kxk, qhk], pk)
                cx.op(ACT, lambda e, pb=pb, mt2=mt2: e.activation(out=pt[:, mt2, :], in_=pb[:, 0:TW], func=AF.Exp), reads=[pk], writes=[ptk])
            p3, p3k = PS.next()
            mm_group(p3[:, 0:TW], [(ones_b, pt[:, mt2, :]) for mt2 in range(2)], [ptk, c_tk], p3k)
            ri, rik = evf.next()
            cx.op(DVE, lambda e: e.reciprocal(out=ri[:], in_=p3[:, 0:TW]), reads=[p3k], writes=[rik])
            for dt_ in range(2):
                pb, pk = PS.next()
                c0 = hd * 256 + dt_ * 128
                mm_group(pb[:, 0:TW], [(vx[:, mt2, c0:c0 + 128], pt[:, mt2, :]) for mt2 in range(2)], [vxk, ptk], pk)
                cx.op(DVE, lambda e, pb=pb, dt_=dt_: e.tensor_tensor(out=oT[:, hd * 2 + dt_, :], in0=pb[:, 0:TW], in1=ri[:], op=ALU.mult),
                      reads=[pk, rik], writes=[oTk])

        NTW = N // TW
        cur_x = ld(0)
        cur_h = prep(*cur_x)
        for i in range(NTW):
            xt, xk = cur_x
            h, hk = cur_h
            nxt_x = ld(i + 1) if i + 1 < NTW else None
            nxt_h = None
            oT, oTk = o_pool.next()
            qa = part_a(h, hk, 0)
            for hd in range(4):
                qn = part_a(h, hk, hd + 1) if hd + 1 < 4 else None
                part_b(hd, qa[0], qa[1], oT, oTk)
                if hd == 1 and nxt_x is not None:
                    nxt_h = prep(*nxt_x)
                qa = qn
            for ct in range(8):
                pb, pk = PS.next()
                mm_group(pb[:, 0:TW], [(Wo[:, k, ct * 128:(ct + 1) * 128], oT[:, k, :]) for k in range(8)], [wo_tk, oTk], pk)
                cx.op(DVE, lambda e, ct=ct, pb=pb: e.tensor_tensor(out=xt[:, ct, :], in0=xt[:, ct, :], in1=pb[:, 0:TW], op=ALU.add),
                      reads=[pk, xk], writes=[xk])
            cx.dma(PL, x2r[:, :, i * TW:(i + 1) * TW], xt[:], xk, reads=[xk], writes=[tk["x2"]])
            cur_x, cur_h = nxt_x, nxt_h
        ph.close()

    def phase3c(l, S):
        ph = Phase(cx, f"p3c_{l}")
        tk = S["tk"]
        stg_pool = ph.pool(2, [128, 8, 512], F32)
        g_tk = Tk(); ph.tks.append(g_tk)
        gf = load_vec(ph, P["norm_ffn"][l], 8, g_tk)
        Wgu, wgu_tk = ph.sbt([128, 8, 2 * DFF], BF16)
        Wd, wd_tk = ph.sbt([128, 22, 1024], BF16)
        load_w(ph, P["w_gate_up"][l], 8, 0, 2 * DFF, gain=(gf, g_tk), stg_pool=stg_pool, w_out=Wgu, w_tk=wgu_tk)
        load_w(ph, P["w_down"][l], 22, 0, 1024, stg_pool=stg_pool, w_out=Wd, w_tk=wd_tk)
        TW = 512
        hid_pool = ph.pool(1, [128, 22, TW], BF16)
        sq_pool = hid_pool
        rs_pool = ph.pool(2, [128, TW], F32)
        evf = ph.pool(2, [128, TW], F32)
        x_pool = stg_pool
        h_pool = ph.pool(1, [128, 8, TW], BF16)
        x2r = S["x2"].rearrange("(kt p) n -> p kt n", p=128)
        x3r = S["x3"].rearrange("(kt p) n -> p kt n", p=128)

        def ld(i):
            xt, xk = x_pool.next()
            cx.dma(SP, xt[:], x2r[:, :, i * TW:(i + 1) * TW], xk, reads=[tk["x2"]], writes=[xk])
            return xt, xk

        nxt = ld(0)
        for i in range(N // TW):
            xt, xk = nxt
            if i + 1 < N // TW:
                nxt = ld(i + 1)
            rs, rk = rms_rstd(ph, xt[:], xk, 8, TW, sq_pool, rs_pool)
            h, hk = h_pool.next()
            cx.op(DVE, lambda e: e.tensor_tensor(out=h[:], in0=xt[:], in1=bc(rs[:], [128, 8, TW], 1), op=ALU.mult),
                  reads=[xk, rk], writes=[hk])
            hid, hidk = hid_pool.next()
            for j in range(22):
                pa, pak = PS.next()
                mm_group(pa[:, 0:TW], [(Wgu[:, k, j * 128:(j + 1) * 128], h[:, k, :]) for k in range(8)], [wgu_tk, hk], pak)
                sa, sak = evf.next()
                cx.op(ACT, lambda e, pa=pa, sa=sa: e.activation(out=sa[:], in_=pa[:, 0:TW], func=AF.Silu), reads=[pak], writes=[sak])
                pb, pk = PS.next()
                mm_group(pb[:, 0:TW], [(Wgu[:, k, DFF + j * 128:DFF + (j + 1) * 128], h[:, k, :]) for k in range(8)], [wgu_tk, hk], pk)
                cx.op(DVE, lambda e, pb=pb, sa=sa, j=j: e.tensor_tensor(out=hid[:, j, :], in0=pb[:, 0:TW], in1=sa[:], op=ALU.mult),
                      reads=[pk, sak], writes=[hidk])
            for ct in range(8):
                pb, pk = PS.next()
                mm_group(pb[:, 0:TW], [(Wd[:, j, ct * 128:(ct + 1) * 128], hid[:, j, :]) for j in range(22)], [wd_tk, hidk], pk)
                cx.op(DVE, lambda e, ct=ct, pb=pb: e.tensor_tensor(out=xt[:, ct, :], in0=xt[:, ct, :], in1=pb[:, 0:TW], op=ALU.add),
                      reads=[pk, xk], writes=[xk])
            cx.dma(PL, x3r[:, :, i * TW:(i + 1) * TW], xt[:], xk, reads=[xk], writes=[S["tk"]["x3"]])
        ph.close()

    ccsem = cx.get_dsem()

    def gather_y(l, S):
        tk = S["tk"]
        cx._need(PL, cx._deps([tk["ysrc"]], [tk["ydst"]]))
        for c in range(N // 256):
            ccsem.count += 1
            nc.gpsimd.collective_compute("AllGather", ALU.bypass, replica_groups=[[0, 1, 2, 3], [4, 5, 6, 7]],
                                         ins=[S["ysrc"][c]], outs=[S["ydst"][c]]).then_inc(ccsem.h, 1)
            cx.ninstr += 1
        tk["ydst"].w[ccsem] = ccsem.count
        tk["ysrc"].r[ccsem] = ccsem.count

    xT, x_tk = xT_in, NOTK
    for l in range(depth):
        S = mk_scratch(l)
        if "p1" in stages:
            phase1(l, xT, x_tk, S)
        if "fox" in stages:
            phase_fox(l, S)
        if "s5" in stages:
            phase_s5(l, S)
        if "hg" in stages:
            phase_hg(l, S)
        if "cc" in stages:
            gather_y(l, S)
        if "p3a" in stages:
            phase3a(l, xT, x_tk, S)
        if "p3b" in stages:
            phase3b(l, S)
        if "p3c" in stages:
            phase3c(l, S)
        xT, x_tk = S["x3"], S["tk"]["x3"]
    G.close()
    return nc, cx, dbg_out


def _consts():
    c = np.zeros((128, 6, 128), np.float32)
    p = np.arange(128)
    c[:, 0, :] = np.eye(128)
    c[:, 1, :] = (p[None, :] >= p[:, None])
    c[:, 2, :] = (p[:, None] // 64 == p[None, :] // 64)
    c[:, 3, :] = np.eye(128)[(p + 64) % 128]
    c[:, 4, :] = 1.0
    c[:, 5, :] = (p[:, None] // 64 == p[None, :] // 64) & (p[:, None] % 64 < p[None, :] % 64)
    return c


_SHARED = ("norm_mix", "fox_qnorm", "fox_knorm", "s5_w_glu", "s5_b_glu", "hg_onorm", "w_branch", "w_out",
           "norm_x", "norm_mem", "xq", "xk", "xv", "xo", "x_qnorm", "x_knorm", "norm_ffn", "w_gate_up", "w_down")
_STAGES = ("p1", "fox", "s5", "hg", "cc", "p3a", "p3b", "p3c")


def _local_params(inputs, j):
    f = lambda k: np.asarray(inputs[k], np.float32)
    w = f("w_in")
    sl = lambda a, b: np.arange(a, b)
    cols = np.concatenate([
        sl(128 * j, 128 * j + 128), sl(512 + 128 * j, 512 + 128 * j + 128), sl(1024 + 128 * j, 1024 + 128 * j + 128),
        sl(1536 + 2 * j, 1536 + 2 * j + 2), sl(1544 + 128 * j, 1544 + 128 * j + 128), sl(2056 + 128 * j, 2056 + 128 * j + 128),
        sl(2568 + 128 * j, 2568 + 128 * j + 128), sl(3080 + 128 * j, 3080 + 128 * j + 128), sl(3592 + 128 * j, 3592 + 128 * j + 128),
        sl(4104, 7176)])
    m = {"w_in": np.ascontiguousarray(w[:, :, cols])}
    m["fox_fbias"] = np.ascontiguousarray(f("fox_fbias")[:, 2 * j:2 * j + 2])
    for k in ("s5_a_re", "s5_a_im", "s5_b_re", "s5_b_im", "s5_c_re", "s5_c_im", "s5_log_dt"):
        m[k] = np.ascontiguousarray(f(k)[:, 8 * j:8 * j + 8])
    m["s5_d"] = np.ascontiguousarray(f("s5_d")[:, 128 * j:128 * j + 128])
    m["hg_lb"] = np.ascontiguousarray(f("hg_lb")[:, 128 * j:128 * j + 128])
    return m


def _run(inputs, N, depth):
    x = np.asarray(inputs["x"], np.float32)
    mem = np.asarray(inputs["mem"], np.float32)
    B = x.shape[0]
    nc, cx, _ = build(N, depth, debug=False, stages=_STAGES)
    shared = {k: np.ascontiguousarray(np.asarray(inputs[k], np.float32)) for k in _SHARED}
    shared["consts"] = _consts()
    shared["nvec"] = np.zeros((128, 80), np.float32)
    shared["jv"] = np.tile(np.arange(TT, dtype=np.float32)[None], (128, 1))
    shared["m16"] = (np.arange(128)[:, None] // 16 == np.arange(8)[None, :]).astype(np.float32)
    loc = [_local_params(inputs, j) for j in range(4)]
    maps = []
    for c in range(8):
        b, j = c // 4, c % 4
        m = dict(shared)
        m.update(loc[j])
        m["xT"] = np.ascontiguousarray(x[b, :N].T)
        m["memT"] = np.ascontiguousarray(mem[b].T)
        maps.append(m)
    res = run_bass_kernel_spmd(nc, maps, core_ids=list(range(8)))
    out = np.stack([np.asarray(res.results[4 * b]["outT"]).T for b in range(B)], axis=0)
    return np.ascontiguousarray(out.astype(np.float32))


def kernel(**inputs):
    return _run(inputs, 16384, DEPTH)
```

```python
import contextlib
import numpy as np
import ml_dtypes
import concourse.bass as bass
import concourse.mybir as mybir
from concourse.bass_utils import run_bass_kernel_spmd

F32 = mybir.dt.float32
BF16 = mybir.dt.bfloat16
AF = mybir.ActivationFunctionType
ALU = mybir.AluOpType
AX = mybir.AxisListType

D = 1024
DEPTH = 4
NMEM = 256
DFF = 2816
INC = 7176
EPS = 1e-6
TT = 512
SL = 64
HC = 64


class Sem:
    def __init__(self, h):
        self.h = h
        self.count = 0


class Tk:
    __slots__ = ("w", "r", "dsem")

    def __init__(self):
        self.w = {}
        self.r = {}
        self.dsem = None


class Eng:
    def __init__(self, name, e, sem, is_pe=False):
        self.name, self.e, self.sem, self.is_pe = name, e, sem, is_pe
        self.waited = {}


class Ctx:
    def __init__(self, nc):
        self.nc = nc
        self.st = contextlib.ExitStack()
        self.sem_free = []
        self.nsem = 0
        self.dma_sems = []
        self.pe = Eng("pe", nc.tensor, self._newsem("pe"), True)
        self.act = Eng("act", nc.scalar, self._newsem("act"))
        self.dve = Eng("dve", nc.vector, self._newsem("dve"))
        self.pool = Eng("pool", nc.gpsimd, self._newsem("pool"))
        self.sp = Eng("sp", nc.sync, self._newsem("sp"))
        self.engs = [self.pe, self.act, self.dve, self.pool, self.sp]
        self.ninstr = 0

    def _newsem(self, name):
        self.nsem += 1
        return Sem(self.st.enter_context(self.nc.semaphore(f"s{self.nsem}_{name}")))

    def get_dsem(self):
        if self.sem_free:
            return self.sem_free.pop()
        s = self._newsem("d")
        self.dma_sems.append(s)
        return s

    def _need(self, E, tokens):
        for S, v in tokens.items():
            if v <= 0 or (E.is_pe and S is E.sem) or E.waited.get(S, 0) >= v:
                continue
            E.e.wait_ge(S.h, v)
            E.waited[S] = v
            self.ninstr += 1

    def _deps(self, reads, writes):
        tok = {}
        for t in reads:
            for S, v in t.w.items():
                if tok.get(S, 0) < v:
                    tok[S] = v
        for t in writes:
            for S, v in t.w.items():
                if tok.get(S, 0) < v:
                    tok[S] = v
            for S, v in t.r.items():
                if tok.get(S, 0) < v:
                    tok[S] = v
        return tok

    def _commit(self, S, v, reads, writes):
        for t in writes:
            t.w = {S: v}
            t.r = {}
        for t in reads:
            if t.r.get(S, 0) < v:
                t.r[S] = v

    def op(self, E, fn, reads=(), writes=(), inc=True):
        self._need(E, self._deps(reads, writes))
        ins = fn(E.e)
        self.ninstr += 1
        if inc:
            E.sem.count += 1
            ins.then_inc(E.sem.h, 1)
            self._commit(E.sem, E.sem.count, reads, writes)
        else:
            self._commit(E.sem, E.sem.count + 1, reads, writes)
        return ins

    def dma(self, Q, out, in_, sb_tk, reads=(), writes=(), **kw):
        self._need(Q, self._deps(reads, writes))
        if sb_tk.dsem is None:
            sb_tk.dsem = self.get_dsem()
        S = sb_tk.dsem
        S.count += 16
        Q.e.dma_start(out=out, in_=in_, **kw).then_inc(S.h, 16)
        self.ninstr += 1
        for t in writes:
            if t is sb_tk:
                t.w = {S: S.count}
                t.r = {}
            else:
                t.w[S] = S.count
        for t in reads:
            if t.r.get(S, 0) < S.count:
                t.r[S] = S.count

    def release(self, tks):
        for t in tks:
            if t.dsem is not None:
                self.sem_free.append(t.dsem)
                t.dsem = None

    def barrier(self):
        tok = {E.sem: E.sem.count for E in self.engs}
        for S in self.dma_sems:
            tok[S] = S.count
        for E in self.engs:
            self._need(E, tok)


class Pool_:
    def __init__(self, tiles):
        self.tiles = [(t, Tk()) for t in tiles]
        self.i = 0

    def next(self):
        t = self.tiles[self.i % len(self.tiles)]
        self.i += 1
        return t

    def tks(self):
        return [t[1] for t in self.tiles]


class Phase:
    def __init__(self, cx, name):
        self.cx, self.name = cx, name
        self.st = contextlib.ExitStack()
        self.tks = []
        self.n = 0

    def sb(self, shape, dt):
        self.n += 1
        return self.st.enter_context(self.cx.nc.sbuf_tensor(f"{self.name}_{self.n}", list(shape), dt))

    def sbt(self, shape, dt):
        tk = Tk()
        self.tks.append(tk)
        return self.sb(shape, dt), tk

    def pool(self, n, shape, dt):
        p = Pool_([self.sb(shape, dt) for _ in range(n)])
        self.tks += p.tks()
        return p

    def close(self):
        self.cx.barrier()
        self.cx.release(self.tks)
        self.st.close()


def bc(ap, shape, axis):
    return ap.unsqueeze(axis).to_broadcast(list(shape))


def build(N, depth, debug=False, stages=("p1", "fox")):
    nc = bass.Bass("TRN2", target_bir_lowering=False)
    cx = Ctx(nc)
    NT = N // TT
    NK = N // 128
    NCH = N // HC

    def din(name, shape, dt=F32):
        return nc.dram_tensor(name, list(shape), dt, kind="ExternalInput").ap()

    dbg_out = {}

    def dscr(name, shape, dt):
        if debug:
            t = nc.dram_tensor(name, list(shape), dt, kind="ExternalOutput").ap()
            dbg_out[name] = t
            return t
        return nc.dram_tensor(name, list(shape), dt).ap()

    xT_in = din("xT", [D, N])
    memT_in = din("memT", [D, NMEM])
    consts = din("consts", [128, 6, 128])
    nvec_in = din("nvec", [128, 80])
    jv_in = din("jv", [128, TT])
    m16_in = din("m16", [128, 8])
    P = {}
    pshapes = dict(
        norm_mix=[DEPTH, D], w_in=[DEPTH, D, 4098], fox_fbias=[DEPTH, 2], fox_qnorm=[DEPTH, 64],
        fox_knorm=[DEPTH, 64], s5_a_re=[DEPTH, 8, 64], s5_a_im=[DEPTH, 8, 64],
        s5_b_re=[DEPTH, 8, 64, 16], s5_b_im=[DEPTH, 8, 64, 16], s5_c_re=[DEPTH, 8, 16, 64],
        s5_c_im=[DEPTH, 8, 16, 64], s5_d=[DEPTH, 128], s5_log_dt=[DEPTH, 8],
        s5_w_glu=[DEPTH, 512, 512], s5_b_glu=[DEPTH, 512], hg_lb=[DEPTH, 128], hg_onorm=[DEPTH, 128],
        w_branch=[DEPTH, 1536, D], w_out=[DEPTH, D, D], norm_x=[DEPTH, D], norm_mem=[DEPTH, D],
        xq=[DEPTH, D, D], xk=[DEPTH, D, D], xv=[DEPTH, D, D], xo=[DEPTH, D, D],
        x_qnorm=[DEPTH, 256], x_knorm=[DEPTH, 256], norm_ffn=[DEPTH, D],
        w_gate_up=[DEPTH, D, 2 * DFF], w_down=[DEPTH, DFF, D])
    for k, s in pshapes.items():
        P[k] = din(k, s)
    out_T = nc.dram_tensor("outT", [D, N], F32, kind="ExternalOutput").ap()

    SP, PL, PE, ACT, DVE = cx.sp, cx.pool, cx.pe, cx.act, cx.dve
    NOTK = Tk()

    G = Phase(cx, "g")
    cst_f = G.sb([128, 6, 128], F32)
    cst_b = G.sb([128, 6, 128], BF16)
    nvec = G.sb([128, 80], F32)
    lb_all = G.sb([128, 1, DEPTH], F32)
    c_tk = Tk()
    cx.dma(SP, cst_f[:], consts, c_tk, writes=[c_tk])
    cx.dma(SP, nvec[:], nvec_in, c_tk, writes=[c_tk])
    cx.op(DVE, lambda e: e.tensor_copy(out=cst_b[:], in_=cst_f[:]), reads=[c_tk], writes=[c_tk])
    ident_b, tri_b, blk64_b, ones_b = cst_b[:, 0, :], cst_b[:, 1, :], cst_b[:, 2, :], cst_b[:, 4, :]
    ident_f, swap_f, ones_f = cst_f[:, 0, :], cst_f[:, 3, :], cst_f[:, 4, :]
    psb = [cx.st.enter_context(nc.psum_tensor(f"ps{i}", [128, 512], F32)) for i in range(8)]
    PS = Pool_(psb[2:8])
    PSA = Pool_(psb[0:2])

    with contextlib.nullcontext():
        lbt = G.sb([128, 1, DEPTH], F32)
        lbs = G.sb([128, 1], F32)
        for l_ in range(DEPTH):
            cx.dma(SP, lbt[:, :, l_], P["hg_lb"][l_].rearrange("(h p) -> p h", p=128), c_tk, writes=[c_tk],
                   allow_slow_non_contiguous=True)
        cx.op(ACT, lambda e: e.activation(out=lbt[:], in_=lbt[:], func=AF.Exp), reads=[c_tk], writes=[c_tk])
        cx.op(DVE, lambda e: e.tensor_reduce(out=lbs[:], in_=lbt[:], axis=AX.X, op=ALU.add), reads=[c_tk], writes=[c_tk])
        cx.op(DVE, lambda e: e.reciprocal(out=lbs[:], in_=lbs[:]), reads=[c_tk], writes=[c_tk])
        cx.op(DVE, lambda e: e.tensor_tensor(out=lbt[:], in0=lbt[:], in1=bc(lbs[:], [128, 1, DEPTH], 2), op=ALU.mult),
              reads=[c_tk], writes=[c_tk])
        cx.op(DVE, lambda e: e.memset(lb_all[:, :, 0:1], 0.0), writes=[c_tk])
        for l in range(1, DEPTH):
            cx.op(DVE, lambda e, l=l: e.tensor_tensor(out=lb_all[:, :, l:l + 1], in0=lb_all[:, :, l - 1:l],
                                                     in1=lbt[:, :, l:l + 1], op=ALU.add), reads=[c_tk], writes=[c_tk])

    def mm_group(out_ap, pairs, reads, out_tk):
        n = len(pairs)
        for i, (lt, rh) in enumerate(pairs):
            cx.op(PE, lambda e, lt=lt, rh=rh, i=i: e.matmul(out_ap, lhsT=lt, rhs=rh, start=(i == 0), stop=(i == n - 1)),
                  reads=reads, writes=[out_tk], inc=(i == n - 1))

    def load_w(ph, src_rows_ap, nk, c0, c1, gain=None, stg_pool=None, w_out=None, w_tk=None, wc0=0):
        CH = 512
        for cc in range(c0, c1, CH):
            ce = min(c1, cc + CH)
            for k0 in range(0, nk, 8):
                k1 = min(nk, k0 + 8)
                stg, stk = stg_pool.next()
                cx.dma(SP, stg[:, 0:k1 - k0, 0:ce - cc],
                       src_rows_ap.rearrange("(kt p) c -> p kt c", p=128)[:, k0:k1, cc:ce], stk, writes=[stk])
                for kt in range(k0, k1):
                    dst = w_out[:, kt, wc0 + cc - c0: wc0 + ce - c0]
                    if gain is not None:
                        cx.op(ACT, lambda e, dst=dst, kt=kt, stg=stg, k0=k0, ce=ce, cc=cc: e.activation(
                            out=dst, in_=stg[:, kt - k0, 0:ce - cc], func=AF.Copy, scale=gain[0][:, kt:kt + 1]),
                            reads=[stk, gain[1]], writes=[w_tk])
                    else:
                        cx.op(PL, lambda e, dst=dst, kt=kt, stg=stg, k0=k0, ce=ce, cc=cc: e.tensor_copy(
                            out=dst, in_=stg[:, kt - k0, 0:ce - cc]), reads=[stk], writes=[w_tk])

    def load_vec(ph, src1d, nk, tk):
        t = ph.sb([128, nk], F32)
        cx.dma(SP, t[:], src1d.rearrange("(kt p) -> p kt", p=128), tk, writes=[tk], allow_slow_non_contiguous=True)
        return t

    def rms_rstd(ph, xt, x_tk, nkt, width, sq_pool, rs_pool):
        sq, sqk = sq_pool.next()
        cx.op(ACT, lambda e: e.activation(out=sq[:, 0:nkt, 0:width], in_=xt, func=AF.Square), reads=[x_tk], writes=[sqk])
        pb, pk = PS.next()
        mm_group(pb[:, 0:width], [(ones_b, sq[:, k, 0:width]) for k in range(nkt)], [sqk, c_tk], pk)
        rs, rk = rs_pool.next()
        cx.op(ACT, lambda e: e.activation(out=rs[:, 0:width], in_=pb[:, 0:width], func=AF.Sqrt, scale=1.0 / (nkt * 128), bias=eps_t[:, 0:1]),
              reads=[pk, c_tk], writes=[rk])
        cx.op(DVE, lambda e: e.reciprocal(out=rs[:, 0:width], in_=rs[:, 0:width]), reads=[rk], writes=[rk])
        return rs, rk

    eps_t = G.sb([128, 1], F32)
    cx.op(DVE, lambda e: e.memset(eps_t[:], EPS), writes=[c_tk])

    def mk_scratch(l):
        S = {}
        S["qT"] = dscr(f"qT{l}", [128, N], BF16)
        S["kT"] = dscr(f"kT{l}", [128, N], BF16)
        S["vP"] = dscr(f"vP{l}", [2, 128, NK, 64], BF16)
        S["lfT"] = dscr(f"lfT{l}", [2, N], F32)
        S["aug"] = dscr(f"aug{l}", [2, 6, 2, N], BF16)
        S["uT"] = dscr(f"uT{l}", [128, N], BF16)
        S["hqT"] = dscr(f"hqT{l}", [128, N], BF16)
        S["lfhT"] = dscr(f"lfhT{l}", [128, N], F32)
        S["khT"] = dscr(f"khT{l}", [128, N], BF16)
        S["hvP"] = dscr(f"hvP{l}", [1, 64, NCH, 128], BF16)
        S["hgT"] = dscr(f"hgT{l}", [128, N], BF16)
        S["ysrc"] = nc.dram_tensor(f"ysrc{l}", [N // 256, 384, 256], BF16).ap()
        S["ydst"] = nc.dram_tensor(f"ydst{l}", [N // 256, 1536, 256], BF16).ap()
        S["x1"] = dscr(f"x1_{l}", [D, N], F32)
        S["x2"] = dscr(f"x2_{l}", [D, N], F32)
        S["x3"] = out_T if l == depth - 1 else dscr(f"x3_{l}", [D, N], F32)
        S["tk"] = {k: Tk() for k in list(S.keys())}
        return S

    def phase1(l, xT, x_tk, S):
        ph = Phase(cx, f"p1_{l}")
        tk = S["tk"]
        W, w_tk = ph.sbt([128, 8, 1026], BF16)
        stg_pool = ph.pool(2, [128, 8, 512], F32)
        g_tk = Tk()
        ph.tks.append(g_tk)
        g1 = load_vec(ph, P["norm_mix"][l], 8, g_tk)
        load_w(ph, P["w_in"][l], 8, 0, 1026, gain=(g1, g_tk), stg_pool=stg_pool, w_out=W, w_tk=w_tk)
        gqk = ph.sb([128, 2], F32)
        for hf in range(2):
            cx.dma(SP, gqk[hf * 64:(hf + 1) * 64, 0:1], P["fox_qnorm"][l].rearrange("(p o) -> p o", o=1), g_tk, writes=[g_tk])
            cx.dma(SP, gqk[hf * 64:(hf + 1) * 64, 1:2], P["fox_knorm"][l].rearrange("(p o) -> p o", o=1), g_tk, writes=[g_tk])
        cx.op(DVE, lambda e: e.tensor_scalar(out=gqk[:, 0:1], in0=gqk[:, 0:1], scalar1=0.125, scalar2=None, op0=ALU.mult),
              reads=[g_tk], writes=[g_tk])
        nfb = ph.sb([2, 1], F32)
        cx.dma(SP, nfb[:], P["fox_fbias"][l].rearrange("(p o) -> p o", o=1), g_tk, writes=[g_tk])
        cx.op(DVE, lambda e: e.tensor_scalar(out=nfb[:], in0=nfb[:], scalar1=-1.0, scalar2=None, op0=ALU.mult), reads=[g_tk], writes=[g_tk])
        oml = ph.sb([128, 1], F32)
        noml = ph.sb([128, 1], F32)
        cx.op(DVE, lambda e: e.tensor_scalar(out=oml[:], in0=lb_all[:, :, l], scalar1=-1.0, scalar2=1.0, op0=ALU.mult, op1=ALU.add),
              reads=[c_tk], writes=[g_tk])
        cx.op(DVE, lambda e: e.tensor_scalar(out=noml[:], in0=oml[:], scalar1=-1.0, scalar2=None, op0=ALU.mult), reads=[g_tk], writes=[g_tk])

        x_pool = ph.pool(2, [128, 8, TT], F32)
        sq_pool = ph.pool(1, [128, 8, TT], BF16)
        h_pool = ph.pool(2, [128, 8, TT], BF16)
        rs_pool = ph.pool(2, [128, TT], F32)
        evb = ph.pool(4, [128, TT], BF16)
        evf = ph.pool(3, [128, TT], F32)
        xr = xT.rearrange("(kt p) n -> p kt n", p=128)

        def ld(i):
            xt, xk = x_pool.next()
            cx.dma(SP, xt[:], xr[:, :, i * TT:(i + 1) * TT], xk, reads=[x_tk], writes=[xk])
            return xt, xk

        def prep(xt, xk):
            rs, rk = rms_rstd(ph, xt[:], xk, 8, TT, sq_pool, rs_pool)
            h_, hk_ = h_pool.next()
            cx.op(DVE, lambda e: e.tensor_tensor(out=h_[:], in0=xt[:], in1=bc(rs[:], [128, 8, TT], 1), op=ALU.mult),
                  reads=[xk, rk], writes=[hk_])
            return h_, hk_

        nxt = ld(0)
        nxt_h = prep(nxt[0], nxt[1])
        for i in range(NT):
            xt, xk = nxt
            h, hk = nxt_h
            nxt = ld(i + 1) if i + 1 < NT else None
            sl = slice(i * TT, (i + 1) * TT)

            def proj(c0, m):
                pb, pk = PS.next()
                mm_group(pb[0:m, :], [(W[:, k, c0:c0 + m], h[:, k, :]) for k in range(8)], [w_tk, hk], pk)
                return pb, pk

            for ct in range(2):
                pb, pk = proj(ct * 128, 128)
                sqh, sk = evb.next()
                cx.op(ACT, lambda e: e.activation(out=sqh[:], in_=pb[:], func=AF.Square), reads=[pk], writes=[sk])
                pb2, pk2 = PS.next()
                mm_group(pb2[:], [(blk64_b, sqh[:])], [sk, c_tk], pk2)
                rh, rhk = evf.next()
                cx.op(ACT, lambda e: e.activation(out=rh[:], in_=pb2[:], func=AF.Sqrt, scale=1.0 / 64, bias=eps_t[:, 0:1]),
                      reads=[pk2, c_tk], writes=[rhk])
                cx.op(DVE, lambda e: e.reciprocal(out=rh[:], in_=rh[:]), reads=[rhk], writes=[rhk])
                qn, qk_ = evb.next()
                gcol = gqk[:, 0:1] if ct < 1 else gqk[:, 1:2]
                cx.op(DVE, lambda e: e.scalar_tensor_tensor(out=qn[:], in0=pb[:], scalar=gcol, in1=rh[:], op0=ALU.mult, op1=ALU.mult),
                      reads=[pk, rhk, g_tk], writes=[qk_])
                dst, dk = (S["qT"], tk["qT"]) if ct < 1 else (S["kT"], tk["kT"])
                r0 = 0
                cx.dma(PL, dst[r0:r0 + 128, sl], qn[:], qk_, reads=[qk_], writes=[dk])
            if nxt is not None:
                nxt_h = prep(nxt[0], nxt[1])
            pb, pk = proj(384, 2)
            ef, ek = evf.next()
            cx.op(ACT, lambda e: e.activation(out=ef[0:2, :], in_=pb[0:2, :], func=AF.Exp, scale=-1.0, bias=nfb[:, 0:1]),
                  reads=[pk, g_tk], writes=[ek])
            cx.op(ACT, lambda e: e.activation(out=ef[0:2, :], in_=ef[0:2, :], func=AF.Ln, bias=1.0), reads=[ek], writes=[ek])
            cx.op(DVE, lambda e: e.tensor_scalar(out=ef[0:2, :], in0=ef[0:2, :], scalar1=-1.0, scalar2=None, op0=ALU.mult),
                  reads=[ek], writes=[ek])
            cx.dma(PL, S["lfT"][:, sl], ef[0:2, :], ek, reads=[ek], writes=[tk["lfT"]])
            for (c0, name, fn) in ((386, "uT", AF.Copy), (514, "hqT", AF.Silu), (898, "hgT", AF.Silu)):
                for ct in range(1):
                    pb, pk = proj(c0 + ct * 128, 128)
                    ob, ok = evb.next()
                    cx.op(ACT, lambda e, fn=fn: e.activation(out=ob[:], in_=pb[:], func=fn), reads=[pk], writes=[ok])
                    cx.dma(PL, S[name][ct * 128:(ct + 1) * 128, sl], ob[:], ok, reads=[ok], writes=[tk[name]])
            for ct in range(1):
                pb, pk = proj(642 + ct * 128, 128)
                sg, sgk = evf.next()
                cx.op(ACT, lambda e: e.activation(out=sg[:], in_=pb[:], func=AF.Sigmoid), reads=[pk], writes=[sgk])
                lf, lfk = evf.next()
                cx.op(ACT, lambda e, ct=ct: e.activation(out=lf[:], in_=sg[:], func=AF.Ln, scale=oml[:, ct:ct + 1], bias=lb_all[:, ct, l:l + 1]),
                      reads=[sgk, g_tk, c_tk], writes=[lfk])
                cx.dma(PL, S["lfhT"][ct * 128:(ct + 1) * 128, sl], lf[:], lfk, reads=[lfk], writes=[tk["lfhT"]])
                kh, khk = evb.next()
                cx.op(DVE, lambda e, ct=ct: e.tensor_scalar(out=kh[:], in0=sg[:], scalar1=noml[:, ct:ct + 1], scalar2=oml[:, ct:ct + 1],
                                                           op0=ALU.mult, op1=ALU.add), reads=[sgk, g_tk], writes=[khk])
                cx.dma(PL, S["khT"][ct * 128:(ct + 1) * 128, sl], kh[:], khk, reads=[khk], writes=[tk["khT"]])
            for ts in range(4):
                ktg = i * 4 + ts
                for which in range(2):
                    c0 = 256 if which == 0 else 770
                    pb, pk = PS.next()
                    mm_group(pb[:, 0:128], [(h[:, k, ts * 128:(ts + 1) * 128], W[:, k, c0:c0 + 128]) for k in range(8)], [w_tk, hk], pk)
                    ob, ok = evb.next()
                    cx.op(ACT, lambda e: e.activation(out=ob[:, 0:128], in_=pb[:, 0:128], func=AF.Copy), reads=[pk], writes=[ok])
                    if which == 0:
                        cx.dma(PL, S["vP"][:, :, ktg, :].rearrange("h p d -> p h d"), ob[:, 0:128].rearrange("p (h d) -> p h d", h=2),
                               ok, reads=[ok], writes=[tk["vP"]])
                    else:
                        for hf in range(2):
                            cx.dma(PL, S["hvP"][:, :, ktg * 2 + hf, :].rearrange("h p d -> p h d"),
                                   ob[hf * 64:(hf + 1) * 64, 0:128].rearrange("p (h d) -> p h d", h=1), ok, reads=[ok], writes=[tk["hvP"]])
        ph.close()

    def phase_fox(l, S):
        ph = Phase(cx, f"fx_{l}")
        tk = S["tk"]
        SEG = N // 64
        lf, lk = ph.sbt([128, SEG], F32)
        cc, ck = ph.sbt([128, SEG], F32)
        r1, r1k = ph.sbt([128, SEG], F32)
        spl = ph.sb([128, 2, 3, SEG], BF16)
        onb = ph.sb([128, SEG], BF16)
        sk_ = Tk(); ph.tks.append(sk_)
        cx.dma(SP, lf[:], S["lfT"].rearrange("h (s j) -> (h s) j", s=64), lk, reads=[tk["lfT"]], writes=[lk])
        cx.op(DVE, lambda e: e.memset(r1[:], 1.0), writes=[r1k])
        cx.op(DVE, lambda e: e.memset(onb[:], 1.0), writes=[sk_])
        cx.op(DVE, lambda e: e.tensor_tensor_scan(out=cc[:], data0=r1[:], data1=lf[:], initial=0.0, op0=ALU.mult, op1=ALU.add),
              reads=[lk, r1k], writes=[ck])
        pb, pk = PS.next()
        mm_group(pb[:, 0:2], [(cst_f[:, 5, :], cc[:, SEG - 2:SEG])], [ck, c_tk], pk)
        off = ph.sb([128, 1], F32)
        cx.op(DVE, lambda e: e.tensor_copy(out=off[:], in_=pb[:, 1:2]), reads=[pk], writes=[lk])
        cx.op(DVE, lambda e: e.tensor_scalar(out=cc[:], in0=cc[:], scalar1=off[:, 0:1], scalar2=None, op0=ALU.add),
              reads=[ck, lk], writes=[ck])
        cx.op(DVE, lambda e: e.tensor_copy(out=spl[:, 0, 0, :], in_=cc[:]), reads=[ck], writes=[sk_])
        cx.op(DVE, lambda e: e.tensor_tensor(out=r1[:], in0=cc[:], in1=spl[:, 0, 0, :], op=ALU.subtract), reads=[ck, sk_], writes=[r1k])
        cx.op(DVE, lambda e: e.tensor_copy(out=spl[:, 0, 1, :], in_=r1[:]), reads=[r1k], writes=[sk_])
        cx.op(DVE, lambda e: e.tensor_tensor(out=r1[:], in0=r1[:], in1=spl[:, 0, 1, :], op=ALU.subtract), reads=[r1k, sk_], writes=[r1k])
        cx.op(DVE, lambda e: e.tensor_copy(out=spl[:, 0, 2, :], in_=r1[:]), reads=[r1k], writes=[sk_])
        cx.op(DVE, lambda e: e.tensor_scalar(out=spl[:, 1, :, :], in0=spl[:, 0, :, :], scalar1=-1.0, scalar2=None, op0=ALU.mult),
              reads=[sk_], writes=[sk_])
        for j in range(3):
            for (side, row, src) in ((0, j, onb[:]), (0, 3 + j, spl[:, 1, j, :]), (1, j, spl[:, 0, j, :]), (1, 3 + j, onb[:])):
                cx.dma(PL, S["aug"][side, row].rearrange("h (s j) -> (h s) j", s=64), src, sk_, reads=[sk_], writes=[tk["aug"]])

        QT = ph.pool(1, [70, N], BF16)
        KT = ph.pool(1, [70, N], BF16)
        VP = ph.pool(1, [128, NK, 65], BF16)
        for (vt, vk) in VP.tiles:
            cx.op(PL, lambda e, vt=vt: e.memset(vt[:, :, 64:65], 1.0), writes=[vk])
        pt_pool = ph.pool(3, [128, TT], BF16)
        rr_pool = ph.pool(2, [65, TT], F32)
        bc_pool = ph.pool(2, [64, TT], F32)
        y_pool = ph.pool(2, [64, TT], BF16)

        def ldh(hd):
            q, qk = QT.next(); k, kk = KT.next(); v, vk = VP.next()
            cx.dma(SP, q[0:64, :], S["qT"][hd * 64:(hd + 1) * 64, :], qk, reads=[tk["qT"]], writes=[qk])
            cx.dma(SP, q[64:70, :], S["aug"][1, :, hd, :], qk, reads=[tk["aug"]], writes=[qk])
            cx.dma(SP, k[0:64, :], S["kT"][hd * 64:(hd + 1) * 64, :], kk, reads=[tk["kT"]], writes=[kk])
            cx.dma(SP, k[64:70, :], S["aug"][0, :, hd, :], kk, reads=[tk["aug"]], writes=[kk])
            cx.dma(SP, v[:, :, 0:64], S["vP"][hd], vk, reads=[tk["vP"]], writes=[vk])
            return (q, qk, k, kk, v, vk)

        for hd in range(2):
            q, qk, k, kk, v, vk = ldh(hd)
            for qt in range(NT):
                ob, ok = PSA.next()
                nkb = 4 * qt + 4
                for kb in range(nkb):
                    r = kb - 4 * qt
                    f0 = 128 * r if r > 0 else 0
                    w = TT - f0
                    sb_, sk2 = PS.next()
                    mm_group(sb_[:, 0:w], [(k[0:70, kb * 128:(kb + 1) * 128], q[0:70, qt * TT + f0:(qt + 1) * TT])], [qk, kk], sk2)
                    pt, ptk = pt_pool.next()
                    cx.op(ACT, lambda e, pt=pt, sb_=sb_, w=w: e.activation(out=pt[:, 0:w], in_=sb_[:, 0:w], func=AF.Exp), reads=[sk2], writes=[ptk])
                    if r >= 0:
                        cx.op(DVE, lambda e, pt=pt: e.tensor_tensor(out=pt[:, 0:128], in0=pt[:, 0:128], in1=tri_b, op=ALU.mult),
                              reads=[ptk, c_tk], writes=[ptk])
                    cx.op(PE, lambda e, pt=pt, w=w, f0=f0, kb=kb, ob=ob, v=v, nkb=nkb: e.matmul(
                        ob[0:65, f0:TT], lhsT=v[:, kb, 0:65], rhs=pt[:, 0:w], start=(kb == 0), stop=(kb == nkb - 1)),
                        reads=[ptk, vk], writes=[ok], inc=True)
                rr, rrk = rr_pool.next()
                cx.op(DVE, lambda e: e.reciprocal(out=rr[64:65, :], in_=ob[64:65, :]), reads=[ok], writes=[rrk])
                bb, bk = PS.next()
                mm_group(bb[0:64, :], [(ones_f[64:65, 0:64], rr[64:65, :])], [rrk, c_tk], bk)
                bs, bsk = bc_pool.next()
                cx.op(ACT, lambda e: e.activation(out=bs[:], in_=bb[0:64, :], func=AF.Copy), reads=[bk], writes=[bsk])
                y, yk = y_pool.next()
                cx.op(DVE, lambda e: e.tensor_tensor(out=y[:], in0=ob[0:64, :], in1=bs[:], op=ALU.mult), reads=[ok, bsk], writes=[yk])
                cx.dma(PL, S["ysrc"][2 * qt:2 * qt + 2, hd * 64:(hd + 1) * 64, :].rearrange("c p n -> p c n"), y[:].rearrange("p (c n) -> p c n", c=2), yk, reads=[yk], writes=[tk["ysrc"]])
        ph.close()

    PI = float(np.pi)

    def phase_s5(l, S):
        ph = Phase(cx, f"s5_{l}")
        tk = S["tk"]
        t_tk = Tk(); ph.tks.append(t_tk)
        tmpi = ph.sb([128, TT], mybir.dt.int32)

        def red_pi(x, w):
            kf = tmpf[:, 0:w]
            cx.op(DVE, lambda e: e.tensor_scalar(out=kf, in0=x, scalar1=1.0 / (2 * PI), scalar2=0.5, op0=ALU.mult, op1=ALU.add),
                  reads=[t_tk], writes=[t_tk])
            cx.op(DVE, lambda e: e.tensor_copy(out=tmpi[:, 0:w], in_=kf), reads=[t_tk], writes=[t_tk])
            cx.op(DVE, lambda e: e.tensor_copy(out=kf, in_=tmpi[:, 0:w]), reads=[t_tk], writes=[t_tk])
            cx.op(DVE, lambda e: e.scalar_tensor_tensor(out=x, in0=kf, scalar=-6.28125, in1=x, op0=ALU.mult, op1=ALU.add),
                  reads=[t_tk], writes=[t_tk])
            cx.op(DVE, lambda e: e.scalar_tensor_tensor(out=x, in0=kf, scalar=-0.0019353071795864769, in1=x, op0=ALU.mult, op1=ALU.add),
                  reads=[t_tk], writes=[t_tk])
            fix_pi(x, w)

        def fix_pi(x, w):
            m = tmpf[:, 0:w]
            cx.op(DVE, lambda e: e.tensor_scalar(out=m, in0=x, scalar1=-PI, scalar2=None, op0=ALU.is_lt), reads=[t_tk], writes=[t_tk])
            cx.op(DVE, lambda e: e.scalar_tensor_tensor(out=x, in0=m, scalar=2 * PI, in1=x, op0=ALU.mult, op1=ALU.add), reads=[t_tk], writes=[t_tk])
            cx.op(DVE, lambda e: e.tensor_scalar(out=m, in0=x, scalar1=PI, scalar2=None, op0=ALU.is_gt), reads=[t_tk], writes=[t_tk])
            cx.op(DVE, lambda e: e.scalar_tensor_tensor(out=x, in0=m, scalar=-2 * PI, in1=x, op0=ALU.mult, op1=ALU.add), reads=[t_tk], writes=[t_tk])
            cx.op(DVE, lambda e: e.tensor_scalar(out=x, in0=x, scalar1=PI, scalar2=-PI, op0=ALU.min, op1=ALU.max), reads=[t_tk], writes=[t_tk])

        def sincos(x, w, sin_out, cos_out):
            red_pi(x, w)
            cx.op(ACT, lambda e: e.activation(out=sin_out, in_=x, func=AF.Sin), reads=[t_tk], writes=[t_tk])
            cx.op(DVE, lambda e: e.tensor_scalar(out=x, in0=x, scalar1=PI / 2, scalar2=None, op0=ALU.add), reads=[t_tk], writes=[t_tk])
            fix_pi(x, w)
            cx.op(ACT, lambda e: e.activation(out=cos_out, in_=x, func=AF.Sin), reads=[t_tk], writes=[t_tk])

        tmpf = ph.sb([128, TT], F32)
        ang = ph.sb([128, TT], F32)
        jv = ph.sb([128, TT], F32)
        m16 = ph.sb([128, 8], F32)
        cx.dma(SP, jv[:], jv_in, t_tk, writes=[t_tk])
        cx.dma(SP, m16[:], m16_in, t_tk, writes=[t_tk])
        anat = ph.sb([8, 2, 128], F32)
        for hf in range(2):
            cx.dma(SP, anat[:, 0, hf * 64:(hf + 1) * 64], P["s5_a_re"][l], t_tk, writes=[t_tk])
            cx.dma(SP, anat[:, 1, hf * 64:(hf + 1) * 64], P["s5_a_im"][l], t_tk, writes=[t_tk])
        dt_pr = ph.sb([128, 8], F32)
        cx.dma(SP, dt_pr[:], P["s5_log_dt"][l:l + 1, :].partition_broadcast(128).rearrange("p o g -> p (o g)"), t_tk, writes=[t_tk])
        cx.op(ACT, lambda e: e.activation(out=dt_pr[:], in_=dt_pr[:], func=AF.Exp), reads=[t_tk], writes=[t_tk])
        mag_pr = ph.sb([128, 8], F32)
        th_pr = ph.sb([128, 8], F32)
        c512 = ph.sb([128, 8], F32)
        s512 = ph.sb([128, 8], F32)
        for which, dst in ((0, mag_pr), (1, th_pr)):
            pb, pk = PS.next()
            cx.op(PE, lambda e, which=which: e.transpose(pb[:, 0:8], anat[:, which, :], ident_f[0:8, 0:8]), reads=[t_tk, c_tk], writes=[pk])
            cx.op(DVE, lambda e, dst=dst: e.tensor_tensor(out=dst[:], in0=pb[:, 0:8], in1=dt_pr[:], op=ALU.mult), reads=[pk, t_tk], writes=[t_tk])
        cx.op(ACT, lambda e: e.activation(out=mag_pr[:], in_=mag_pr[:], func=AF.Exp), reads=[t_tk], writes=[t_tk])
        cx.op(DVE, lambda e: e.tensor_scalar(out=ang[:, 0:8], in0=th_pr[:], scalar1=float(TT), scalar2=None, op0=ALU.mult), reads=[t_tk], writes=[t_tk])
        sincos(ang[:, 0:8], 8, s512[:], c512[:])
        cosT = ph.sb([128, 8, TT], BF16)
        sinT = ph.sb([128, 8, TT], BF16)
        for g in range(8):
            cx.op(DVE, lambda e, g=g: e.tensor_scalar(out=ang[:], in0=jv[:], scalar1=th_pr[:, g:g + 1], scalar2=None, op0=ALU.mult),
                  reads=[t_tk], writes=[t_tk])
            sincos(ang[:], TT, sinT[:, g, :], cosT[:, g, :])
        Bst = ph.sb([128, 8, 128], BF16)
        Bsw = ph.sb([128, 8, 128], BF16)
        Cg = ph.sb([128, 8, 128], BF16)
        dsk = load_vec(ph, P["s5_d"][l], 1, t_tk)
        cx.op(PL, lambda e: e.memset(Cg[:], 0.0), writes=[t_tk])
        sm = [ph.sb([128, 64], F32) for _ in range(12)]
        ar_, ai_, mg_, th_, sn_, cs_, nr_, ni_, dn_, cr_, ci_, t_ = sm
        dt_gh = ph.sb([128, 1], F32)
        bnat = ph.sb([64, 2, 128], F32)
        bgh = ph.sb([128, 2, 64], F32)
        ball = ph.sb([128, 2, 128], F32)
        cnat = ph.sb([128, 128], F32)
        call = ph.sb([128, 128], F32)
        for Q in range(1):
            G0 = Q * 8
            for g in range(8):
                cx.dma(SP, ar_[g * 16:(g + 1) * 16, :], P["s5_a_re"][l, G0 + g:G0 + g + 1, :].partition_broadcast(16).rearrange("p o g -> p (o g)"), t_tk, writes=[t_tk])
                cx.dma(SP, ai_[g * 16:(g + 1) * 16, :], P["s5_a_im"][l, G0 + g:G0 + g + 1, :].partition_broadcast(16).rearrange("p o g -> p (o g)"), t_tk, writes=[t_tk])
                cx.dma(SP, dt_gh[g * 16:(g + 1) * 16, :], P["s5_log_dt"][l:l + 1, G0 + g:G0 + g + 1].partition_broadcast(16).rearrange("p o g -> p (o g)"), t_tk, writes=[t_tk])
            cx.op(ACT, lambda e: e.activation(out=dt_gh[:], in_=dt_gh[:], func=AF.Exp), reads=[t_tk], writes=[t_tk])
            cx.op(ACT, lambda e: e.activation(out=mg_[:], in_=ar_[:], func=AF.Exp, scale=dt_gh[:, 0:1]), reads=[t_tk], writes=[t_tk])
            cx.op(DVE, lambda e: e.tensor_scalar(out=th_[:], in0=ai_[:], scalar1=dt_gh[:, 0:1], scalar2=None, op0=ALU.mult), reads=[t_tk], writes=[t_tk])
            sincos(th_[:], 64, sn_[:], cs_[:])
            TTo = lambda o, a, b, op: cx.op(DVE, lambda e: e.tensor_tensor(out=o, in0=a, in1=b, op=op), reads=[t_tk], writes=[t_tk])
            TTo(nr_[:], mg_[:], cs_[:], ALU.mult)
            cx.op(DVE, lambda e: e.tensor_scalar(out=nr_[:], in0=nr_[:], scalar1=-1.0, scalar2=None, op0=ALU.add), reads=[t_tk], writes=[t_tk])
            TTo(ni_[:], mg_[:], sn_[:], ALU.mult)
            TTo(dn_[:], ar_[:], ar_[:], ALU.mult)
            TTo(t_[:], ai_[:], ai_[:], ALU.mult)
            TTo(dn_[:], dn_[:], t_[:], ALU.add)
            cx.op(DVE, lambda e: e.reciprocal(out=dn_[:], in_=dn_[:]), reads=[t_tk], writes=[t_tk])
            TTo(cr_[:], nr_[:], ar_[:], ALU.mult)
            TTo(t_[:], ni_[:], ai_[:], ALU.mult)
            TTo(cr_[:], cr_[:], t_[:], ALU.add)
            TTo(cr_[:], cr_[:], dn_[:], ALU.mult)
            TTo(ci_[:], ni_[:], ar_[:], ALU.mult)
            TTo(t_[:], nr_[:], ai_[:], ALU.mult)
            TTo(ci_[:], ci_[:], t_[:], ALU.subtract)
            TTo(ci_[:], ci_[:], dn_[:], ALU.mult)
            cx.dma(SP, bnat[:, 0, :].rearrange("p (g h) -> p g h", h=16), P["s5_b_re"][l, G0:G0 + 8].rearrange("g p h -> p g h"), t_tk, writes=[t_tk])
            cx.dma(SP, bnat[:, 1, :].rearrange("p (g h) -> p g h", h=16), P["s5_b_im"][l, G0:G0 + 8].rearrange("g p h -> p g h"), t_tk, writes=[t_tk])
            for ri in range(2):
                pb, pk = PS.next()
                cx.op(PE, lambda e, ri=ri, pb=pb: e.transpose(pb[:, 0:64], bnat[:, ri, :], ident_f[0:64, 0:64]), reads=[t_tk, c_tk], writes=[pk])
                cx.op(DVE, lambda e, ri=ri, pb=pb: e.tensor_copy(out=bgh[:, ri, :], in_=pb[:, 0:64]), reads=[pk], writes=[t_tk])
            TTo(ball[:, 0, 0:64], cr_[:], bgh[:, 0, :], ALU.mult)
            TTo(t_[:], ci_[:], bgh[:, 1, :], ALU.mult)
            TTo(ball[:, 0, 0:64], ball[:, 0, 0:64], t_[:], ALU.subtract)
            TTo(ball[:, 0, 64:128], cr_[:], bgh[:, 1, :], ALU.mult)
            TTo(t_[:], ci_[:], bgh[:, 0, :], ALU.mult)
            TTo(ball[:, 0, 64:128], ball[:, 0, 64:128], t_[:], ALU.add)
            cx.op(DVE, lambda e: e.tensor_copy(out=ball[:, 1, 0:64], in_=ball[:, 0, 64:128]), reads=[t_tk], writes=[t_tk])
            cx.op(DVE, lambda e: e.tensor_scalar(out=ball[:, 1, 64:128], in0=ball[:, 0, 0:64], scalar1=-1.0, scalar2=None, op0=ALU.mult),
                  reads=[t_tk], writes=[t_tk])
            for g in range(8):
                cx.op(DVE, lambda e, g=g: e.tensor_scalar(out=Bst[:, G0 + g, :], in0=ball[:, 0, :], scalar1=m16[:, g:g + 1], scalar2=None, op0=ALU.mult),
                      reads=[t_tk], writes=[t_tk])
                cx.op(DVE, lambda e, g=g: e.tensor_scalar(out=Bsw[:, G0 + g, :], in0=ball[:, 1, :], scalar1=m16[:, g:g + 1], scalar2=None, op0=ALU.mult),
                      reads=[t_tk], writes=[t_tk])
            cx.dma(SP, cnat[:, 0:64], P["s5_c_re"][l, G0:G0 + 8].rearrange("g h p -> (g h) p"), t_tk, writes=[t_tk])
            cx.dma(SP, cnat[:, 64:128], P["s5_c_im"][l, G0:G0 + 8].rearrange("g h p -> (g h) p"), t_tk, writes=[t_tk])
            pb, pk = PS.next()
            cx.op(PE, lambda e, pb=pb: e.transpose(pb[:, 0:128], cnat[:], ident_f), reads=[t_tk, c_tk], writes=[pk])
            cx.op(DVE, lambda e, pb=pb: e.tensor_copy(out=call[0:64, :], in_=pb[0:64, 0:128]), reads=[pk], writes=[t_tk])
            cx.op(DVE, lambda e, pb=pb: e.tensor_scalar(out=call[64:128, :], in0=pb[64:128, 0:128], scalar1=-1.0, scalar2=None, op0=ALU.mult),
                  reads=[pk], writes=[t_tk])
            for g in range(8):
                cx.op(DVE, lambda e, g=g: e.tensor_copy(out=Cg[:, G0 + g, g * 16:(g + 1) * 16], in_=call[:, g * 16:(g + 1) * 16]),
                      reads=[t_tk], writes=[t_tk])
        vin = ph.sb([128, 8], F32)
        vsin = ph.sb([128, 8], F32)
        cx.op(DVE, lambda e: e.memset(vin[:], 0.0), writes=[t_tk])
        cx.op(DVE, lambda e: e.memset(vsin[:], 0.0), writes=[t_tk])
        st_tk = Tk(); ph.tks.append(st_tk)
        u_pool = ph.pool(2, [128, 1, TT], BF16)
        f_pool = ph.pool(6, [128, TT], F32)
        v_pool = ph.pool(2, [128, 2, TT], F32)
        h_pool = ph.pool(2, [128, TT], BF16)
        c_pool = ph.pool(2, [128, 2], F32)
        y_pool = ph.pool(2, [128, TT], F32)
        z_pool = ph.pool(2, [128, TT], BF16)
        ur = S["uT"].rearrange("(q p) n -> p q n", p=128)

        def ld(i):
            u, uk = u_pool.next()
            cx.dma(SP, u[:], ur[:, :, i * TT:(i + 1) * TT], uk, reads=[tk["uT"]], writes=[uk])
            return u, uk

        nxt = ld(0)
        for i in range(NT):
            u, uk = nxt
            if i + 1 < NT:
                nxt = ld(i + 1)
            for Q in range(1):
                py, pyk = PSA.next()
                for g in range(8):
                    G_ = Q * 8 + g
                    pd, pdk = PS.next()
                    mm_group(pd[:], [(Bst[:, G_, :], u[:, Q, :])], [t_tk, uk], pdk)
                    pds, pdsk = PS.next()
                    mm_group(pds[:], [(Bsw[:, G_, :], u[:, Q, :])], [t_tk, uk], pdsk)
                    t1, t1k = f_pool.next(); t2, t2k = f_pool.next()
                    cx.op(DVE, lambda e: e.tensor_tensor(out=t1[:], in0=pd[:], in1=cosT[:, G_, :], op=ALU.mult), reads=[pdk, t_tk], writes=[t1k])
                    cx.op(DVE, lambda e: e.tensor_tensor(out=t2[:], in0=pds[:], in1=sinT[:, G_, :], op=ALU.mult), reads=[pdsk, t_tk], writes=[t2k])
                    cx.op(DVE, lambda e: e.tensor_tensor(out=t1[:], in0=t1[:], in1=t2[:], op=ALU.add), reads=[t1k, t2k], writes=[t1k])
                    t3, t3k = f_pool.next(); t4, t4k = f_pool.next()
                    cx.op(DVE, lambda e: e.tensor_tensor(out=t3[:], in0=pds[:], in1=cosT[:, G_, :], op=ALU.mult), reads=[pdsk, t_tk], writes=[t3k])
                    cx.op(DVE, lambda e: e.tensor_tensor(out=t4[:], in0=pd[:], in1=sinT[:, G_, :], op=ALU.mult), reads=[pdk, t_tk], writes=[t4k])
                    cx.op(DVE, lambda e: e.tensor_tensor(out=t3[:], in0=t3[:], in1=t4[:], op=ALU.subtract), reads=[t3k, t4k], writes=[t3k])
                    vv, vvk = v_pool.next()
                    magb = mag_pr[:, G_:G_ + 1].to_broadcast([128, TT])
                    cx.op(DVE, lambda e: e.tensor_tensor_scan(out=vv[:, 0, :], data0=magb, data1=t1[:], initial=vin[:, G_:G_ + 1], op0=ALU.mult, op1=ALU.add),
                          reads=[t1k, t_tk, st_tk], writes=[vvk])
                    cx.op(DVE, lambda e: e.tensor_tensor_scan(out=vv[:, 1, :], data0=magb, data1=t3[:], initial=vsin[:, G_:G_ + 1], op0=ALU.mult, op1=ALU.add),
                          reads=[t3k, t_tk, st_tk], writes=[vvk])
                    cc_, cck = c_pool.next()
                    cx.op(DVE, lambda e: e.tensor_scalar(out=cc_[:, 0:1], in0=vv[:, 1, TT - 1:TT], scalar1=s512[:, G_:G_ + 1], scalar2=None, op0=ALU.mult),
                          reads=[vvk, t_tk], writes=[cck])
                    cx.op(DVE, lambda e: e.tensor_scalar(out=cc_[:, 1:2], in0=vv[:, 0, TT - 1:TT], scalar1=s512[:, G_:G_ + 1], scalar2=None, op0=ALU.mult),
                          reads=[vvk, t_tk], writes=[cck])
                    cx.op(DVE, lambda e: e.scalar_tensor_tensor(out=vin[:, G_:G_ + 1], in0=vv[:, 0, TT - 1:TT], scalar=c512[:, G_:G_ + 1], in1=cc_[:, 0:1],
                                                               op0=ALU.mult, op1=ALU.subtract), reads=[vvk, cck, t_tk], writes=[st_tk])
                    cx.op(DVE, lambda e: e.scalar_tensor_tensor(out=vsin[:, G_:G_ + 1], in0=vv[:, 1, TT - 1:TT], scalar=c512[:, G_:G_ + 1], in1=cc_[:, 1:2],
                                                               op0=ALU.mult, op1=ALU.add), reads=[vvk, cck, t_tk], writes=[st_tk])
                    cx.op(DVE, lambda e: e.tensor_tensor(out=t2[:], in0=vv[:, 0, :], in1=cosT[:, G_, :], op=ALU.mult), reads=[vvk, t_tk], writes=[t2k])
                    cx.op(DVE, lambda e: e.tensor_tensor(out=t4[:], in0=vv[:, 1, :], in1=sinT[:, G_, :], op=ALU.mult), reads=[vvk, t_tk], writes=[t4k])
                    hh, hhk = h_pool.next()
                    cx.op(DVE, lambda e: e.tensor_tensor(out=hh[:], in0=t2[:], in1=t4[:], op=ALU.subtract), reads=[t2k, t4k], writes=[hhk])
                    cx.op(PE, lambda e, g=g: e.matmul(py[:], lhsT=Cg[:, G_, :], rhs=hh[:], start=(g == 0), stop=(g == 7)),
                          reads=[hhk, t_tk], writes=[pyk], inc=True)
                yv, yvk = y_pool.next()
                cx.op(DVE, lambda e: e.scalar_tensor_tensor(out=yv[:], in0=u[:, Q, :], scalar=dsk[:, Q:Q + 1], in1=py[:], op0=ALU.mult, op1=ALU.add),
                      reads=[uk, pyk, t_tk], writes=[yvk])
                g1_, g1k = f_pool.next()
                cx.op(DVE, lambda e: e.tensor_tensor(out=g1_[:], in0=yv[:], in1=yv[:], op=ALU.mult), reads=[yvk], writes=[g1k])
                cx.op(DVE, lambda e: e.tensor_scalar(out=g1_[:], in0=g1_[:], scalar1=0.044715, scalar2=1.0, op0=ALU.mult, op1=ALU.add), reads=[g1k], writes=[g1k])
                cx.op(DVE, lambda e: e.tensor_tensor(out=g1_[:], in0=g1_[:], in1=yv[:], op=ALU.mult), reads=[g1k, yvk], writes=[g1k])
                cx.op(ACT, lambda e: e.activation(out=g1_[:], in_=g1_[:], func=AF.Sigmoid, scale=2.0 * 0.7978845608028654), reads=[g1k], writes=[g1k])
                z, zk = z_pool.next()
                cx.op(DVE, lambda e: e.tensor_tensor(out=z[:], in0=yv[:], in1=g1_[:], op=ALU.mult), reads=[yvk, g1k], writes=[zk])
                cx.dma(PL, S["ysrc"][2 * i:2 * i + 2, 128:256, :].rearrange("c p n -> p c n"), z[:].rearrange("p (c n) -> p c n", c=2), zk, reads=[zk], writes=[tk["ysrc"]])
        ph.close()

    def phase_hg(l, S):
        ph = Phase(cx, f"hg_{l}")
        tk = S["tk"]
        g_tk = Tk(); ph.tks.append(g_tk)
        og = ph.sb([128, 1], F32)
        cx.dma(SP, og[:], P["hg_onorm"][l].rearrange("(p o) -> p o", o=1), g_tk, writes=[g_tk])
        msk = ph.sb([128, TT], F32)
        cx.op(DVE, lambda e: e.memset(msk[:], 1.0), writes=[g_tk])
        cx.op(DVE, lambda e: e.memset(msk[:].rearrange("p (c j) -> p c j", j=HC)[:, :, 0:1], 0.0), writes=[g_tk])
        inb = ph.pool(2, [128, 3, TT], BF16)
        inf = ph.pool(2, [128, TT], F32)
        inv = ph.pool(2, [64, 8, 128], BF16)
        b_pool = ph.pool(2, [128, TT], F32)
        d_pool = ph.pool(2, [128, TT], F32)
        e_pool = ph.pool(3, [128, TT], F32)
        w_pool = ph.pool(2, [128, 4, TT], BF16)
        eb_pool = ph.pool(2, [128, 8], F32)
        am_pool = ph.pool(2, [64, TT], BF16)
        kt_pool = ph.pool(2, [64, 8, 128], BF16)
        stb_pool = ph.pool(2, [128, 9, 128], BF16)
        stf = [ph.sbt([128, 128], F32) for _ in range(2)]
        sq_pool = ph.pool(2, [128, TT], BF16)
        rs_pool = ph.pool(2, [128, TT], F32)
        y_pool = ph.pool(2, [128, TT], BF16)
        nch = TT // HC

        def ld(hd, i):
            ib, ibk = inb.next(); lf, lfk = inf.next(); v, vk = inv.next()
            sl = slice(i * TT, (i + 1) * TT)
            r0 = hd * 128
            cx.dma(SP, ib[:, 0, :], S["hqT"][r0:r0 + 128, sl], ibk, reads=[tk["hqT"]], writes=[ibk])
            cx.dma(SP, ib[:, 1, :], S["khT"][r0:r0 + 128, sl], ibk, reads=[tk["khT"]], writes=[ibk])
            cx.dma(SP, ib[:, 2, :], S["hgT"][r0:r0 + 128, sl], ibk, reads=[tk["hgT"]], writes=[ibk])
            cx.dma(SP, lf[:], S["lfhT"][r0:r0 + 128, sl], lfk, reads=[tk["lfhT"]], writes=[lfk])
            cx.dma(SP, v[:], S["hvP"][hd, :, i * nch:(i + 1) * nch, :], vk, reads=[tk["hvP"]], writes=[vk])
            return ib, ibk, lf, lfk, v, vk

        for hd in range(1):
            cur = 0
            cx.op(DVE, lambda e: e.memset(stf[0][0][:], 0.0), writes=[stf[0][1]])
            stb, stbk = stb_pool.next()
            cx.op(PL, lambda e, stb=stb: e.memset(stb[:, 0, :], 0.0), writes=[stbk])
            nxt = ld(hd, 0)
            for i in range(NT):
                ib, ibk, lf, lfk, v, vk = nxt
                if i + 1 < NT:
                    nxt = ld(hd, i + 1)
                b, bk = b_pool.next()
                cx.op(DVE, lambda e: e.tensor_tensor_scan(out=b[:], data0=msk[:], data1=lf[:], initial=0.0, op0=ALU.mult, op1=ALU.add),
                      reads=[lfk, g_tk], writes=[bk])
                b3 = b[:].rearrange("p (c j) -> p c j", j=HC)
                w, wk = w_pool.next()
                d1, d1k = d_pool.next()
                cx.op(DVE, lambda e: e.tensor_tensor(out=d1[:].rearrange("p (c j) -> p c j", j=HC), in0=b3,
                                                     in1=b3[:, :, HC // 2 - 1:HC // 2].to_broadcast([128, nch, HC]), op=ALU.subtract),
                      reads=[bk], writes=[d1k])
                e1, e1k = e_pool.next()
                cx.op(ACT, lambda e: e.activation(out=e1[:], in_=d1[:], func=AF.Exp), reads=[d1k], writes=[e1k])
                cx.op(DVE, lambda e: e.tensor_tensor(out=w[:, 0, :], in0=ib[:, 0, :], in1=e1[:], op=ALU.mult), reads=[ibk, e1k], writes=[wk])
                e2, e2k = e_pool.next()
                cx.op(ACT, lambda e: e.activation(out=e2[:], in_=d1[:], func=AF.Exp, scale=-1.0), reads=[d1k], writes=[e2k])
                cx.op(DVE, lambda e: e.tensor_tensor(out=w[:, 1, :], in0=ib[:, 1, :], in1=e2[:], op=ALU.mult), reads=[ibk, e2k], writes=[wk])
                e3, e3k = e_pool.next()
                cx.op(ACT, lambda e: e.activation(out=e3[:], in_=b[:], func=AF.Exp), reads=[bk], writes=[e3k])
                cx.op(DVE, lambda e: e.tensor_tensor(out=w[:, 2, :], in0=ib[:, 0, :], in1=e3[:], op=ALU.mult), reads=[ibk, e3k], writes=[wk])
                d2, d2k = d_pool.next()
                cx.op(DVE, lambda e: e.tensor_tensor(out=d2[:].rearrange("p (c j) -> p c j", j=HC), in0=b3,
                                                     in1=b3[:, :, HC - 1:HC].to_broadcast([128, nch, HC]), op=ALU.subtract),
                      reads=[bk], writes=[d2k])
                e4, e4k = e_pool.next()
                cx.op(ACT, lambda e: e.activation(out=e4[:], in_=d2[:], func=AF.Exp, scale=-1.0), reads=[d2k], writes=[e4k])
                cx.op(DVE, lambda e: e.tensor_tensor(out=w[:, 3, :], in0=ib[:, 1, :], in1=e4[:], op=ALU.mult), reads=[ibk, e4k], writes=[wk])
                eb, ebk = eb_pool.next()
                cx.op(ACT, lambda e: e.activation(out=eb[:].unsqueeze(2), in_=b3[:, :, HC - 1:HC], func=AF.Exp), reads=[bk], writes=[ebk])
                pa, pak = PS.next()
                for c in range(nch):
                    cs = slice(c * HC, (c + 1) * HC)
                    cx.op(PE, lambda e, cs=cs: e.matmul(pa[0:HC, cs], lhsT=w[:, 1, cs], rhs=w[:, 0, cs], start=True, stop=True),
                          reads=[wk], writes=[pak], inc=(c == nch - 1))
                ptp, ptk_ = PS.next()
                ptb = ptp[:].bitcast(BF16)
                for c in range(nch):
                    cs = slice(c * HC, (c + 1) * HC)
                    cx.op(PE, lambda e, cs=cs, c=c: e.transpose(ptb[0:HC, c * 128:(c + 1) * 128], w[:, 3, cs], ident_b),
                          reads=[wk, c_tk], writes=[ptk_], inc=(c == nch - 1))
                am, amk = am_pool.next()
                cx.op(DVE, lambda e: e.tensor_tensor(out=am[:].rearrange("p (c j) -> p c j", j=HC),
                                                     in0=pa[0:HC, :].rearrange("p (c j) -> p c j", j=HC),
                                                     in1=tri_b[0:HC, 0:HC].unsqueeze(1).to_broadcast([HC, nch, HC]), op=ALU.mult),
                      reads=[pak, c_tk], writes=[amk])
                kt, ktk = kt_pool.next()
                cx.op(ACT, lambda e: e.activation(out=kt[:].rearrange("p c k -> p (c k)"), in_=ptb[0:HC, 0:nch * 128], func=AF.Copy),
                      reads=[ptk_], writes=[ktk])
                kvs = []
                for half in range(2):
                    pkv, pkvk = PS.next()
                    for c4 in range(4):
                        c = half * 4 + c4
                        cx.op(PE, lambda e, c=c, c4=c4, pkv=pkv: e.matmul(pkv[:, c4 * 128:(c4 + 1) * 128], lhsT=kt[:, c, :], rhs=v[:, c, :],
                                                                        start=True, stop=True),
                              reads=[ktk, vk], writes=[pkvk], inc=(c4 == 3))
                    kvs.append((pkv, pkvk))
                for c in range(nch):
                    pkv, pkvk = kvs[c // 4]
                    src, srck = stf[cur]
                    dst, dstk = stf[1 - cur]
                    cx.op(DVE, lambda e, c=c, pkv=pkv, src=src, dst=dst: e.scalar_tensor_tensor(
                        out=dst[:], in0=src[:], scalar=eb[:, c:c + 1], in1=pkv[:, (c % 4) * 128:(c % 4 + 1) * 128], op0=ALU.mult, op1=ALU.add),
                        reads=[srck, ebk, pkvk], writes=[dstk])
                    cur = 1 - cur
                    if c < nch - 1:
                        cx.op(PL, lambda e, c=c, dst=dst, stb=stb: e.tensor_copy(out=stb[:, c + 1, :], in_=dst[:]), reads=[dstk], writes=[stbk])
                    else:
                        nstb, nstbk = stb_pool.next()
                        cx.op(PL, lambda e, dst=dst, nstb=nstb: e.tensor_copy(out=nstb[:, 0, :], in_=dst[:]), reads=[dstk], writes=[nstbk])
                po, pok = PS.next()
                for c in range(nch):
                    cs = slice(c * HC, (c + 1) * HC)
                    cx.op(PE, lambda e, cs=cs, c=c, stb=stb: e.matmul(po[:, cs], lhsT=stb[:, c, :], rhs=w[:, 2, cs], start=True, stop=False),
                          reads=[stbk, wk], writes=[pok], inc=False)
                    cx.op(PE, lambda e, cs=cs, c=c: e.matmul(po[:, cs], lhsT=v[:, c, :], rhs=am[0:HC, cs], start=False, stop=True),
                          reads=[vk, amk], writes=[pok], inc=(c == nch - 1))
                stb, stbk = nstb, nstbk
                sq, sqk = sq_pool.next()
                cx.op(ACT, lambda e: e.activation(out=sq[:], in_=po[:], func=AF.Square), reads=[pok], writes=[sqk])
                pn, pnk = PS.next()
                mm_group(pn[:], [(ones_b, sq[:])], [sqk, c_tk], pnk)
                rs, rk = rs_pool.next()
                cx.op(ACT, lambda e: e.activation(out=rs[:], in_=pn[:], func=AF.Sqrt, scale=1.0 / 128, bias=eps_t[:, 0:1]),
                      reads=[pnk, c_tk], writes=[rk])
                cx.op(DVE, lambda e: e.reciprocal(out=rs[:], in_=rs[:]), reads=[rk], writes=[rk])
                cx.op(DVE, lambda e: e.scalar_tensor_tensor(out=rs[:], in0=po[:], scalar=og[:, 0:1], in1=rs[:], op0=ALU.mult, op1=ALU.mult),
                      reads=[pok, rk, g_tk], writes=[rk])
                y, yk = y_pool.next()
                cx.op(DVE, lambda e: e.tensor_tensor(out=y[:], in0=rs[:], in1=ib[:, 2, :], op=ALU.mult), reads=[rk, ibk], writes=[yk])
                cx.dma(PL, S["ysrc"][2 * i:2 * i + 2, 256:384, :].rearrange("c p n -> p c n"), y[:].rearrange("p (c n) -> p c n", c=2), yk, reads=[yk], writes=[tk["ysrc"]])
        ph.close()

    def phase3a(l, xT, x_tk, S):
        ph = Phase(cx, f"p3a_{l}")
        tk = S["tk"]
        stg_pool = ph.pool(2, [128, 8, 512], F32)
        g_tk = Tk(); ph.tks.append(g_tk)
        g1 = load_vec(ph, P["norm_mix"][l], 8, g_tk)
        bglu = load_vec(ph, P["s5_b_glu"][l], 4, g_tk)
        Wg, wg_tk = ph.sbt([128, 8, 3072], BF16)
        Wb, wb_tk = ph.sbt([128, 12, 1024], BF16)
        Wo, wo_tk = ph.sbt([128, 8, 1024], BF16)
        Wgl, wgl_tk = ph.sbt([128, 4, 512], BF16)
        load_w(ph, P["w_in"][l], 8, 1026, 4098, gain=(g1, g_tk), stg_pool=stg_pool, w_out=Wg, w_tk=wg_tk)
        load_w(ph, P["w_branch"][l], 12, 0, 1024, stg_pool=stg_pool, w_out=Wb, w_tk=wb_tk)
        load_w(ph, P["w_out"][l], 8, 0, 1024, stg_pool=stg_pool, w_out=Wo, w_tk=wo_tk)
        load_w(ph, P["s5_w_glu"][l], 4, 0, 512, stg_pool=stg_pool, w_out=Wgl, w_tk=wgl_tk)
        TW = 512
        x_pool = stg_pool
        y_pool = ph.pool(2, [128, 12, TW], BF16)
        sq_pool = ph.pool(1, [128, 8, TW], BF16)
        h_pool = ph.pool(2, [128, 8, TW], BF16)
        rs_pool = ph.pool(2, [128, TW], F32)
        mg_pool = ph.pool(1, [128, 8, TW], BF16)
        mgs, mgsk = ph.sbt([128, 4, TW], BF16)
        evf = ph.pool(4, [128, TW], F32)
        xr = xT.rearrange("(kt p) n -> p kt n", p=128)
        x1r = S["x1"].rearrange("(kt p) n -> p kt n", p=128)

        def ld(i):
            xt, xk = x_pool.next()
            cx.dma(SP, xt[:], xr[:, :, i * TW:(i + 1) * TW], xk, reads=[x_tk], writes=[xk])
            yt, yk = y_pool.next()
            for hf_ in range(2):
                for b_ in range(3):
                    cx.dma(SP, yt[:, b_ * 4:(b_ + 1) * 4, hf_ * 256:(hf_ + 1) * 256],
                           S["ydst"][2 * i + hf_].rearrange("(r b p) n -> b p r n", r=4, b=3, p=128)[b_], yk, reads=[tk["ydst"]], writes=[yk])
            return xt, xk, yt, yk

        def prep(xt, xk):
            rs, rk = rms_rstd(ph, xt[:], xk, 8, TW, sq_pool, rs_pool)
            h, hk = h_pool.next()
            cx.op(DVE, lambda e: e.tensor_tensor(out=h[:], in0=xt[:], in1=bc(rs[:], [128, 8, TW], 1), op=ALU.mult),
                  reads=[xk, rk], writes=[hk])
            return h, hk

        nxt = ld(0)
        nxt_h = prep(nxt[0], nxt[1])
        for i in range(N // TW):
            xt, xk, yt, yk = nxt
            h, hk = nxt_h
            nxt = ld(i + 1) if i + 1 < N // TW else None
            for ct in range(4):
                pb, pk = PS.next()
                mm_group(pb[:, 0:TW], [(Wgl[:, k, ct * 128:(ct + 1) * 128], yt[:, 4 + k, :]) for k in range(4)], [wgl_tk, yk], pk)
                sg, sgk = evf.next()
                cx.op(ACT, lambda e, ct=ct: e.activation(out=sg[:], in_=pb[:, 0:TW], func=AF.Sigmoid, bias=bglu[:, ct:ct + 1]),
                      reads=[pk, g_tk], writes=[sgk])
                cx.op(DVE, lambda e, ct=ct: e.tensor_tensor(out=mgs[:, ct, :], in0=yt[:, 4 + ct, :], in1=sg[:], op=ALU.mult),
                      reads=[yk, sgk], writes=[mgsk])
            mg, mgk = mg_pool.next()
            for ct in range(8):
                if ct == 4 and nxt is not None:
                    nxt_h = prep(nxt[0], nxt[1])
                acc, acck = evf.next()
                for b in range(3):
                    pg, pgk = PS.next()
                    c0 = b * 1024 + ct * 128
                    mm_group(pg[:, 0:TW], [(Wg[:, k, c0:c0 + 128], h[:, k, :]) for k in range(8)], [wg_tk, hk], pgk)
                    gs, gsk = evf.next()
                    cx.op(ACT, lambda e: e.activation(out=gs[:], in_=pg[:, 0:TW], func=AF.Sigmoid), reads=[pgk], writes=[gsk])
                    pbr, pbk = PS.next()
                    if b == 1:
                        prs = [(Wb[:, 4 + k, ct * 128:(ct + 1) * 128], mgs[:, k, :]) for k in range(4)]
                        rd = [wb_tk, mgsk]
                    else:
                        prs = [(Wb[:, b * 4 + k, ct * 128:(ct + 1) * 128], yt[:, b * 4 + k, :]) for k in range(4)]
                        rd = [wb_tk, yk]
                    mm_group(pbr[:, 0:TW], prs, rd, pbk)
                    if b == 0:
                        cx.op(DVE, lambda e: e.tensor_tensor(out=acc[:], in0=pbr[:, 0:TW], in1=gs[:], op=ALU.mult),
                              reads=[pbk, gsk], writes=[acck])
                    else:
                        cx.op(DVE, lambda e: e.tensor_tensor(out=gs[:], in0=pbr[:, 0:TW], in1=gs[:], op=ALU.mult),
                              reads=[pbk, gsk], writes=[gsk])
                        dstap = mg[:, ct, :] if b == 2 else acc[:]
                        dstk = mgk if b == 2 else acck
                        cx.op(DVE, lambda e, dstap=dstap: e.tensor_tensor(out=dstap, in0=acc[:], in1=gs[:], op=ALU.add),
                              reads=[acck, gsk], writes=[dstk])
            for ct in range(8):
                pb, pk = PS.next()
                mm_group(pb[:, 0:TW], [(Wo[:, k, ct * 128:(ct + 1) * 128], mg[:, k, :]) for k in range(8)], [wo_tk, mgk], pk)
                cx.op(DVE, lambda e, ct=ct: e.tensor_tensor(out=xt[:, ct, :], in0=xt[:, ct, :], in1=pb[:, 0:TW], op=ALU.add),
                      reads=[pk, xk], writes=[xk])
            cx.dma(PL, x1r[:, :, i * TW:(i + 1) * TW], xt[:], xk, reads=[xk], writes=[tk["x1"]])
        ph.close()

    def phase3b(l, S):
        ph = Phase(cx, f"p3b_{l}")
        tk = S["tk"]
        stg_pool = ph.pool(2, [128, 8, 512], F32)
        g_tk = Tk(); ph.tks.append(g_tk)
        gx = load_vec(ph, P["norm_x"][l], 8, g_tk)
        gm = load_vec(ph, P["norm_mem"][l], 8, g_tk)
        gq = load_vec(ph, P["x_qnorm"][l], 2, g_tk)
        gk = load_vec(ph, P["x_knorm"][l], 2, g_tk)
        cx.op(DVE, lambda e: e.tensor_scalar(out=gq[:], in0=gq[:], scalar1=1.0 / 16, scalar2=None, op0=ALU.mult), reads=[g_tk], writes=[g_tk])
        Wq, wq_tk = ph.sbt([128, 8, 1024], BF16)
        Wo, wo_tk = ph.sbt([128, 8, 1024], BF16)
        Wkv, wkv_tk = ph.sbt([128, 8, 1024], BF16)
        load_w(ph, P["xq"][l], 8, 0, 1024, gain=(gx, g_tk), stg_pool=stg_pool, w_out=Wq, w_tk=wq_tk)
        load_w(ph, P["xo"][l], 8, 0, 1024, stg_pool=stg_pool, w_out=Wo, w_tk=wo_tk)
        TW = 512
        sq_pool = ph.pool(1, [128, 8, TW], BF16)
        rs_pool = ph.pool(2, [128, TW], F32)
        evb = ph.pool(4, [128, TW], BF16)
        evf = ph.pool(4, [128, TW], F32)
        mt_, mk = ph.sbt([128, 8, NMEM], F32)
        cx.dma(SP, mt_[:], memT_in.rearrange("(kt p) n -> p kt n", p=128), mk, writes=[mk])
        rs, rk = rms_rstd(ph, mt_[:], mk, 8, NMEM, sq_pool, rs_pool)
        hm, hmk = ph.sbt([128, 8, NMEM], BF16)
        cx.op(DVE, lambda e: e.tensor_tensor(out=hm[:], in0=mt_[:], in1=bc(rs[:, 0:NMEM], [128, 8, NMEM], 1), op=ALU.mult),
              reads=[mk, rk], writes=[hmk])
        kx, kxk = ph.sbt([128, 8, NMEM], BF16)
        vx, vxk = ph.sbt([128, 2, 1024], BF16)
        load_w(ph, P["xk"][l], 8, 0, 1024, gain=(gm, g_tk), stg_pool=stg_pool, w_out=Wkv, w_tk=wkv_tk)
        for hd in range(4):
            pbs = []
            for dt_ in range(2):
                ct = hd * 2 + dt_
                pb, pk = PS.next()
                mm_group(pb[:, 0:NMEM], [(Wkv[:, k, ct * 128:(ct + 1) * 128], hm[:, k, :]) for k in range(8)], [wkv_tk, hmk], pk)
                pbs.append((pb, pk))
            sqs = []
            for (pb, pk) in pbs:
                sq, sqk = evb.next()
                cx.op(ACT, lambda e, pb=pb, sq=sq: e.activation(out=sq[:, 0:NMEM], in_=pb[:, 0:NMEM], func=AF.Square), reads=[pk], writes=[sqk])
                sqs.append((sq, sqk))
            p2, p2k = PS.next()
            mm_group(p2[:, 0:NMEM], [(ones_b, sq[:, 0:NMEM]) for (sq, _) in sqs], [sqs[0][1], sqs[1][1], c_tk], p2k)
            rh, rhk = evf.next()
            cx.op(ACT, lambda e: e.activation(out=rh[:, 0:NMEM], in_=p2[:, 0:NMEM], func=AF.Sqrt, scale=1.0 / 256, bias=eps_t[:, 0:1]),
                  reads=[p2k, c_tk], writes=[rhk])
            cx.op(DVE, lambda e: e.reciprocal(out=rh[:, 0:NMEM], in_=rh[:, 0:NMEM]), reads=[rhk], writes=[rhk])
            for dt_, (pb, pk) in enumerate(pbs):
                cx.op(DVE, lambda e, pb=pb, dt_=dt_: e.scalar_tensor_tensor(out=kx[:, hd * 2 + dt_, :], in0=pb[:, 0:NMEM], scalar=gk[:, dt_:dt_ + 1],
                                                                       in1=rh[:, 0:NMEM], op0=ALU.mult, op1=ALU.mult),
                      reads=[pk, rhk, g_tk], writes=[kxk])
        load_w(ph, P["xv"][l], 8, 0, 1024, gain=(gm, g_tk), stg_pool=stg_pool, w_out=Wkv, w_tk=wkv_tk)
        for mt2 in range(2):
            for cc_ in range(2):
                pb, pk = PS.next()
                mm_group(pb[:], [(hm[:, k, mt2 * 128:(mt2 + 1) * 128], Wkv[:, k, cc_ * 512:(cc_ + 1) * 512]) for k in range(8)], [wkv_tk, hmk], pk)
                cx.op(ACT, lambda e, pb=pb, mt2=mt2, cc_=cc_: e.activation(out=vx[:, mt2, cc_ * 512:(cc_ + 1) * 512], in_=pb[:], func=AF.Copy),
                      reads=[pk], writes=[vxk])
        x_pool = stg_pool
        h_pool = ph.pool(2, [128, 8, TW], BF16)
        qh_pool = ph.pool(2, [128, 2, TW], BF16)
        pt_pool = ph.pool(2, [128, 2, TW], BF16)
        o_pool = ph.pool(1, [128, 8, TW], BF16)
        x1r = S["x1"].rearrange("(kt p) n -> p kt n", p=128)
        x2r = S["x2"].rearrange("(kt p) n -> p kt n", p=128)

        def ld(i):
            xt, xk = x_pool.next()
            cx.dma(SP, xt[:], x1r[:, :, i * TW:(i + 1) * TW], xk, reads=[tk["x1"]], writes=[xk])
            return xt, xk

        def prep(xt, xk):
            rs, rk = rms_rstd(ph, xt[:], xk, 8, TW, sq_pool, rs_pool)
            h, hk = h_pool.next()
            cx.op(DVE, lambda e: e.tensor_tensor(out=h[:], in0=xt[:], in1=bc(rs[:], [128, 8, TW], 1), op=ALU.mult),
                  reads=[xk, rk], writes=[hk])
            return h, hk

        def part_a(h, hk, hd):
            pbs = []
            for dt_ in range(2):
                ct = hd * 2 + dt_
                pb, pk = PS.next()
                mm_group(pb[:, 0:TW], [(Wq[:, k, ct * 128:(ct + 1) * 128], h[:, k, :]) for k in range(8)], [wq_tk, hk], pk)
                pbs.append((pb, pk))
            sqs = []
            for (pb, pk) in pbs:
                sq, sqk = evb.next()
                cx.op(ACT, lambda e, pb=pb, sq=sq: e.activation(out=sq[:], in_=pb[:, 0:TW], func=AF.Square), reads=[pk], writes=[sqk])
                sqs.append((sq, sqk))
            p2, p2k = PS.next()
            mm_group(p2[:, 0:TW], [(ones_b, sq[:]) for (sq, _) in sqs], [sqs[0][1], sqs[1][1], c_tk], p2k)
            rh, rhk = evf.next()
            cx.op(ACT, lambda e: e.activation(out=rh[:], in_=p2[:, 0:TW], func=AF.Sqrt, scale=1.0 / 256, bias=eps_t[:, 0:1]),
                  reads=[p2k, c_tk], writes=[rhk])
            cx.op(DVE, lambda e: e.reciprocal(out=rh[:], in_=rh[:]), reads=[rhk], writes=[rhk])
            qh, qhk = qh_pool.next()
            for dt_, (pb, pk) in enumerate(pbs):
                cx.op(DVE, lambda e, pb=pb, dt_=dt_: e.scalar_tensor_tensor(out=qh[:, dt_, :], in0=pb[:, 0:TW], scalar=gq[:, dt_:dt_ + 1],
                                                                       in1=rh[:], op0=ALU.mult, op1=ALU.mult),
                      reads=[pk, rhk, g_tk], writes=[qhk])
            return qh, qhk

        def part_b(hd, qh, qhk, oT, oTk):
            pt, ptk = pt_pool.next()
            for mt2 in range(2):
                pb, pk = PS.next()
                mm_group(pb[:, 0:TW], [(kx[:, hd * 2 + dt_, mt2 * 128:(mt2 + 1) * 128], qh[:, dt_, :]) for dt_ in range(2)], [kxk, qhk], pk)
                cx.op(ACT, lambda e, pb=pb, mt2=mt2: e.activation(out=pt[:, mt2, :], in_=pb[:, 0:TW], func=AF.Exp), reads=[pk], writes=[ptk])
            p3, p3k = PS.next()
            mm_group(p3[:, 0:TW], [(ones_b, pt[:, mt2, :]) for mt2 in range(2)], [ptk, c_tk], p3k)
            ri, rik = evf.next()
            cx.op(DVE, lambda e: e.reciprocal(out=ri[:], in_=p3[:, 0:TW]), reads=[p3k], writes=[rik])
            for dt_ in range(2):
                pb, pk = PS.next()
                c0 = hd * 256 + dt_ * 128
                mm_group(pb[:, 0:TW], [(vx[:, mt2, c0:c0 + 128], pt[:, mt2, :]) for mt2 in range(2)], [vxk, ptk], pk)
                cx.op(DVE, lambda e, pb=pb, dt_=dt_: e.tensor_tensor(out=oT[:, hd * 2 + dt_, :], in0=pb[:, 0:TW], in1=ri[:], op=ALU.mult),
                      reads=[pk, rik], writes=[oTk])

        NTW = N // TW
        cur_x = ld(0)
        cur_h = prep(*cur_x)
        for i in range(NTW):
            xt, xk = cur_x
            h, hk = cur_h
            nxt_x = ld(i + 1) if i + 1 < NTW else None
            nxt_h = None
            oT, oTk = o_pool.next()
            qa = part_a(h, hk, 0)
            for hd in range(4):
                qn = part_a(h, hk, hd + 1) if hd + 1 < 4 else None
                part_b(hd, qa[0], qa[1], oT, oTk)
                if hd == 1 and nxt_x is not None:
                    nxt_h = prep(*nxt_x)
                qa = qn
            for ct in range(8):
                pb, pk = PS.next()
                mm_group(pb[:, 0:TW], [(Wo[:, k, ct * 128:(ct + 1) * 128], oT[:, k, :]) for k in range(8)], [wo_tk, oTk], pk)
                cx.op(DVE, lambda e, ct=ct, pb=pb: e.tensor_tensor(out=xt[:, ct, :], in0=xt[:, ct, :], in1=pb[:, 0:TW], op=ALU.add),
                      reads=[pk, xk], writes=[xk])
            cx.dma(PL, x2r[:, :, i * TW:(i + 1) * TW], xt[:], xk, reads=[xk], writes=[tk["x2"]])
            cur_x, cur_h = nxt_x, nxt_h
        ph.close()

    def phase3c(l, S):
        ph = Phase(cx, f"p3c_{l}")
        tk = S["tk"]
        stg_pool = ph.pool(2, [128, 8, 512], F32)
        g_tk = Tk(); ph.tks.append(g_tk)
        gf = load_vec(ph, P["norm_ffn"][l], 8, g_tk)
        Wgu, wgu_tk = ph.sbt([128, 8, 2 * DFF], BF16)
        Wd, wd_tk = ph.sbt([128, 22, 1024], BF16)
        load_w(ph, P["w_gate_up"][l], 8, 0, 2 * DFF, gain=(gf, g_tk), stg_pool=stg_pool, w_out=Wgu, w_tk=wgu_tk)
        load_w(ph, P["w_down"][l], 22, 0, 1024, stg_pool=stg_pool, w_out=Wd, w_tk=wd_tk)
        TW = 512
        hid_pool = ph.pool(1, [128, 22, TW], BF16)
        sq_pool = hid_pool
        rs_pool = ph.pool(2, [128, TW], F32)
        evf = ph.pool(2, [128, TW], F32)
        x_pool = stg_pool
        h_pool = ph.pool(1, [128, 8, TW], BF16)
        x2r = S["x2"].rearrange("(kt p) n -> p kt n", p=128)
        x3r = S["x3"].rearrange("(kt p) n -> p kt n", p=128)

        def ld(i):
            xt, xk = x_pool.next()
            cx.dma(SP, xt[:], x2r[:, :, i * TW:(i + 1) * TW], xk, reads=[tk["x2"]], writes=[xk])
            return xt, xk

        nxt = ld(0)
        for i in range(N // TW):
            xt, xk = nxt
            if i + 1 < N // TW:
                nxt = ld(i + 1)
            rs, rk = rms_rstd(ph, xt[:], xk, 8, TW, sq_pool, rs_pool)
            h, hk = h_pool.next()
            cx.op(DVE, lambda e: e.tensor_tensor(out=h[:], in0=xt[:], in1=bc(rs[:], [128, 8, TW], 1), op=ALU.mult),
                  reads=[xk, rk], writes=[hk])
            hid, hidk = hid_pool.next()
            for j in range(22):
                pa, pak = PS.next()
                mm_group(pa[:, 0:TW], [(Wgu[:, k, j * 128:(j + 1) * 128], h[:, k, :]) for k in range(8)], [wgu_tk, hk], pak)
                sa, sak = evf.next()
                cx.op(ACT, lambda e, pa=pa, sa=sa: e.activation(out=sa[:], in_=pa[:, 0:TW], func=AF.Silu), reads=[pak], writes=[sak])
                pb, pk = PS.next()
                mm_group(pb[:, 0:TW], [(Wgu[:, k, DFF + j * 128:DFF + (j + 1) * 128], h[:, k, :]) for k in range(8)], [wgu_tk, hk], pk)
                cx.op(DVE, lambda e, pb=pb, sa=sa, j=j: e.tensor_tensor(out=hid[:, j, :], in0=pb[:, 0:TW], in1=sa[:], op=ALU.mult),
                      reads=[pk, sak], writes=[hidk])
            for ct in range(8):
                pb, pk = PS.next()
                mm_group(pb[:, 0:TW], [(Wd[:, j, ct * 128:(ct + 1) * 128], hid[:, j, :]) for j in range(22)], [wd_tk, hidk], pk)
                cx.op(DVE, lambda e, ct=ct, pb=pb: e.tensor_tensor(out=xt[:, ct, :], in0=xt[:, ct, :], in1=pb[:, 0:TW], op=ALU.add),
                      reads=[pk, xk], writes=[xk])
            cx.dma(PL, x3r[:, :, i * TW:(i + 1) * TW], xt[:], xk, reads=[xk], writes=[S["tk"]["x3"]])
        ph.close()

    ccsem = cx.get_dsem()

    def gather_y(l, S):
        tk = S["tk"]
        cx._need(PL, cx._deps([tk["ysrc"]], [tk["ydst"]]))
        for c in range(N // 256):
            ccsem.count += 1
            nc.gpsimd.collective_compute("AllGather", ALU.bypass, replica_groups=[[0, 1, 2, 3], [4, 5, 6, 7]],
                                         ins=[S["ysrc"][c]], outs=[S["ydst"][c]]).then_inc(ccsem.h, 1)
            cx.ninstr += 1
        tk["ydst"].w[ccsem] = ccsem.count
        tk["ysrc"].r[ccsem] = ccsem.count

    xT, x_tk = xT_in, NOTK
    for l in range(depth):
        S = mk_scratch(l)
        if "p1" in stages:
            phase1(l, xT, x_tk, S)
        if "fox" in stages:
            phase_fox(l, S)
        if "s5" in stages:
            phase_s5(l, S)
        if "hg" in stages:
            phase_hg(l, S)
        if "cc" in stages:
            gather_y(l, S)
        if "p3a" in stages:
            phase3a(l, xT, x_tk, S)
        if "p3b" in stages:
            phase3b(l, S)
        if "p3c" in stages:
            phase3c(l, S)
        xT, x_tk = S["x3"], S["tk"]["x3"]
    G.close()
    return nc, cx, dbg_out


def _consts():
    c = np.zeros((128, 6, 128), np.float32)
    p = np.arange(128)
    c[:, 0, :] = np.eye(128)
    c[:, 1, :] = (p[None, :] >= p[:, None])
    c[:, 2, :] = (p[:, None] // 64 == p[None, :] // 64)
    c[:, 3, :] = np.eye(128)[(p + 64) % 128]
    c[:, 4, :] = 1.0
    c[:, 5, :] = (p[:, None] // 64 == p[None, :] // 64) & (p[:, None] % 64 < p[None, :] % 64)
    return c


_SHARED = ("norm_mix", "fox_qnorm", "fox_knorm", "s5_w_glu", "s5_b_glu", "hg_onorm", "w_branch", "w_out",
           "norm_x", "norm_mem", "xq", "xk", "xv", "xo", "x_qnorm", "x_knorm", "norm_ffn", "w_gate_up", "w_down")
_STAGES = ("p1", "fox", "s5", "hg", "cc", "p3a", "p3b", "p3c")


def _local_params(inputs, j):
    f = lambda k: np.asarray(inputs[k], np.float32)
    w = f("w_in")
    sl = lambda a, b: np.arange(a, b)
    cols = np.concatenate([
        sl(128 * j, 128 * j + 128), sl(512 + 128 * j, 512 + 128 * j + 128), sl(1024 + 128 * j, 1024 + 128 * j + 128),
        sl(1536 + 2 * j, 1536 + 2 * j + 2), sl(1544 + 128 * j, 1544 + 128 * j + 128), sl(2056 + 128 * j, 2056 + 128 * j + 128),
        sl(2568 + 128 * j, 2568 + 128 * j + 128), sl(3080 + 128 * j, 3080 + 128 * j + 128), sl(3592 + 128 * j, 3592 + 128 * j + 128),
        sl(4104, 7176)])
    m = {"w_in": np.ascontiguousarray(w[:, :, cols])}
    m["fox_fbias"] = np.ascontiguousarray(f("fox_fbias")[:, 2 * j:2 * j + 2])
    for k in ("s5_a_re", "s5_a_im", "s5_b_re", "s5_b_im", "s5_c_re", "s5_c_im", "s5_log_dt"):
        m[k] = np.ascontiguousarray(f(k)[:, 8 * j:8 * j + 8])
    m["s5_d"] = np.ascontiguousarray(f("s5_d")[:, 128 * j:128 * j + 128])
    m["hg_lb"] = np.ascontiguousarray(f("hg_lb")[:, 128 * j:128 * j + 128])
    return m


def _run(inputs, N, depth):
    x = np.asarray(inputs["x"], np.float32)
    mem = np.asarray(inputs["mem"], np.float32)
    B = x.shape[0]
    nc, cx, _ = build(N, depth, debug=False, stages=_STAGES)
    shared = {k: np.ascontiguousarray(np.asarray(inputs[k], np.float32)) for k in _SHARED}
    shared["consts"] = _consts()
    shared["nvec"] = np.zeros((128, 80), np.float32)
    shared["jv"] = np.tile(np.arange(TT, dtype=np.float32)[None], (128, 1))
    shared["m16"] = (np.arange(128)[:, None] // 16 == np.arange(8)[None, :]).astype(np.float32)
    loc = [_local_params(inputs, j) for j in range(4)]
    maps = []
    for c in range(8):
        b, j = c // 4, c % 4
        m = dict(shared)
        m.update(loc[j])
        m["xT"] = np.ascontiguousarray(x[b, :N].T)
        m["memT"] = np.ascontiguousarray(mem[b].T)
        maps.append(m)
    res = run_bass_kernel_spmd(nc, maps, core_ids=list(range(8)))
    out = np.stack([np.asarray(res.results[4 * b]["outT"]).T for b in range(B)], axis=0)
    return np.ascontiguousarray(out.astype(np.float32))


def kernel(**inputs):
    return _run(inputs, 16384, DEPTH)
```
